# Optimizing a Trainium2 kernel written in Bass

```python
import math
import jax, jax.numpy as jnp
from jax import lax
import numpy as np

D_MODEL = 2048
BATCH = 8
SEQ = 2048
DEPTH = 1

PLE_DIM = 256
MIX_WIDTH = D_MODEL
N_HEADS = 8
QK_NOPE_DIM = 128
QK_ROPE_DIM = 64
V_HEAD_DIM = 128
QK_HEAD_DIM = QK_NOPE_DIM + QK_ROPE_DIM
Q_LORA = 512
KV_LORA = 256
ATTN_WIDTH = N_HEADS * V_HEAD_DIM
ROPE_THETA = 10000.0
Q_BLOCK = 128
SSM_WIDTH = MIX_WIDTH - ATTN_WIDTH
SSM_GROUP = 16
SSM_GROUPS = SSM_WIDTH // SSM_GROUP
SSM_STATE = 64
DT_MIN = 1e-3
DT_MAX = 1e-1
N_IN = Q_LORA + KV_LORA + QK_ROPE_DIM + SSM_WIDTH
D_FF = int(math.ceil((8 * D_MODEL / 3) / 256) * 256)
EPS = 1e-6

kernel_name = "hybrid_mla_s5_parallel_heads"


def rms_norm(t, g):
    tf = t.astype(jnp.float32)
    y = tf * lax.rsqrt(jnp.mean(tf * tf, axis=-1, keepdims=True) + EPS)
    return (y * g.astype(jnp.float32)).astype(t.dtype)


def rope_tables(positions):
    inv_freq = 1.0 / (ROPE_THETA ** (jnp.arange(0, QK_ROPE_DIM, 2, dtype=jnp.float32) / QK_ROPE_DIM))
    ang = positions.astype(jnp.float32)[..., None] * inv_freq
    return jnp.cos(ang)[:, None], jnp.sin(ang)[:, None]


def apply_rope(t, cos, sin):
    half = QK_ROPE_DIM // 2
    t1 = t[..., :half].astype(jnp.float32)
    t2 = t[..., half:].astype(jnp.float32)
    return jnp.concatenate([t1 * cos - t2 * sin, t2 * cos + t1 * sin], axis=-1).astype(t.dtype)


def causal_block_attention(q, k, v):
    L = q.shape[2]
    scale = QK_HEAD_DIM ** -0.5
    outs = []
    for i in range(L // Q_BLOCK):
        kv_len = (i + 1) * Q_BLOCK
        q_blk = q[:, :, i * Q_BLOCK:kv_len]
        s = jnp.einsum('bhqd,bhkd->bhqk', q_blk, k[:, :, :kv_len]).astype(jnp.float32) * scale
        q_idx = i * Q_BLOCK + jnp.arange(Q_BLOCK)[:, None]
        k_idx = jnp.arange(kv_len)[None, :]
        s = jnp.where(k_idx <= q_idx, s, -jnp.inf)
        pr = jax.nn.softmax(s, axis=-1).astype(v.dtype)
        outs.append(jnp.einsum('bhqk,bhkd->bhqd', pr, v[:, :, :kv_len]))
    return jnp.concatenate(outs, axis=2)


def mla_mixer(z_q, z_kv, z_kr, cos, sin, g_q_lora, w_uq, g_kv_lora, w_ukv, g_q_head, g_k_head):
    B_, L, _ = z_q.shape
    c_q = rms_norm(z_q, g_q_lora)
    q = (c_q @ w_uq).reshape(B_, L, N_HEADS, QK_HEAD_DIM)
    c_kv = rms_norm(z_kv, g_kv_lora)
    kv = (c_kv @ w_ukv).reshape(B_, L, N_HEADS, QK_NOPE_DIM + V_HEAD_DIM)
    k_nope, v = kv[..., :QK_NOPE_DIM], kv[..., QK_NOPE_DIM:]
    k_rope = jnp.broadcast_to(z_kr[:, :, None, :], (B_, L, N_HEADS, QK_ROPE_DIM))
    k = jnp.concatenate([k_nope, k_rope], axis=-1)
    q = rms_norm(q, g_q_head).transpose(0, 2, 1, 3)
    k = rms_norm(k, g_k_head).transpose(0, 2, 1, 3)
    v = v.transpose(0, 2, 1, 3)
    q = jnp.concatenate([q[..., :QK_NOPE_DIM], apply_rope(q[..., QK_NOPE_DIM:], cos, sin)], axis=-1)
    k = jnp.concatenate([k[..., :QK_NOPE_DIM], apply_rope(k[..., QK_NOPE_DIM:], cos, sin)], axis=-1)
    o = causal_block_attention(q, k, v)
    return o.transpose(0, 2, 1, 3).reshape(B_, L, ATTN_WIDTH)


def _ssm_combine(earlier, later):
    ar_i, ai_i, br_i, bi_i = earlier
    ar_j, ai_j, br_j, bi_j = later
    ar = ar_j * ar_i - ai_j * ai_i
    ai = ar_j * ai_i + ai_j * ar_i
    br = ar_j * br_i - ai_j * bi_i + br_j
    bi = ar_j * bi_i + ai_j * br_i + bi_j
    return ar, ai, br, bi


def s5_mixer(u, lam_re, lam_im, log_dt, b_re, b_im, c_re, c_im, d_skip, w_glu, b_glu):
    B_, L, _ = u.shape
    f32 = jnp.float32
    uf = u.astype(f32).reshape(B_, L, SSM_GROUPS, SSM_GROUP)
    lr = jnp.minimum(lam_re.astype(f32), -1e-4)
    li = lam_im.astype(f32)
    dt = jnp.exp(log_dt.astype(f32))[:, None]
    mag = jnp.exp(lr * dt)
    abar_re = mag * jnp.cos(li * dt)
    abar_im = mag * jnp.sin(li * dt)
    den = lr * lr + li * li
    num_re = abar_re - 1.0
    num_im = abar_im
    coef_re = (num_re * lr + num_im * li) / den
    coef_im = (num_im * lr - num_re * li) / den
    br = b_re.astype(f32)
    bim = b_im.astype(f32)
    bb_re = coef_re[..., None] * br - coef_im[..., None] * bim
    bb_im = coef_re[..., None] * bim + coef_im[..., None] * br
    bu_re = jnp.einsum('blgh,gph->blgp', uf, bb_re)
    bu_im = jnp.einsum('blgh,gph->blgp', uf, bb_im)
    a_re = jnp.broadcast_to(abar_re[None, None], (1, L, SSM_GROUPS, SSM_STATE))
    a_im = jnp.broadcast_to(abar_im[None, None], (1, L, SSM_GROUPS, SSM_STATE))
    _, _, s_re, s_im = lax.associative_scan(_ssm_combine, (a_re, a_im, bu_re, bu_im), axis=1)
    y = (jnp.einsum('blgp,ghp->blgh', s_re, c_re.astype(f32))
         - jnp.einsum('blgp,ghp->blgh', s_im, c_im.astype(f32))
         + d_skip.astype(f32) * uf)
    y = jax.nn.gelu(y.reshape(B_, L, SSM_WIDTH).astype(u.dtype))
    return y * jax.nn.sigmoid(y @ w_glu + b_glu)


def setup_inputs(seed: int = 0) -> dict:
    key = jax.random.key(seed)
    ks = jax.random.split(key, 32)
    f32 = jnp.float32

    def nrm(k, shape, scale):
        return jax.random.normal(k, shape, f32) * scale

    def gain(k, n):
        return 1.0 + 0.01 * jax.random.normal(k, (DEPTH, n), f32)

    x = jax.random.normal(ks[0], (BATCH, SEQ, D_MODEL), f32)
    p = jax.random.normal(ks[1], (DEPTH, BATCH, SEQ, PLE_DIM), f32)
    offs = jax.random.randint(ks[2], (BATCH, 1), 0, 1024, dtype=jnp.int32)
    positions = (jnp.arange(SEQ, dtype=jnp.int32)[None, :] + offs).astype(jnp.int32)

    G, P, H = SSM_GROUPS, SSM_STATE, SSM_GROUP
    lam_re = -0.5 + 0.01 * jax.random.normal(ks[3], (DEPTH, G, P), f32)
    lam_im = (math.pi * jnp.arange(P, dtype=f32))[None, None] + 0.01 * jax.random.normal(ks[4], (DEPTH, G, P), f32)
    log_dt = jax.random.uniform(ks[5], (DEPTH, G), f32, math.log(DT_MIN), math.log(DT_MAX))

    return {
        "x": x,
        "p": p,
        "positions": positions,
        "g_mix_norm": gain(ks[6], D_MODEL),
        "w_in": nrm(ks[7], (DEPTH, D_MODEL, N_IN), D_MODEL ** -0.5),
        "g_q_lora": gain(ks[8], Q_LORA),
        "w_uq": nrm(ks[9], (DEPTH, Q_LORA, N_HEADS * QK_HEAD_DIM), Q_LORA ** -0.5),
        "g_kv_lora": gain(ks[10], KV_LORA),
        "w_ukv": nrm(ks[11], (DEPTH, KV_LORA, N_HEADS * (QK_NOPE_DIM + V_HEAD_DIM)), KV_LORA ** -0.5),
        "g_q_head": gain(ks[12], QK_HEAD_DIM),
        "g_k_head": gain(ks[13], QK_HEAD_DIM),
        "lam_re": lam_re,
        "lam_im": lam_im,
        "log_dt": log_dt,
        "b_re": nrm(ks[14], (DEPTH, G, P, H), (2 * H) ** -0.5),
        "b_im": nrm(ks[15], (DEPTH, G, P, H), (2 * H) ** -0.5),
        "c_re": nrm(ks[16], (DEPTH, G, H, P), (2 * P) ** -0.5 * 4.0),
        "c_im": nrm(ks[17], (DEPTH, G, H, P), (2 * P) ** -0.5 * 4.0),
        "d_skip": nrm(ks[18], (DEPTH, G, H), 1.0),
        "w_glu": nrm(ks[19], (DEPTH, SSM_WIDTH, SSM_WIDTH), SSM_WIDTH ** -0.5),
        "b_glu": nrm(ks[20], (DEPTH, SSM_WIDTH), 0.01),
        "g_out_attn": gain(ks[21], ATTN_WIDTH),
        "g_out_ssm": gain(ks[22], SSM_WIDTH),
        "w_o": nrm(ks[23], (DEPTH, MIX_WIDTH, D_MODEL), MIX_WIDTH ** -0.5),
        "g_ffn_norm": gain(ks[24], D_MODEL),
        "w_gate": nrm(ks[25], (DEPTH, D_MODEL, D_FF), D_MODEL ** -0.5),
        "w_up": nrm(ks[26], (DEPTH, D_MODEL, D_FF), D_MODEL ** -0.5),
        "w_down": nrm(ks[27], (DEPTH, D_FF, D_MODEL), D_FF ** -0.5),
        "g_ple_norm": gain(ks[28], D_MODEL),
        "w_ple_gate": nrm(ks[29], (DEPTH, D_MODEL, D_MODEL), D_MODEL ** -0.5),
        "w_ple_proj": nrm(ks[30], (DEPTH, PLE_DIM, D_MODEL), PLE_DIM ** -0.5),
    }


def reference(x, p, positions, g_mix_norm, w_in, g_q_lora, w_uq, g_kv_lora, w_ukv,
              g_q_head, g_k_head, lam_re, lam_im, log_dt, b_re, b_im, c_re, c_im, d_skip,
              w_glu, b_glu, g_out_attn, g_out_ssm, w_o, g_ffn_norm, w_gate, w_up, w_down,
              g_ple_norm, w_ple_gate, w_ple_proj):
    cos, sin = rope_tables(positions)
    o1 = Q_LORA
    o2 = o1 + KV_LORA
    o3 = o2 + QK_ROPE_DIM
    for i in range(DEPTH):
        h = rms_norm(x, g_mix_norm[i])
        z = h @ w_in[i]
        o_attn = mla_mixer(z[..., :o1], z[..., o1:o2], z[..., o2:o3], cos, sin,
                           g_q_lora[i], w_uq[i], g_kv_lora[i], w_ukv[i],
                           g_q_head[i], g_k_head[i])
        o_ssm = s5_mixer(z[..., o3:], lam_re[i], lam_im[i], log_dt[i], b_re[i], b_im[i],
                         c_re[i], c_im[i], d_skip[i], w_glu[i], b_glu[i])
        mixed = jnp.concatenate([rms_norm(o_attn, g_out_attn[i]),
                                 rms_norm(o_ssm, g_out_ssm[i])], axis=-1)
        x = x + mixed @ w_o[i]
        hf = rms_norm(x, g_ffn_norm[i])
        x = x + (jax.nn.silu(hf @ w_gate[i]) * (hf @ w_up[i])) @ w_down[i]
        gate = jax.nn.sigmoid(rms_norm(x, g_ple_norm[i]) @ w_ple_gate[i])
        x = x + gate * (p[i] @ w_ple_proj[i])
    return x
```

```python
import bisect
import math
from contextlib import ExitStack
import numpy as np
import concourse.bass as bass
import concourse.mybir as mybir
from concourse.bass_utils import run_bass_kernel_spmd

F32 = mybir.dt.float32
BF16 = mybir.dt.bfloat16
I32 = mybir.dt.int32
AF = mybir.ActivationFunctionType
ALU = mybir.AluOpType
AX = mybir.AxisListType

COMPUTE = ("tensor", "vector", "scalar", "gpsimd")
ALLENG = ("tensor", "vector", "scalar", "gpsimd", "sync")

L = 2048
D = 2048
NT = 16
DFF = 5632
EPS = 1e-6


class Tok:
    __slots__ = ("name", "last_writer", "readers", "sem", "dma_count")

    def __init__(self, name="t"):
        self.name = name
        self.last_writer = None
        self.readers = []
        self.sem = None
        self.dma_count = 0


class Op:
    __slots__ = ("eng", "fn", "edeps", "ddeps", "is_dma", "ordinal", "dma_tok")

    def __init__(self, eng, fn, is_dma):
        self.eng = eng
        self.fn = fn
        self.edeps = {}
        self.ddeps = {}
        self.is_dma = is_dma
        self.ordinal = None
        self.dma_tok = None


class Prog:
    def __init__(self, nc, stack):
        self.nc = nc
        self.stack = stack
        self.ops = {e: [] for e in ALLENG}
        self.eng_sem = {}
        self.eng_count = {e: 0 for e in COMPUTE}
        for e in COMPUTE:
            self.eng_sem[e] = stack.enter_context(nc.semaphore("s_" + e))
        self.nsem = 4
        self.ALL = Tok("ALL")
        self.nops = 0

    def tok(self, name="t"):
        return Tok(name)

    def toks(self, n, name="t"):
        return [Tok("%s%d" % (name, i)) for i in range(n)]

    def _dep_on(self, op, prev):
        if prev is None or prev is op:
            return
        if prev.is_dma:
            tok = prev.dma_tok
            cur = op.ddeps.get(id(tok))
            if cur is None or cur[1] < tok.dma_count:
                op.ddeps[id(tok)] = (tok, tok.dma_count)
            return
        if prev.eng == "tensor" and op.eng == "tensor" and not op.is_dma:
            return
        cur = op.edeps.get(prev.eng)
        if cur is None or cur < prev.ordinal:
            op.edeps[prev.eng] = prev.ordinal

    def add(self, eng, fn, reads=(), writes=(), dma=False, _barrier=False):
        op = Op(eng, fn, dma)
        self.nops += 1
        if not dma:
            self.eng_count[eng] += 1
            op.ordinal = self.eng_count[eng]
        reads = list(reads)
        writes = list(writes)
        if _barrier:
            writes.append(self.ALL)
        else:
            reads.append(self.ALL)
        for t in reads:
            self._dep_on(op, t.last_writer)
        for t in writes:
            self._dep_on(op, t.last_writer)
            for r in t.readers:
                self._dep_on(op, r)
        if dma:
            tok = writes[0]
            if tok.sem is None:
                tok.sem = self.stack.enter_context(self.nc.semaphore("d%d" % self.nsem))
                self.nsem += 1
            tok.dma_count += 16
            op.dma_tok = tok
        for t in writes:
            t.last_writer = op
            t.readers = []
        for t in reads:
            t.readers.append(op)
        self.ops[eng].append(op)
        return op

    def emit(self, final_waits=()):
        nc = self.nc
        needed = {e: set() for e in COMPUTE}
        for e, lst in self.ops.items():
            for op in lst:
                for pe, o in op.edeps.items():
                    needed[pe].add(o)
        ranks = {e: sorted(needed[e]) for e in COMPUTE}

        def rank(e, o):
            return bisect.bisect_right(ranks[e], o)

        with nc.Block() as block:
            def make(e):
                def body(eng):
                    waited_e = {}
                    waited_d = {}
                    for op in self.ops[e]:
                        for pe, o in op.edeps.items():
                            val = rank(pe, o)
                            if waited_e.get(pe, 0) >= val:
                                continue
                            eng.wait_ge(self.eng_sem[pe], val)
                            waited_e[pe] = val
                        for k, (tok, cnt) in op.ddeps.items():
                            if waited_d.get(k, 0) >= cnt:
                                continue
                            eng.wait_ge(tok.sem, cnt)
                            waited_d[k] = cnt
                        ins = op.fn(eng)
                        if op.is_dma:
                            ins.then_inc(op.dma_tok.sem, 16)
                        elif op.ordinal in needed[e]:
                            ins.then_inc(self.eng_sem[e], 1)
                    if e == "sync":
                        for tok in final_waits:
                            eng.wait_ge(tok.sem, tok.dma_count)
                return body
            block.tensor(make("tensor"))
            block.vector(make("vector"))
            block.scalar(make("scalar"))
            block.gpsimd(make("gpsimd"))
            block.sync(make("sync"))


WSPECS = [
    ("w_in", 2048, 1856), ("w_glu", 1024, 1024), ("w_uq", 512, 1536), ("w_ukv", 256, 2048),
    ("w_o", 2048, 2048), ("w_gate", 2048, 5632), ("w_up", 2048, 5632), ("w_down", 5632, 2048),
    ("w_ple_gate", 2048, 2048), ("w_ple_proj", 256, 2048),
]
SMALL = [
    ("g_mix_norm", [1, 2048]), ("g_q_lora", [1, 512]), ("g_kv_lora", [1, 256]),
    ("g_q_head", [1, 192]), ("g_k_head", [1, 192]), ("lam_re", [64, 64]), ("lam_im", [64, 64]),
    ("log_dt", [1, 64]), ("b_re", [64, 64, 16]), ("b_im", [64, 64, 16]), ("c_re", [64, 16, 64]),
    ("c_im", [64, 16, 64]), ("d_skip", [1, 1024]), ("b_glu", [1, 1024]), ("g_out_attn", [1, 1024]),
    ("g_out_ssm", [1, 1024]), ("g_ffn_norm", [1, 2048]), ("g_ple_norm", [1, 2048]),
]

ARENA_BYTES = 200 * 1024


def build(stage="all"):
    nc = bass.Bass("TRN2", target_bir_lowering=False)
    dr = {}
    dr["x"] = nc.dram_tensor("x", [L, D], F32, kind="ExternalInput").ap()
    dr["p"] = nc.dram_tensor("p", [L, 256], F32, kind="ExternalInput").ap()
    dr["pos"] = nc.dram_tensor("pos", [1, L], I32, kind="ExternalInput").ap()
    for n, r, c in WSPECS:
        dr[n] = nc.dram_tensor(n, [r, c], F32, kind="ExternalInput").ap()
    for n, shp in SMALL:
        dr[n] = nc.dram_tensor(n, shp, F32, kind="ExternalInput").ap()
    out = nc.dram_tensor("out", [L, D], F32, kind="ExternalOutput").ap()
    wb = {}
    for n, r, c in WSPECS:
        wb[n] = nc.dram_tensor(n + "_bf", [r, c], BF16).ap()
    kscr = nc.dram_tensor("kscr", [64, 16, 15, 16], BF16).ap()
    dbg = None
    if stage != "all":
        dbg = nc.dram_tensor("dbg", [128, 16384], F32, kind="ExternalOutput").ap()

    with ExitStack() as st:
        P = Prog(nc, st)
        sbt = lambda n, s, d: st.enter_context(nc.sbuf_tensor(n, s, d))
        arena = sbt("arena", [128, ARENA_BYTES // 2], BF16)

        def view(off, shape, dt, parts=(0, 128)):
            esz = 4 if dt in (F32, I32) else 2
            n = int(np.prod(shape))
            nb = n * esz
            assert off % 4 == 0 and off + nb <= ARENA_BYTES, (off, nb)
            a = arena[parts[0]:parts[1], off // 2:(off + nb) // 2]
            if dt != BF16:
                a = a.bitcast(dt)
            if len(shape) > 1:
                names = " ".join("d%d" % i for i in range(len(shape)))
                kw = {"d%d" % i: int(s) for i, s in enumerate(shape)}
                a = a.rearrange("p (%s) -> p %s" % (names, names), **kw)
            return a

        KB = 1024
        pb = [st.enter_context(nc.psum_tensor("pb%d" % i, [128, 512], F32)) for i in range(8)]
        tb = P.toks(8, "pb")
        pbb = [b[:].bitcast(BF16) for b in pb]

        ident_bf = sbt("ident_bf", [128, 128], BF16)
        ident_f = sbt("ident_f", [128, 128], F32)
        mask_bf = sbt("mask_bf", [128, 128], BF16)
        ones_bf = sbt("ones_bf", [128, 128], BF16)
        scr1 = sbt("scr1", [128, 4], F32)
        t_const = P.tok("const")

        def barrier():
            P.add("vector", lambda e: e.memset(scr1[:, 0:1], 0.0), _barrier=True)

        V = lambda fn, r=(), w=(): P.add("vector", fn, reads=r, writes=w)
        A = lambda fn, r=(), w=(): P.add("scalar", fn, reads=r, writes=w)
        G = lambda fn, r=(), w=(): P.add("gpsimd", fn, reads=r, writes=w)
        T = lambda fn, r=(), w=(): P.add("tensor", fn, reads=r, writes=w)
        DS = lambda fn, r=(), w=(): P.add("sync", fn, reads=r, writes=w, dma=True)
        DG = lambda fn, r=(), w=(): P.add("gpsimd", fn, reads=r, writes=w, dma=True)

        G(lambda e: e.memset(ident_bf[:], 0.0), w=[t_const])
        G(lambda e: e.affine_select(out=ident_bf[:], in_=ident_bf[:], pattern=[[-1, 128]], compare_op=ALU.not_equal, fill=1.0, base=0, channel_multiplier=1), r=[t_const], w=[t_const])
        G(lambda e: e.memset(ident_f[:], 0.0), w=[t_const])
        G(lambda e: e.affine_select(out=ident_f[:], in_=ident_f[:], pattern=[[-1, 128]], compare_op=ALU.not_equal, fill=1.0, base=0, channel_multiplier=1), r=[t_const], w=[t_const])
        G(lambda e: e.memset(mask_bf[:], 1.0), w=[t_const])
        G(lambda e: e.affine_select(out=mask_bf[:], in_=mask_bf[:], pattern=[[1, 128]], compare_op=ALU.is_ge, fill=0.0, base=0, channel_multiplier=-1), r=[t_const], w=[t_const])
        G(lambda e: e.memset(ones_bf[:], 1.0), w=[t_const])

        colv = sbt("colv", [128, 4, 8], F32)
        t_col = P.tok("col")
        for idx, nm in enumerate(("b_glu", "g_out_ssm", "g_out_ssm", "g_out_attn")):
            DS(lambda e, idx=idx, nm=nm: e.dma_start(out=colv[:, idx, :], in_=dr[nm][0, :].rearrange("(c p) -> p c", p=128), allow_slow_non_contiguous=True), w=[t_col])
        V(lambda e: e.reciprocal(out=colv[:, 2, :], in_=colv[:, 2, :]), r=[t_col], w=[t_col])

        t_wb = {n: P.tok("wb_" + n) for n, _, _ in WSPECS}

        def precast(n):
            r, c = [(rr, cc) for nn, rr, cc in WSPECS if nn == n][0]
            step = 256
            for r0 in range(0, r, step):
                r1 = min(r, r0 + step)
                DG(lambda e, r0=r0, r1=r1: e.dma_start(out=wb[n][r0:r1, :], in_=dr[n][r0:r1, :]), w=[t_wb[n]])

        need_w = {"ssm": ["w_in", "w_glu"], "pa": ["w_in"], "attn": ["w_in", "w_uq", "w_ukv"], "pb": ["w_in"]}.get(stage, [n for n, _, _ in WSPECS])
        for n in need_w:
            precast(n)

        def rstd_act(dst, src, n):
            return [lambda e: e.activation(out=dst, in_=src, func=AF.Ln, scale=1.0 / n, bias=EPS),
                    lambda e: e.activation(out=dst, in_=dst, func=AF.Exp, scale=-0.5)]

        o_toep, o_mt, o_ca, o_a8, o_uT = 0, 16 * KB, 32 * KB, 50 * KB, 52 * KB
        Toep = view(o_toep, [64, 128], BF16)
        MT = view(o_mt, [64, 2, 64], BF16)
        CAall = view(o_ca, [32, 2, 9, 16], BF16)
        A8dup = view(o_a8, [32, 2], F32)
        A8sw = view(o_a8 + 256, [32, 2], F32)
        uT = view(o_uT, [64, 256], BF16)
        t_toep, t_mt, t_ca, t_a8, t_uT = P.toks(5, "tab")

        def _phase0():
            base = 84 * KB
            cur = [base]

            def al(shape, dt, parts=(0, 128)):
                esz = 4 if dt in (F32, I32) else 2
                nb = int(np.prod(shape)) * esz
                nb = (nb + 31) // 32 * 32
                v = view(cur[0], shape, dt, parts)
                cur[0] += nb
                return v

            s32 = lambda: al([32], F32)
            LR, LI, LDT, DT, XX, TH, MAG, CS, SN, AR, AI, DEN, NR, CFR, CFI, T1, T2, T3 = [s32() for _ in range(18)]
            TI = al([32], I32)
            BRE = al([32, 16], F32); BIM = al([32, 16], F32)
            BBR = al([32, 16], F32); BBI = al([32, 16], F32)
            CST = [al([4, 128], F32), al([4, 128], F32)]
            CT = [al([32, 16], F32), al([32, 16], F32)]
            APr = al([9, 32], F32); APi = al([9, 32], F32)
            AB = al([32, 2, 128], BF16)
            BBbf = al([32, 2, 16], BF16)
            Kflat = al([64, 128], BF16, (0, 16))
            Dbc = al([64, 16], F32, (0, 16))
            Dd = al([64, 16], F32, (0, 16))
            W1 = al([32, 16], F32); W2 = al([32, 16], F32); W3 = al([32, 16], F32)
            Z0 = al([64, 112], BF16, (0, 16))
            t0 = P.tok("p0")
            tS = [t0]

            for gh in range(2):
                ps_ = slice(64 * gh, 64 * gh + 64)
                gs = slice(32 * gh, 32 * gh + 32)
                DS(lambda e, ps_=ps_, gs=gs: e.dma_start(out=LR[ps_, :], in_=dr["lam_re"][gs, :].rearrange("g p -> p g"), allow_slow_non_contiguous=True), w=[t0])
                DS(lambda e, ps_=ps_, gs=gs: e.dma_start(out=LI[ps_, :], in_=dr["lam_im"][gs, :].rearrange("g p -> p g"), allow_slow_non_contiguous=True), w=[t0])
                DS(lambda e, ps_=ps_, gs=gs: e.dma_start(out=LDT[ps_, :], in_=dr["log_dt"][0, gs].partition_broadcast(64)), w=[t0])
                DS(lambda e, ps_=ps_, gs=gs: e.dma_start(out=BRE[ps_, :, :], in_=dr["b_re"][gs, :, :].rearrange("g p h -> p g h")), w=[t0])
                DS(lambda e, ps_=ps_, gs=gs: e.dma_start(out=BIM[ps_, :, :], in_=dr["b_im"][gs, :, :].rearrange("g p h -> p g h")), w=[t0])
                for gt in range(4):
                    g0 = 32 * gh + 8 * gt
                    for ri, nm in enumerate(("c_re", "c_im")):
                        DS(lambda e, g0=g0, gt=gt, gh=gh, ri=ri, nm=nm: e.dma_start(out=CST[ri][:, gt, 64 * gh:64 * gh + 64], in_=dr[nm][g0:g0 + 8, :, :].rearrange("g h p -> (g h) p")), w=[t0])
            DS(lambda e: e.dma_start(out=Dbc[:, :, :].rearrange("p g h -> p (g h)"), in_=dr["d_skip"][0, :].partition_broadcast(16)), w=[t0])
            V(lambda e: e.memset(Z0[:], 0.0), w=[t0])

            def vv(fn):
                V(fn, r=[t0, t_const], w=[t0])

            def aa(fn):
                A(fn, r=[t0], w=[t0])

            vv(lambda e: e.tensor_scalar(out=LR[:], in0=LR[:], scalar1=-1e-4, scalar2=None, op0=ALU.min))
            aa(lambda e: e.activation(out=DT[:], in_=LDT[:], func=AF.Exp))
            vv(lambda e: e.tensor_tensor(out=XX[:], in0=LR[:], in1=DT[:], op=ALU.mult))
            vv(lambda e: e.tensor_tensor(out=TH[:], in0=LI[:], in1=DT[:], op=ALU.mult))
            vv(lambda e: e.tensor_scalar(out=MAG[:], in0=XX[:], scalar1=1.0 / 8, scalar2=1.0, op0=ALU.mult, op1=ALU.add))
            for k in range(7, 0, -1):
                vv(lambda e, k=k: e.scalar_tensor_tensor(out=MAG[:], in0=XX[:], scalar=1.0 / k, in1=MAG[:], op0=ALU.mult, op1=ALU.mult))
                vv(lambda e: e.tensor_scalar(out=MAG[:], in0=MAG[:], scalar1=1.0, scalar2=None, op0=ALU.add))

            def trig(dst, shift):
                vv(lambda e: e.tensor_scalar(out=T1[:], in0=TH[:], scalar1=float(shift), scalar2=1.0 / (2 * math.pi), op0=ALU.add, op1=ALU.mult))
                vv(lambda e: e.tensor_copy(out=TI[:], in_=T1[:]))
                vv(lambda e: e.tensor_copy(out=T2[:], in_=TI[:]))
                vv(lambda e: e.tensor_scalar(out=T1[:], in0=TH[:], scalar1=float(shift), scalar2=None, op0=ALU.add))
                vv(lambda e: e.scalar_tensor_tensor(out=T1[:], in0=T2[:], scalar=-2 * math.pi, in1=T1[:], op0=ALU.mult, op1=ALU.add))
                vv(lambda e: e.tensor_scalar(out=T1[:], in0=T1[:], scalar1=-math.pi, scalar2=math.pi, op0=ALU.max, op1=ALU.min))
                aa(lambda e: e.activation(out=dst, in_=T1[:], func=AF.Sin))

            trig(SN[:], 0.0)
            trig(CS[:], math.pi / 2)
            vv(lambda e: e.tensor_tensor(out=AR[:], in0=MAG[:], in1=CS[:], op=ALU.mult))
            vv(lambda e: e.tensor_tensor(out=AI[:], in0=MAG[:], in1=SN[:], op=ALU.mult))
            vv(lambda e: e.tensor_tensor(out=DEN[:], in0=LR[:], in1=LR[:], op=ALU.mult))
            vv(lambda e: e.tensor_tensor(out=T1[:], in0=LI[:], in1=LI[:], op=ALU.mult))
            vv(lambda e: e.tensor_tensor(out=DEN[:], in0=DEN[:], in1=T1[:], op=ALU.add))
            vv(lambda e: e.reciprocal(out=DEN[:], in_=DEN[:]))
            vv(lambda e: e.tensor_scalar(out=NR[:], in0=AR[:], scalar1=-1.0, scalar2=None, op0=ALU.add))
            vv(lambda e: e.tensor_tensor(out=T1[:], in0=NR[:], in1=LR[:], op=ALU.mult))
            vv(lambda e: e.tensor_tensor(out=T2[:], in0=AI[:], in1=LI[:], op=ALU.mult))
            vv(lambda e: e.tensor_tensor(out=T1[:], in0=T1[:], in1=T2[:], op=ALU.add))
            vv(lambda e: e.tensor_tensor(out=CFR[:], in0=T1[:], in1=DEN[:], op=ALU.mult))
            vv(lambda e: e.tensor_tensor(out=T1[:], in0=AI[:], in1=LR[:], op=ALU.mult))
            vv(lambda e: e.tensor_tensor(out=T2[:], in0=NR[:], in1=LI[:], op=ALU.mult))
            vv(lambda e: e.tensor_tensor(out=T1[:], in0=T1[:], in1=T2[:], op=ALU.subtract))
            vv(lambda e: e.tensor_tensor(out=CFI[:], in0=T1[:], in1=DEN[:], op=ALU.mult))
            vv(lambda e: e.memset(APr[:, 0, :], 1.0))
            vv(lambda e: e.memset(APi[:, 0, :], 0.0))
            vv(lambda e: e.tensor_copy(out=APr[:, 1, :], in_=AR[:]))
            vv(lambda e: e.tensor_copy(out=APi[:, 1, :], in_=AI[:]))
            for k in range(2, 9):
                vv(lambda e, k=k: e.tensor_tensor(out=T1[:], in0=APr[:, k - 1, :], in1=AR[:], op=ALU.mult))
                vv(lambda e, k=k: e.tensor_tensor(out=T2[:], in0=APi[:, k - 1, :], in1=AI[:], op=ALU.mult))
                vv(lambda e, k=k: e.tensor_tensor(out=APr[:, k, :], in0=T1[:], in1=T2[:], op=ALU.subtract))
                vv(lambda e, k=k: e.tensor_tensor(out=T1[:], in0=APr[:, k - 1, :], in1=AI[:], op=ALU.mult))
                vv(lambda e, k=k: e.tensor_tensor(out=T2[:], in0=APi[:, k - 1, :], in1=AR[:], op=ALU.mult))
                vv(lambda e, k=k: e.tensor_tensor(out=APi[:, k, :], in0=T1[:], in1=T2[:], op=ALU.add))
            V(lambda e: e.tensor_copy(out=A8dup[:, :, 0], in_=APr[:, 8, :]), r=[t0], w=[t_a8])
            V(lambda e: e.tensor_copy(out=A8dup[:, :, 1], in_=APr[:, 8, :]), r=[t0], w=[t_a8])
            V(lambda e: e.tensor_scalar(out=A8sw[:, :, 0], in0=APi[:, 8, :], scalar1=-1.0, scalar2=None, op0=ALU.mult), r=[t0], w=[t_a8])
            V(lambda e: e.tensor_copy(out=A8sw[:, :, 1], in_=APi[:, 8, :]), r=[t0], w=[t_a8])

            def bc16(a):
                return a.unsqueeze(2).to_broadcast([128, 32, 16])

            vv(lambda e: e.tensor_tensor(out=W1[:], in0=BRE[:], in1=bc16(CFR[:]), op=ALU.mult))
            vv(lambda e: e.tensor_tensor(out=W2[:], in0=BIM[:], in1=bc16(CFI[:]), op=ALU.mult))
            vv(lambda e: e.tensor_tensor(out=BBR[:], in0=W1[:], in1=W2[:], op=ALU.subtract))
            vv(lambda e: e.tensor_tensor(out=W1[:], in0=BIM[:], in1=bc16(CFR[:]), op=ALU.mult))
            vv(lambda e: e.tensor_tensor(out=W2[:], in0=BRE[:], in1=bc16(CFI[:]), op=ALU.mult))
            vv(lambda e: e.tensor_tensor(out=BBI[:], in0=W1[:], in1=W2[:], op=ALU.add))
            vv(lambda e: e.tensor_copy(out=BBbf[:, :, 0, :], in_=BBR[:]))
            vv(lambda e: e.tensor_copy(out=BBbf[:, :, 1, :], in_=BBI[:]))
            for ri in range(2):
                for gt in range(4):
                    T(lambda e, ri=ri, gt=gt: e.transpose(out=pb[ri][:, 128 * gt:128 * gt + 128], in_=CST[ri][:, gt, :], identity=ident_f[:]), r=[t0, t_const], w=[tb[ri]])
                V(lambda e, ri=ri: e.tensor_copy(out=CT[ri][:, :, :].rearrange("p g h -> p (g h)"), in_=pb[ri][:, :]), r=[tb[ri], t0], w=[t0])
            for d in range(9):
                vv(lambda e, d=d: e.tensor_tensor(out=W1[:], in0=CT[0][:], in1=bc16(APr[:, d, :]), op=ALU.mult))
                vv(lambda e, d=d: e.tensor_tensor(out=W2[:], in0=CT[1][:], in1=bc16(APi[:, d, :]), op=ALU.mult))
                V(lambda e, d=d: e.tensor_tensor(out=CAall[:, :, 0, d, :], in0=W1[:], in1=W2[:], op=ALU.subtract), r=[t0], w=[t_ca])
                vv(lambda e, d=d: e.tensor_tensor(out=W1[:], in0=CT[0][:], in1=bc16(APi[:, d, :]), op=ALU.mult))
                vv(lambda e, d=d: e.tensor_tensor(out=W2[:], in0=CT[1][:], in1=bc16(APr[:, d, :]), op=ALU.mult))
                vv(lambda e, d=d: e.tensor_tensor(out=W1[:], in0=W1[:], in1=W2[:], op=ALU.add))
                V(lambda e, d=d: e.tensor_scalar(out=CAall[:, :, 1, d, :], in0=W1[:], scalar1=-1.0, scalar2=None, op0=ALU.mult), r=[t0], w=[t_ca])
            ABv = AB[:, :, :, :].rearrange("p g r (j h) -> p g r j h", j=8)
            for j in range(8):
                k = 7 - j
                vv(lambda e, k=k: e.tensor_tensor(out=W1[:], in0=BBR[:], in1=bc16(APr[:, k, :]), op=ALU.mult))
                vv(lambda e, k=k: e.tensor_tensor(out=W2[:], in0=BBI[:], in1=bc16(APi[:, k, :]), op=ALU.mult))
                vv(lambda e, j=j: e.tensor_tensor(out=ABv[:, :, 0, j, :], in0=W1[:], in1=W2[:], op=ALU.subtract))
                vv(lambda e, k=k: e.tensor_tensor(out=W1[:], in0=BBR[:], in1=bc16(APi[:, k, :]), op=ALU.mult))
                vv(lambda e, k=k: e.tensor_tensor(out=W2[:], in0=BBI[:], in1=bc16(APr[:, k, :]), op=ALU.mult))
                vv(lambda e, j=j: e.tensor_tensor(out=ABv[:, :, 1, j, :], in0=W1[:], in1=W2[:], op=ALU.add))
            for blk in range(8):
                bk = 2 + (blk % 2)
                for s in range(8):
                    gg = blk * 4 + s // 2
                    ri = s % 2
                    T(lambda e, bk=bk, s=s, gg=gg, ri=ri: e.transpose(out=pbb[bk][:, 128 * s:128 * s + 128], in_=AB[:, gg, ri, :], identity=ident_bf[:]), r=[t0, t_const], w=[tb[bk]])
                for gh in range(2):
                    src = pbb[bk][:, :].rearrange("p (g r h q) -> p g r h q", g=4, r=2, h=2)[:, :, :, gh, :]
                    dst = MT[:, 32 * gh + blk * 4:32 * gh + blk * 4 + 4, :, :]
                    V(lambda e, src=src, dst=dst: e.tensor_copy(out=dst, in_=src), r=[tb[bk]], w=[t_mt])
            vv(lambda e: e.tensor_tensor(out=Dd[:], in0=Dbc[:], in1=ident_f[0:16, 0:16].unsqueeze(1).to_broadcast([16, 64, 16]), op=ALU.mult))
            for blk in range(16):
                bk = 4 + (blk % 2)
                for s in range(4):
                    g = blk * 4 + s
                    gh, gg = g // 32, g % 32
                    prt = slice(64 * gh, 64 * gh + 64)
                    for ri in range(2):
                        T(lambda e, bk=bk, s=s, prt=prt, gg=gg, ri=ri: e.matmul(pb[bk][0:16, 128 * s:128 * s + 128], lhsT=BBbf[prt, gg, ri, :], rhs=CAall[prt, gg, ri, 0:8, :].rearrange("p d h -> p (d h)"), start=(ri == 0), stop=(ri == 1)), r=[t0, t_ca], w=[tb[bk]])
                V(lambda e, bk=bk, blk=blk: e.tensor_copy(out=Kflat[:, 4 * blk:4 * blk + 4, :], in_=pb[bk][0:16, :].rearrange("p (g c) -> p g c", g=4)), r=[tb[bk], t0], w=[t0])
                V(lambda e, bk=bk, blk=blk: e.tensor_tensor(out=Kflat[:, 4 * blk:4 * blk + 4, 0:16], in0=pb[bk][0:16, :].rearrange("p (g c) -> p g c", g=4)[:, :, 0:16], in1=Dd[:, 4 * blk:4 * blk + 4, :], op=ALU.add), r=[tb[bk], t0], w=[t0])
            t_ks = P.tok("kscr")
            DS(lambda e: e.dma_start(out=kscr[:, :, 7:15, :].rearrange("g h d e -> h g (d e)"), in_=Kflat[:, :, :]), r=[t0], w=[t_ks])
            DS(lambda e: e.dma_start(out=kscr[:, :, 0:7, :].rearrange("g h d e -> h g (d e)"), in_=Z0[:, :, :]), r=[t0], w=[t_ks])
            kflat_d = kscr.rearrange("g h d e -> h g (d e)")
            for j in range(8):
                DS(lambda e, j=j: e.dma_start(out=Toep[16 * j:16 * j + 16, :, :], in_=kflat_d[:, :, (7 - j) * 16:(7 - j) * 16 + 128]), r=[t_ks], w=[t_toep])
            barrier()

        if stage in ("all", "ssm", "p0"):
            _phase0()

        def load_gain_bc(dst, name, tok):
            DS(lambda e: e.dma_start(out=dst, in_=dr[name][0, :].partition_broadcast(128)), w=[tok])

        def _phase1():
            Wu = view(84 * KB, [16, 1024], BF16)
            hT = view(116 * KB, [16, 1024], BF16)
            xbuf = [view(148 * KB, [2048], F32), view(156 * KB, [2048], F32)]
            hb = [view(164 * KB, [2048], BF16), view(168 * KB, [2048], BF16)]
            junk = view(172 * KB, [2048], BF16)
            gmix = view(176 * KB, [2048], F32)
            ublk = view(184 * KB, [64, 8, 16], BF16)
            stat = sbt("statA", [128, 32], F32)
            t_Wu, t_hT, t_junk, t_gm, t_ublk, t_stat = P.toks(6, "pa")
            t_x = P.toks(2, "x"); t_hb = P.toks(2, "hb")
            load_gain_bc(gmix, "g_mix_norm", t_gm)
            DS(lambda e: e.dma_start(out=Wu, in_=wb["w_in"][:, 832:1856].rearrange("(c p) n -> p c n", p=128)), r=[t_wb["w_in"]], w=[t_Wu])
            for hf in range(2):
                for i in range(8):
                    ti = hf * 8 + i
                    b = ti % 2
                    DS(lambda e, ti=ti, b=b: e.dma_start(out=xbuf[b], in_=dr["x"][128 * ti:128 * ti + 128, :]), w=[t_x[b]])
                    A(lambda e, b=b, ti=ti: e.activation(out=junk, in_=xbuf[b], func=AF.Square, accum_out=stat[:, ti:ti + 1]), r=[t_x[b]], w=[t_junk, t_stat])
                    for f in rstd_act(stat[:, 16 + ti:17 + ti], stat[:, ti:ti + 1], 2048):
                        A(f, r=[t_stat], w=[t_stat])
                    V(lambda e, b=b, ti=ti: e.scalar_tensor_tensor(out=hb[b], in0=xbuf[b], scalar=stat[:, 16 + ti:17 + ti], in1=gmix, op0=ALU.mult, op1=ALU.mult), r=[t_x[b], t_stat, t_gm], w=[t_hb[b]])
                    bk = [0, 1, 2, 3][(2 * ti) % 4:(2 * ti) % 4 + 2]
                    for kc in range(16):
                        bb = bk[kc // 8]
                        T(lambda e, b=b, kc=kc, bb=bb: e.transpose(out=pbb[bb][:, 128 * (kc % 8):128 * (kc % 8) + 128], in_=hb[b][:, 128 * kc:128 * kc + 128], identity=ident_bf[:]), r=[t_hb[b], t_const], w=[tb[bb]])
                    srcs = [pbb[bk[q]][:, :].rearrange("p (k t) -> p k t", k=8) for q in range(2)]
                    dsts = [hT[:, 8 * q:8 * q + 8, 128 * i:128 * i + 128] for q in range(2)]
                    A(lambda e, s_=srcs[0], d_=dsts[0]: e.activation(out=d_, in_=s_, func=AF.Copy), r=[tb[bk[0]]], w=[t_hT])
                    V(lambda e, s_=srcs[1], d_=dsts[1]: e.tensor_copy(out=d_, in_=s_), r=[tb[bk[1]]], w=[t_hT])
                for tau in range(8):
                    for cc in range(2):
                        bk = 4 + (tau * 2 + cc) % 4
                        for kc in range(16):
                            T(lambda e, bk=bk, kc=kc, tau=tau, cc=cc: e.matmul(pb[bk][:, :], lhsT=hT[:, kc, tau:1024:8], rhs=Wu[:, kc, 512 * cc:512 * cc + 512], start=(kc == 0), stop=(kc == 15)), r=[t_hT, t_Wu], w=[tb[bk]])
                        src = pb[bk][:, :].rearrange("p (g h) -> p g h", g=32)
                        dst = ublk[:, 32 * cc:32 * cc + 32, tau, :]
                        if cc == 0:
                            A(lambda e, src=src, dst=dst: e.activation(out=dst, in_=src, func=AF.Copy), r=[tb[bk]], w=[t_ublk])
                        else:
                            V(lambda e, src=src, dst=dst: e.tensor_copy(out=dst, in_=src), r=[tb[bk]], w=[t_ublk])
                for g8 in range(8):
                    bk = g8 % 2
                    for s in range(8):
                        g = g8 * 8 + s
                        T(lambda e, bk=bk, s=s, g=g: e.transpose(out=pbb[bk][:, 128 * s:128 * s + 128], in_=ublk[:, g, :, :].rearrange("p t h -> p (t h)"), identity=ident_bf[:]), r=[t_ublk, t_const], w=[tb[bk]])
                    src = pbb[bk][:, :].rearrange("p (g c) -> p g c", g=8)
                    dst = uT[:, 8 * g8:8 * g8 + 8, 128 * hf:128 * hf + 128]
                    if g8 % 2 == 0:
                        A(lambda e, src=src, dst=dst: e.activation(out=dst, in_=src, func=AF.Copy), r=[tb[bk]], w=[t_uT])
                    else:
                        V(lambda e, src=src, dst=dst: e.tensor_copy(out=dst, in_=src), r=[tb[bk]], w=[t_uT])
            barrier()

        if stage in ("all", "ssm", "pa"):
            _phase1()

        t_dbg = P.tok("dbg")

        def dump(src, col0, ncols, parts=128, rd=(), dst=None):
            d_ = dbg[0:parts, col0:col0 + ncols] if dst is None else dst
            DS(lambda e: e.dma_start(out=d_, in_=src, allow_slow_non_contiguous=True), r=list(rd), w=[t_dbg])

        osT = view(0, [8, 2048], BF16)
        t_osT = P.tok("osT")
        def _phase2():
            dS = view(84 * KB, [32, 2, 256], F32)
            tmp1 = view(148 * KB, [32, 2], F32)
            tmp2 = view(148 * KB + 256, [32, 2], F32)
            Sb = view(149 * KB, [32, 2, 257], BF16)
            t_dS, t_tmp1, t_tmp2, t_Sb = P.toks(4, "ps")
            for gg in range(32):
                bk = gg % 4
                for gh in range(2):
                    g = 32 * gh + gg
                    for ri in range(2):
                        T(lambda e, bk=bk, gh=gh, g=g, ri=ri: e.matmul(pb[bk][64 * gh:64 * gh + 64, 256 * ri:256 * ri + 256], lhsT=MT[:, g, ri, :], rhs=uT[:, g, :], start=True, stop=True), r=[t_mt, t_uT], w=[tb[bk]])
                if gg % 2 == 0:
                    V(lambda e, bk=bk, gg=gg: e.tensor_copy(out=dS[:, gg, :, :].rearrange("p r c -> p (r c)"), in_=pb[bk][:, :]), r=[tb[bk]], w=[t_dS])
                else:
                    A(lambda e, bk=bk, gg=gg: e.activation(out=dS[:, gg, :, :].rearrange("p r c -> p (r c)"), in_=pb[bk][:, :], func=AF.Copy), r=[tb[bk]], w=[t_dS])
            for c in range(1, 256):
                V(lambda e, c=c: e.tensor_tensor(out=tmp1, in0=dS[:, :, :, c - 1], in1=A8dup, op=ALU.mult), r=[t_dS, t_a8], w=[t_tmp1])
                V(lambda e, c=c: e.tensor_tensor(out=tmp2, in0=dS[:, :, ::-1, c - 1], in1=A8sw, op=ALU.mult), r=[t_dS, t_a8], w=[t_tmp2])
                V(lambda e, c=c: e.tensor_tensor(out=dS[:, :, :, c], in0=dS[:, :, :, c], in1=tmp1, op=ALU.add), r=[t_tmp1, t_dS], w=[t_dS])
                V(lambda e, c=c: e.tensor_tensor(out=dS[:, :, :, c], in0=dS[:, :, :, c], in1=tmp2, op=ALU.add), r=[t_tmp2, t_dS], w=[t_dS])
            V(lambda e: e.memset(Sb[:, :, :, 0:1], 0.0), w=[t_Sb])
            V(lambda e: e.tensor_copy(out=Sb[:, 0:16, :, 1:257], in_=dS[:, 0:16, :, :]), r=[t_dS], w=[t_Sb])
            A(lambda e: e.activation(out=Sb[:, 16:32, :, 1:257], in_=dS[:, 16:32, :, :], func=AF.Copy), r=[t_dS], w=[t_Sb])
            if stage == "ssm":
                dump(dS[:, :, :, 7::8], 8192, 2048, rd=[t_dS], dst=dbg[:, 8192:10240].rearrange("p (g r c) -> p g r c", g=32, r=2))
            barrier()
            gB = view(84 * KB, [2, 8, 1024], BF16)
            gT = view(116 * KB, [8, 2048], BF16)
            sqt = view(182 * KB, [512], F32)
            wv = view(184 * KB, [512], F32)
            sg = view(186 * KB, [512], F32)
            t_gB, t_gT, t_sqt, t_wv, t_sg = P.toks(5, "ps2")
            for ct in range(2):
                for g4 in range(16):
                    bk = g4 % 4
                    for s in range(4):
                        g = 4 * g4 + s
                        gh, gg = g // 32, g % 32
                        prt = slice(64 * gh, 64 * gh + 64)
                        o_ = pb[bk][:, 128 * s:128 * s + 128]
                        T(lambda e, o_=o_, g=g, ct=ct: e.matmul(o_, lhsT=uT[:, g, 128 * ct:128 * ct + 128], rhs=Toep[:, g, :], start=True, stop=False), r=[t_uT, t_toep], w=[tb[bk]])
                        for ri in range(2):
                            T(lambda e, o_=o_, prt=prt, gg=gg, ri=ri, ct=ct: e.matmul(o_, lhsT=Sb[prt, gg, ri, 128 * ct:128 * ct + 128], rhs=CAall[prt, gg, ri, 1:9, :].rearrange("p d h -> p (d h)"), start=False, stop=(ri == 1)), r=[t_Sb, t_ca], w=[tb[bk]])
                    A(lambda e, bk=bk: e.activation(out=sqt, in_=pb[bk][:, :], func=AF.Square), r=[tb[bk]], w=[t_sqt])
                    V(lambda e: e.tensor_scalar(out=wv, in0=sqt, scalar1=0.044715, scalar2=1.0, op0=ALU.mult, op1=ALU.add), r=[t_sqt], w=[t_wv])
                    V(lambda e, bk=bk: e.tensor_tensor(out=wv, in0=wv, in1=pb[bk][:, :], op=ALU.mult), r=[t_wv, tb[bk]], w=[t_wv])
                    A(lambda e: e.activation(out=sg, in_=wv, func=AF.Sigmoid, scale=1.5957691216057308), r=[t_wv], w=[t_sg])
                    dst = gB[:, ct, :, 64 * g4:64 * g4 + 64].rearrange("p t (g h) -> p g t h", g=4)
                    V(lambda e, bk=bk, dst=dst: e.tensor_tensor(out=dst, in0=sg.rearrange("p (g t h) -> p g t h", g=4, t=8), in1=pb[bk][:, :].rearrange("p (g t h) -> p g t h", g=4, t=8), op=ALU.mult), r=[t_sg, tb[bk]], w=[t_gB])
            for ct in range(2):
                for tau in range(8):
                    bk = 4 + (ct * 8 + tau) % 4
                    for kc in range(8):
                        T(lambda e, bk=bk, kc=kc, ct=ct, tau=tau: e.transpose(out=pbb[bk][:, 128 * kc:128 * kc + 128], in_=gB[:, ct, tau, 128 * kc:128 * kc + 128], identity=ident_bf[:]), r=[t_gB, t_const], w=[tb[bk]])
                    src = pbb[bk][:, :].rearrange("p (k c) -> p k c", k=8)
                    dst = gT[:, :, 1024 * ct + tau:1024 * ct + 1024:8]
                    if tau % 2 == 0:
                        V(lambda e, src=src, dst=dst: e.tensor_copy(out=dst, in_=src), r=[tb[bk]], w=[t_gT])
                    else:
                        A(lambda e, src=src, dst=dst: e.activation(out=dst, in_=src, func=AF.Copy), r=[tb[bk]], w=[t_gT])
            if stage == "ssm":
                gdump = view(149 * KB, [2048], F32)
                t_gd = P.tok("gd")
                V(lambda e: e.tensor_copy(out=gdump, in_=gT[:, 3, :]), r=[t_gT], w=[t_gd])
                dump(gdump, 2048, 2048, rd=[t_gd])
            barrier()
            Wglu = view(148 * KB, [8, 1024], BF16)
            sqs = view(164 * KB, [8, 512], BF16)
            sgb = view(172 * KB, [512], BF16)
            rb = view(173 * KB, [512], BF16)
            rbf = view(174 * KB, [512], F32)
            t_Wglu, t_sqs, t_sgb, t_rb = P.toks(4, "ps3")
            DS(lambda e: e.dma_start(out=Wglu, in_=wb["w_glu"].rearrange("(c p) n -> p c n", p=128)), r=[t_wb["w_glu"]], w=[t_Wglu])
            for tq in range(4):
                tsl = slice(512 * tq, 512 * tq + 512)
                for oc in range(8):
                    bk = oc % 4
                    for kc in range(8):
                        T(lambda e, bk=bk, kc=kc, oc=oc, tsl=tsl: e.matmul(pb[bk][:, :], lhsT=Wglu[:, kc, 128 * oc:128 * oc + 128], rhs=gT[:, kc, tsl], start=(kc == 0), stop=(kc == 7)), r=[t_Wglu, t_gT], w=[tb[bk]])
                    A(lambda e, bk=bk, oc=oc: e.activation(out=sgb, in_=pb[bk][:, :], func=AF.Sigmoid, bias=colv[:, 0, oc:oc + 1]), r=[tb[bk], t_col], w=[t_sgb])
                    V(lambda e, oc=oc, tsl=tsl: e.scalar_tensor_tensor(out=osT[:, oc, tsl], in0=gT[:, oc, tsl], scalar=colv[:, 1, oc:oc + 1], in1=sgb, op0=ALU.mult, op1=ALU.mult), r=[t_gT, t_col, t_sgb], w=[t_osT])
                    A(lambda e, oc=oc, tsl=tsl: e.activation(out=sqs[:, oc, :], in_=osT[:, oc, tsl], func=AF.Square, scale=colv[:, 2, oc:oc + 1]), r=[t_osT, t_col], w=[t_sqs])
                bq = 4 + tq % 2
                for oc in range(8):
                    T(lambda e, bq=bq, oc=oc: e.matmul(pb[bq][:, :], lhsT=ones_bf[:], rhs=sqs[:, oc, :], start=(oc == 0), stop=(oc == 7)), r=[t_sqs, t_const], w=[tb[bq]])
                A(lambda e, bq=bq: e.activation(out=rbf, in_=pb[bq][:, :], func=AF.Ln, scale=1.0 / 1024, bias=EPS), r=[tb[bq]], w=[t_rb])
                A(lambda e: e.activation(out=rbf, in_=rbf, func=AF.Exp, scale=-0.5), r=[t_rb], w=[t_rb])
                for oc in range(8):
                    V(lambda e, oc=oc, tsl=tsl: e.tensor_tensor(out=osT[:, oc, tsl], in0=osT[:, oc, tsl], in1=rbf, op=ALU.mult), r=[t_rb, t_osT], w=[t_osT])
            if stage == "ssm":
                od = view(84 * KB, [2048], F32)
                t_od = P.tok("od")
                V(lambda e: e.tensor_copy(out=od, in_=osT[:, 5, :]), r=[t_osT], w=[t_od])
                dump(od, 0, 2048, rd=[t_od])
            barrier()


        if stage in ("all", "ssm"):
            _phase2()

        cqT = view(32 * KB, [4, 2048], BF16)
        ckvT = view(48 * KB, [2, 2048], BF16)
        Rk = view(56 * KB, [16, 64], F32)
        CS2 = view(72 * KB, [16, 2, 32], F32)
        SN2 = view(76 * KB, [16, 2, 32], F32)
        oaT = view(80 * KB, [8, 2048], BF16)
        t_cqT, t_ckvT, t_Rk, t_trig, t_oaT = P.toks(5, "pb")
        statB = sbt("statB", [128, 16, 8], F32)
        t_statB = P.tok("statB")
        gsm = sbt("gsm", [128, 1152], F32)
        t_gsm = P.tok("gsm")
        statC = sbt("statC", [128, 16, 8], F32)
        ssqA = sbt("ssqA", [128, 16, 8], F32)
        t_statC, t_ssqA = P.toks(2, "pc2")
        def _phase3():
            Wq = view(80 * KB, [16, 832], BF16)
            xbuf = [view(112 * KB, [2048], F32), view(120 * KB, [2048], F32)]
            hb = [view(128 * KB, [2048], BF16), view(132 * KB, [2048], BF16)]
            junk = view(136 * KB, [2048], BF16)
            gmix = view(140 * KB, [2048], F32)
            hTt = [view(148 * KB, [16, 128], BF16), view(152 * KB, [16, 128], BF16)]
            cqb = view(156 * KB, [512], BF16)
            ckvb = view(157 * KB, [256], BF16)
            krg = view(158 * KB, [2, 32], F32)
            kA = view(158 * KB + 256, [2, 32], F32)
            kB_ = view(158 * KB + 512, [2, 32], F32)
            posi = view(159 * KB, [16], I32)
            posf = view(159 * KB + 64, [16], F32)
            invf = view(159 * KB + 128, [32], F32)
            ANG = view(160 * KB, [16, 32], F32)
            AN2 = view(162 * KB, [16, 32], F32)
            AN3 = view(164 * KB, [16, 32], F32)
            ANI = view(166 * KB, [16, 32], I32)
            t_Wq, t_junk, t_gm, t_cqb, t_ckvb, t_krg, t_ang = P.toks(7, "pb2")
            t_x = P.toks(2, "x"); t_hb = P.toks(2, "hb"); t_hTt = P.toks(2, "hTt")
            load_gain_bc(gmix, "g_mix_norm", t_gm)
            for nm, a, b_ in (("g_q_lora", 0, 512), ("g_kv_lora", 512, 768), ("g_q_head", 768, 960), ("g_k_head", 960, 1152)):
                DS(lambda e, nm=nm, a=a, b_=b_: e.dma_start(out=gsm[:, a:b_], in_=dr[nm][0, :].partition_broadcast(128)), w=[t_gsm])
            V(lambda e: e.tensor_scalar(out=gsm[:, 768:960], in0=gsm[:, 768:960], scalar1=192 ** -0.5, scalar2=None, op0=ALU.mult), r=[t_gsm], w=[t_gsm])
            DS(lambda e: e.dma_start(out=Wq, in_=wb["w_in"][:, 0:832].rearrange("(c p) n -> p c n", p=128)), r=[t_wb["w_in"]], w=[t_Wq])
            DS(lambda e: e.dma_start(out=posi, in_=dr["pos"][0, :].rearrange("(t p) -> p t", p=128), allow_slow_non_contiguous=True), w=[t_ang])
            V(lambda e: e.tensor_copy(out=posf, in_=posi), r=[t_ang], w=[t_ang])
            for i in range(32):
                val = float(np.float32(1.0) / np.float32(np.float32(10000.0) ** np.float32(np.float32(2 * i) / np.float32(64.0))))
                G(lambda e, i=i, val=val: e.memset(invf[:, i:i + 1], val), w=[t_ang])
            V(lambda e: e.tensor_tensor(out=ANG, in0=posf.unsqueeze(2).to_broadcast([128, 16, 32]), in1=invf.unsqueeze(1).to_broadcast([128, 16, 32]), op=ALU.mult), r=[t_ang], w=[t_ang])

            def trig2(dst, shift, sgn):
                V(lambda e: e.tensor_scalar(out=AN2, in0=ANG, scalar1=float(shift), scalar2=1.0 / (2 * math.pi), op0=ALU.add, op1=ALU.mult), r=[t_ang], w=[t_ang])
                V(lambda e: e.tensor_copy(out=ANI, in_=AN2), r=[t_ang], w=[t_ang])
                V(lambda e: e.tensor_copy(out=AN3, in_=ANI), r=[t_ang], w=[t_ang])
                V(lambda e: e.tensor_scalar(out=AN2, in0=ANG, scalar1=float(shift), scalar2=None, op0=ALU.add), r=[t_ang], w=[t_ang])
                V(lambda e: e.scalar_tensor_tensor(out=AN2, in0=AN3, scalar=-2 * math.pi, in1=AN2, op0=ALU.mult, op1=ALU.add), r=[t_ang], w=[t_ang])
                V(lambda e: e.tensor_scalar(out=AN2, in0=AN2, scalar1=-math.pi, scalar2=math.pi, op0=ALU.max, op1=ALU.min), r=[t_ang], w=[t_ang])
                A(lambda e: e.activation(out=dst, in_=AN2, func=AF.Sin, scale=float(sgn)), r=[t_ang], w=[t_trig])

            trig2(CS2[:, :, 0, :], math.pi / 2, 1.0)
            trig2(CS2[:, :, 1, :], math.pi / 2, 1.0)
            trig2(SN2[:, :, 0, :], 0.0, -1.0)
            trig2(SN2[:, :, 1, :], 0.0, 1.0)

            for ti in range(16):
                b = ti % 2
                tsl = slice(128 * ti, 128 * ti + 128)
                DS(lambda e, ti=ti, b=b: e.dma_start(out=xbuf[b], in_=dr["x"][128 * ti:128 * ti + 128, :]), w=[t_x[b]])
                A(lambda e, b=b, ti=ti: e.activation(out=junk, in_=xbuf[b], func=AF.Square, accum_out=statB[:, ti, 0:1]), r=[t_x[b]], w=[t_junk, t_statB])
                for f in rstd_act(statB[:, ti, 1:2], statB[:, ti, 0:1], 2048):
                    A(f, r=[t_statB], w=[t_statB])
                V(lambda e, b=b, ti=ti: e.scalar_tensor_tensor(out=hb[b], in0=xbuf[b], scalar=statB[:, ti, 1:2], in1=gmix, op0=ALU.mult, op1=ALU.mult), r=[t_x[b], t_statB, t_gm], w=[t_hb[b]])
                bk = [0, 1] if ti % 2 == 0 else [2, 3]
                for kc in range(16):
                    bb = bk[kc // 8]
                    T(lambda e, b=b, kc=kc, bb=bb: e.transpose(out=pbb[bb][:, 128 * (kc % 8):128 * (kc % 8) + 128], in_=hb[b][:, 128 * kc:128 * kc + 128], identity=ident_bf[:]), r=[t_hb[b], t_const], w=[tb[bb]])
                A(lambda e, b=b, bb=bk[0]: e.activation(out=hTt[b][:, 0:8, :], in_=pbb[bb][:, :].rearrange("p (k t) -> p k t", k=8), func=AF.Copy), r=[tb[bk[0]]], w=[t_hTt[b]])
                V(lambda e, b=b, bb=bk[1]: e.tensor_copy(out=hTt[b][:, 8:16, :], in_=pbb[bb][:, :].rearrange("p (k t) -> p k t", k=8)), r=[tb[bk[1]]], w=[t_hTt[b]])
                bq, bkv = (4, 5) if ti % 2 == 0 else (6, 7)
                for kc in range(16):
                    T(lambda e, b=b, kc=kc, bq=bq: e.matmul(pb[bq][:, :], lhsT=hTt[b][:, kc, :], rhs=Wq[:, kc, 0:512], start=(kc == 0), stop=(kc == 15)), r=[t_hTt[b], t_Wq], w=[tb[bq]])
                for kc in range(16):
                    T(lambda e, b=b, kc=kc, bkv=bkv: e.matmul(pb[bkv][:, 0:320], lhsT=hTt[b][:, kc, :], rhs=Wq[:, kc, 512:832], start=(kc == 0), stop=(kc == 15)), r=[t_hTt[b], t_Wq], w=[tb[bkv]])
                A(lambda e, bq=bq, ti=ti: e.activation(out=junk[:, 0:512], in_=pb[bq][:, :], func=AF.Square, accum_out=statB[:, ti, 2:3]), r=[tb[bq]], w=[t_junk, t_statB])
                for f in rstd_act(statB[:, ti, 3:4], statB[:, ti, 2:3], 512):
                    A(f, r=[t_statB], w=[t_statB])
                V(lambda e, bq=bq, ti=ti: e.scalar_tensor_tensor(out=cqb, in0=pb[bq][:, :], scalar=statB[:, ti, 3:4], in1=gsm[:, 0:512], op0=ALU.mult, op1=ALU.mult), r=[tb[bq], t_statB, t_gsm], w=[t_cqb])
                A(lambda e, bkv=bkv, ti=ti: e.activation(out=junk[:, 0:256], in_=pb[bkv][:, 0:256], func=AF.Square, accum_out=statB[:, ti, 4:5]), r=[tb[bkv]], w=[t_junk, t_statB])
                for f in rstd_act(statB[:, ti, 5:6], statB[:, ti, 4:5], 256):
                    A(f, r=[t_statB], w=[t_statB])
                V(lambda e, bkv=bkv, ti=ti: e.scalar_tensor_tensor(out=ckvb, in0=pb[bkv][:, 0:256], scalar=statB[:, ti, 5:6], in1=gsm[:, 512:768], op0=ALU.mult, op1=ALU.mult), r=[tb[bkv], t_statB, t_gsm], w=[t_ckvb])
                A(lambda e, bkv=bkv, ti=ti: e.activation(out=junk[:, 0:64], in_=pb[bkv][:, 256:320], func=AF.Square, accum_out=statB[:, ti, 6:7]), r=[tb[bkv]], w=[t_junk, t_statB])
                V(lambda e, bkv=bkv: e.tensor_tensor(out=krg.rearrange("p a b -> p (a b)"), in0=pb[bkv][:, 256:320], in1=gsm[:, 1088:1152], op=ALU.mult), r=[tb[bkv], t_gsm], w=[t_krg])
                V(lambda e, ti=ti: e.tensor_tensor(out=kA, in0=krg, in1=CS2[:, ti, :, :], op=ALU.mult), r=[t_krg, t_trig], w=[t_krg])
                V(lambda e, ti=ti: e.tensor_tensor(out=kB_, in0=krg[:, ::-1, :], in1=SN2[:, ti, :, :], op=ALU.mult), r=[t_krg, t_trig], w=[t_krg])
                V(lambda e, ti=ti: e.tensor_tensor(out=Rk[:, ti, :].rearrange("p (a b) -> p a b", a=2), in0=kA, in1=kB_, op=ALU.add), r=[t_krg], w=[t_Rk])
                for kc in range(4):
                    T(lambda e, kc=kc, bq=bq: e.transpose(out=pbb[bq][:, 128 * kc:128 * kc + 128], in_=cqb[:, 128 * kc:128 * kc + 128], identity=ident_bf[:]), r=[t_cqb, t_const], w=[tb[bq]])
                for kc in range(2):
                    T(lambda e, kc=kc, bq=bq: e.transpose(out=pbb[bq][:, 512 + 128 * kc:512 + 128 * kc + 128], in_=ckvb[:, 128 * kc:128 * kc + 128], identity=ident_bf[:]), r=[t_ckvb, t_const], w=[tb[bq]])
                A(lambda e, bq=bq, tsl=tsl: e.activation(out=cqT[:, :, tsl], in_=pbb[bq][:, 0:512].rearrange("p (k t) -> p k t", k=4), func=AF.Copy), r=[tb[bq]], w=[t_cqT])
                V(lambda e, bq=bq, tsl=tsl: e.tensor_copy(out=ckvT[:, :, tsl], in_=pbb[bq][:, 512:768].rearrange("p (k t) -> p k t", k=2)), r=[tb[bq]], w=[t_ckvT])
            if stage == "pb":
                od = view(170 * KB, [2048], F32)
                t_od = P.tok("od")
                V(lambda e: e.tensor_copy(out=od, in_=cqT[:, 1, :]), r=[t_cqT], w=[t_od])
                dump(od, 0, 2048, rd=[t_od])
                od2 = view(178 * KB, [2048], F32)
                V(lambda e: e.tensor_copy(out=od2, in_=ckvT[:, 1, :]), r=[t_ckvT], w=[t_od])
                dump(od2, 2048, 2048, rd=[t_od])
                dump(Rk.rearrange("p a b -> p (a b)"), 4096, 1024, rd=[t_Rk])
                dump(CS2.rearrange("p a b c -> p (a b c)"), 5120, 1024, rd=[t_trig])
                dump(SN2.rearrange("p a b c -> p (a b c)"), 6144, 1024, rd=[t_trig])
            barrier()

        if stage in ("all", "attn", "pb"):
            _phase3()

        def _phase4():
            Wuq = view(112 * KB, [4, 1536], BF16)
            Wukv = view(124 * KB, [2, 2048], BF16)
            t_Wuq, t_Wukv = P.toks(2, "pcw")
            DS(lambda e: e.dma_start(out=Wuq, in_=wb["w_uq"].rearrange("(c p) n -> p c n", p=128)), r=[t_wb["w_uq"]], w=[t_Wuq])
            DS(lambda e: e.dma_start(out=Wukv, in_=wb["w_ukv"].rearrange("(c p) n -> p c n", p=128)), r=[t_wb["w_ukv"]], w=[t_Wukv])
            hbuf = []
            for s_ in range(2):
                o0 = (132 + 21 * s_) * KB
                hbuf.append(dict(
                    QT=view(o0, [2048], BF16), QTr=view(o0 + 4 * KB, [2048], BF16), KT=view(o0 + 8 * KB, [2048], BF16),
                    KTr=view(o0 + 12 * KB, [2048], BF16), Vh=view(o0 + 16 * KB, [16, 130], BF16),
                    t=P.toks(5, "hb%d" % s_)))
            Pt = [view((174 + i) * KB, [512], BF16) for i in range(3)]
            t_Pt = P.toks(3, "Pt")
            qr = view(177 * KB, [2, 32], F32)
            qA = view(177 * KB + 256, [2, 32], F32)
            qB = view(177 * KB + 512, [2, 32], F32)
            qb_ = [view(178 * KB, [256], BF16), view(178 * KB + 512, [256], BF16)]
            kb_ = [view(179 * KB, [256], BF16), view(179 * KB + 512, [256], BF16)]
            oh = [view(180 * KB, [128], BF16), view(180 * KB + 256, [128], BF16)]
            RB = view(181 * KB, [128], F32)
            t_qr, t_RB = P.toks(2, "pc1")
            t_qb = P.toks(2, "qb"); t_kb = P.toks(2, "kb"); t_oh = P.toks(2, "oh")
            junkc = view(182 * KB, [128], BF16)
            t_junkc = P.tok("junkc")
            V(lambda e: e.memset(ssqA[:, :, :], 0.0), w=[t_ssqA])
            for s_ in range(2):
                V(lambda e, s_=s_: e.memset(hbuf[s_]["Vh"][:, :, 128:130], 1.0), w=[hbuf[s_]["t"][4]])
                V(lambda e, s_=s_: e.memset(qb_[s_][:, 192:256], 0.0), w=[t_qb[s_]])
                V(lambda e, s_=s_: e.memset(kb_[s_][:, 192:256], 0.0), w=[t_kb[s_]])
            pi_ = [0]
            import os as _os
            NH = int(_os.environ.get('KNH', '8')); KNT = int(_os.environ.get('KNT', '16')); KPART = int(_os.environ.get('KPART', '9')); NQSB = int(_os.environ.get('KNQSB', '4'))
            for h in range(NH):
                HB = hbuf[h % 2]
                tQT, tQTr, tKT, tKTr, tVh = HB["t"]
                for ti in range(KNT):
                    tsl = slice(128 * ti, 128 * ti + 128)
                    bk = 6 + ti % 2
                    pr = ti % 2
                    for kc in range(4):
                        T(lambda e, bk=bk, kc=kc, tsl=tsl, h=h: e.matmul(pb[bk][:, 0:192], lhsT=cqT[:, kc, tsl], rhs=Wuq[:, kc, 192 * h:192 * h + 192], start=(kc == 0), stop=(kc == 3)), r=[t_cqT, t_Wuq], w=[tb[bk]])
                    for kc in range(2):
                        T(lambda e, bk=bk, kc=kc, tsl=tsl, h=h: e.matmul(pb[bk][:, 192:448], lhsT=ckvT[:, kc, tsl], rhs=Wukv[:, kc, 256 * h:256 * h + 256], start=(kc == 0), stop=(kc == 1)), r=[t_ckvT, t_Wukv], w=[tb[bk]])
                    if KPART < 2:
                        continue
                    A(lambda e, bk=bk, ti=ti: e.activation(out=oh[0], in_=pb[bk][:, 0:128], func=AF.Square, accum_out=statC[:, ti, 0:1]), r=[tb[bk]], w=[t_oh[0], t_statC])
                    A(lambda e, bk=bk, ti=ti: e.activation(out=oh[0][:, 0:64], in_=pb[bk][:, 128:192], func=AF.Square, accum_out=statC[:, ti, 5:6]), r=[tb[bk]], w=[t_oh[0], t_statC])
                    A(lambda e, bk=bk, ti=ti: e.activation(out=oh[0], in_=pb[bk][:, 192:320], func=AF.Square, accum_out=statC[:, ti, 1:2]), r=[tb[bk]], w=[t_oh[0], t_statC])
                    V(lambda e, ti=ti: e.tensor_tensor(out=statC[:, ti, 0:1], in0=statC[:, ti, 0:1], in1=statC[:, ti, 5:6], op=ALU.add), r=[t_statC], w=[t_statC])
                    V(lambda e, ti=ti: e.tensor_tensor(out=statC[:, ti, 1:2], in0=statC[:, ti, 1:2], in1=statB[:, ti, 6:7], op=ALU.add), r=[t_statC, t_statB], w=[t_statC])
                    A(lambda e, ti=ti: e.activation(out=statC[:, ti, 2:4], in_=statC[:, ti, 0:2], func=AF.Ln, scale=1.0 / 192, bias=EPS), r=[t_statC], w=[t_statC])
                    A(lambda e, ti=ti: e.activation(out=statC[:, ti, 2:4], in_=statC[:, ti, 2:4], func=AF.Exp, scale=-0.5), r=[t_statC], w=[t_statC])
                    if KPART < 3:
                        continue
                    V(lambda e, bk=bk, ti=ti, pr=pr: e.scalar_tensor_tensor(out=qb_[pr][:, 0:128], in0=pb[bk][:, 0:128], scalar=statC[:, ti, 2:3], in1=gsm[:, 768:896], op0=ALU.mult, op1=ALU.mult), r=[tb[bk], t_statC, t_gsm], w=[t_qb[pr]])
                    V(lambda e, bk=bk, ti=ti: e.scalar_tensor_tensor(out=qr.rearrange("p a b -> p (a b)"), in0=pb[bk][:, 128:192], scalar=statC[:, ti, 2:3], in1=gsm[:, 896:960], op0=ALU.mult, op1=ALU.mult), r=[tb[bk], t_statC, t_gsm], w=[t_qr])
                    V(lambda e, ti=ti: e.tensor_tensor(out=qA, in0=qr, in1=CS2[:, ti, :, :], op=ALU.mult), r=[t_qr, t_trig], w=[t_qr])
                    V(lambda e, ti=ti: e.tensor_tensor(out=qB, in0=qr[:, ::-1, :], in1=SN2[:, ti, :, :], op=ALU.mult), r=[t_qr, t_trig], w=[t_qr])
                    V(lambda e, pr=pr: e.tensor_tensor(out=qb_[pr][:, 128:192].rearrange("p (a b) -> p a b", a=2), in0=qA, in1=qB, op=ALU.add), r=[t_qr], w=[t_qb[pr]])
                    if KPART < 4:
                        continue
                    V(lambda e, bk=bk, ti=ti, pr=pr: e.scalar_tensor_tensor(out=kb_[pr][:, 0:128], in0=pb[bk][:, 192:320], scalar=statC[:, ti, 3:4], in1=gsm[:, 960:1088], op0=ALU.mult, op1=ALU.mult), r=[tb[bk], t_statC, t_gsm], w=[t_kb[pr]])
                    A(lambda e, ti=ti, pr=pr: e.activation(out=kb_[pr][:, 128:192], in_=Rk[:, ti, :], func=AF.Copy, scale=statC[:, ti, 3:4]), r=[t_Rk, t_statC], w=[t_kb[pr]])
                    A(lambda e, bk=bk, ti=ti, HB=HB: e.activation(out=HB["Vh"][:, ti, 0:128], in_=pb[bk][:, 320:448], func=AF.Copy), r=[tb[bk]], w=[tVh])
                    if KPART < 5:
                        continue
                    T(lambda e, bk=bk, pr=pr: e.transpose(out=pbb[bk][:, 0:128], in_=qb_[pr][:, 0:128], identity=ident_bf[:]), r=[t_qb[pr], t_const], w=[tb[bk]])
                    T(lambda e, bk=bk, pr=pr: e.transpose(out=pbb[bk][:, 128:256], in_=qb_[pr][:, 128:256], identity=ident_bf[:]), r=[t_qb[pr], t_const], w=[tb[bk]])
                    T(lambda e, bk=bk, pr=pr: e.transpose(out=pbb[bk][:, 256:384], in_=kb_[pr][:, 0:128], identity=ident_bf[:]), r=[t_kb[pr], t_const], w=[tb[bk]])
                    T(lambda e, bk=bk, pr=pr: e.transpose(out=pbb[bk][:, 384:512], in_=kb_[pr][:, 128:256], identity=ident_bf[:]), r=[t_kb[pr], t_const], w=[tb[bk]])
                    if KPART < 6:
                        continue
                    A(lambda e, bk=bk, HB=HB, tsl=tsl: e.activation(out=HB["QT"][:, tsl], in_=pbb[bk][:, 0:128], func=AF.Copy), r=[tb[bk]], w=[tQT])
                    A(lambda e, bk=bk, HB=HB, tsl=tsl: e.activation(out=HB["KT"][:, tsl], in_=pbb[bk][:, 256:384], func=AF.Copy), r=[tb[bk]], w=[tKT])
                    if KPART < 7:
                        continue
                    A(lambda e, bk=bk, HB=HB, tsl=tsl: e.activation(out=HB["QTr"][:, tsl], in_=pbb[bk][:, 128:256], func=AF.Copy), r=[tb[bk]], w=[tQTr])
                    A(lambda e, bk=bk, HB=HB, tsl=tsl: e.activation(out=HB["KTr"][:, tsl], in_=pbb[bk][:, 384:512], func=AF.Copy), r=[tb[bk]], w=[tKTr])
                for qsb in range(NQSB):
                    for kb in range(4 * qsb + 4):
                        q0 = max(512 * qsb, 128 * kb)
                        q1 = 512 * qsb + 512
                        nq = q1 - q0
                        sb_ = 4 + (pi_[0] % 2)
                        pt = pi_[0] % 3
                        pi_[0] += 1
                        ksl = slice(128 * kb, 128 * kb + 128)
                        T(lambda e, sb_=sb_, nq=nq, ksl=ksl, q0=q0, q1=q1, HB=HB: e.matmul(pb[sb_][:, 0:nq], lhsT=HB["KT"][:, ksl], rhs=HB["QT"][:, q0:q1], start=True, stop=False), r=[tKT, tQT], w=[tb[sb_]])
                        T(lambda e, sb_=sb_, nq=nq, ksl=ksl, q0=q0, q1=q1, HB=HB: e.matmul(pb[sb_][:, 0:nq], lhsT=HB["KTr"][0:64, ksl], rhs=HB["QTr"][0:64, q0:q1], start=False, stop=True), r=[tKTr, tQTr], w=[tb[sb_]])
                        A(lambda e, sb_=sb_, nq=nq, pt=pt: e.activation(out=Pt[pt][:, 0:nq], in_=pb[sb_][:, 0:nq], func=AF.Exp), r=[tb[sb_]], w=[t_Pt[pt]])
                        if 128 * kb >= 512 * qsb:
                            V(lambda e, pt=pt: e.tensor_tensor(out=Pt[pt][:, 0:128], in0=Pt[pt][:, 0:128], in1=mask_bf[:], op=ALU.mult), r=[t_Pt[pt], t_const], w=[t_Pt[pt]])
                        for j in range(nq // 128):
                            qi = (q0 + 128 * j) // 128
                            ob = qi % 4
                            T(lambda e, ob=ob, pt=pt, j=j, kb=kb, qi=qi, HB=HB: e.matmul(pb[ob][:, 0:130], lhsT=Pt[pt][:, 128 * j:128 * j + 128], rhs=HB["Vh"][:, kb, 0:130], start=(kb == 0), stop=(kb == qi)), r=[t_Pt[pt], tVh], w=[tb[ob]])
                    for j in range(4):
                        qi = 4 * qsb + j
                        ob = qi % 4
                        pr = qi % 2
                        tsl = slice(128 * qi, 128 * qi + 128)
                        V(lambda e, ob=ob, qi=qi: e.reciprocal(out=statC[:, qi, 4:5], in_=pb[ob][:, 128:129]), r=[tb[ob]], w=[t_statC])
                        A(lambda e, ob=ob, qi=qi, pr=pr: e.activation(out=oh[pr], in_=pb[ob][:, 0:128], func=AF.Copy, scale=statC[:, qi, 4:5]), r=[tb[ob], t_statC], w=[t_oh[pr]])
                        A(lambda e, ob=ob, qi=qi, h=h: e.activation(out=junkc[:, 0:128], in_=pb[ob][:, 0:128], func=AF.Square, scale=statC[:, qi, 4:5], accum_out=ssqA[:, qi, h:h + 1]), r=[tb[ob], t_statC], w=[t_junkc, t_ssqA])
                        tbk = 6 + qi % 2
                        T(lambda e, tbk=tbk, pr=pr: e.transpose(out=pbb[tbk][:, 512:640], in_=oh[pr], identity=ident_bf[:]), r=[t_oh[pr], t_const], w=[tb[tbk]])
                        A(lambda e, tbk=tbk, h=h, tsl=tsl: e.activation(out=oaT[:, h, tsl], in_=pbb[tbk][:, 512:640], func=AF.Copy, scale=colv[:, 3, h:h + 1]), r=[tb[tbk], t_col], w=[t_oaT])
            V(lambda e: e.tensor_reduce(out=statC[:, :, 6:7], in_=ssqA[:, :, :], axis=AX.X, op=ALU.add), r=[t_ssqA], w=[t_statC])
            A(lambda e: e.activation(out=statC[:, :, 7:8], in_=statC[:, :, 6:7], func=AF.Ln, scale=1.0 / 1024, bias=EPS), r=[t_statC], w=[t_statC])
            A(lambda e: e.activation(out=statC[:, :, 7:8], in_=statC[:, :, 7:8], func=AF.Exp, scale=-0.5), r=[t_statC], w=[t_statC])
            if stage == "attn":
                od = view(132 * KB, [2048], F32)
                t_od = P.tok("od")
                for ii, kc in enumerate((0, 5)):
                    V(lambda e, kc=kc: e.tensor_copy(out=od, in_=oaT[:, kc, :]), r=[t_oaT], w=[t_od])
                    dump(od, 2048 * ii, 2048, rd=[t_od])
                od2 = view(140 * KB, [2048], F32)
                V(lambda e: e.tensor_copy(out=od2, in_=cqT[:, 1, :]), r=[t_cqT], w=[t_od])
                dump(od2, 4096, 2048, rd=[t_od])
            barrier()

        if stage in ("all", "attn"):
            _phase4()

        t_out = P.tok("out")
        def _phase5():
            X = view(32 * KB, [4, 2048], F32)
            hT = view(64 * KB, [16, 512], BF16)
            actT = view(112 * KB, [11, 512], BF16)
            slots = [view((123 + 16 * i) * KB, [8192], BF16) for i in range(3)]
            t_slot = P.toks(3, "slot")
            gffn = view(171 * KB, [2048], F32)
            gple = view(179 * KB, [2048], F32)
            hb = view(187 * KB, [2048], BF16)
            junk = view(191 * KB, [2048], BF16)
            sg = [view(195 * KB, [512], F32), view(197 * KB, [512], F32)]
            pf = view(199 * KB, [256], F32)
            pT = view(160 * KB + 0, [2, 512], BF16)
            statD = sbt("statD", [128, 8], F32)
            t_X, t_hT, t_act, t_gf, t_gp, t_hb, t_junk, t_pf, t_pT, t_statD = P.toks(10, "pd")
            t_sg = P.toks(2, "sg")
            pbf = gsm[:, 0:128].bitcast(BF16)
            pT = gsm[:, 128:640].bitcast(BF16).rearrange("p (k t) -> p k t", k=2)
            load_gain_bc(gffn, "g_ffn_norm", t_gf)
            load_gain_bc(gple, "g_ple_norm", t_gp)
            slot_i = [0]

            def next_slot():
                i = slot_i[0] % 3
                slot_i[0] += 1
                return i

            def super_tile(sI):
                t0_ = 512 * sI
                pend = []

                def tokr(tt):
                    return slice(t0_ + 128 * tt, t0_ + 128 * tt + 128)

                for tt in range(4):
                    DG(lambda e, tt=tt: e.dma_start(out=X[:, tt, :], in_=dr["x"][t0_ + 128 * tt:t0_ + 128 * tt + 128, :]), w=[t_X])

                def norm_to_hT(gb, t_g):
                    for tt in range(4):
                        A(lambda e, tt=tt: e.activation(out=junk, in_=X[:, tt, :], func=AF.Square, accum_out=statD[:, 0:1]), r=[t_X], w=[t_junk, t_statD])
                        for f in rstd_act(statD[:, 1:2], statD[:, 0:1], 2048):
                            A(f, r=[t_statD], w=[t_statD])
                        V(lambda e, tt=tt: e.scalar_tensor_tensor(out=hb, in0=X[:, tt, :], scalar=statD[:, 1:2], in1=gb, op0=ALU.mult, op1=ALU.mult), r=[t_X, t_statD, t_g], w=[t_hb])
                        for kc in range(16):
                            bb = 6 + kc // 8
                            T(lambda e, kc=kc, bb=bb: e.transpose(out=pbb[bb][:, 128 * (kc % 8):128 * (kc % 8) + 128], in_=hb[:, 128 * kc:128 * kc + 128], identity=ident_bf[:]), r=[t_hb, t_const], w=[tb[bb]])
                        for q in range(2):
                            A(lambda e, q=q, tt=tt: e.activation(out=hT[:, 8 * q:8 * q + 8, 128 * tt:128 * tt + 128], in_=pbb[6 + q][:, :].rearrange("p (k t) -> p k t", k=8), func=AF.Copy), r=[tb[6 + q]], w=[t_hT])

                def mk_wo(cc):
                    def load(si):
                        DS(lambda e: e.dma_start(out=slots[si].rearrange("p (c n) -> p c n", c=16), in_=wb["w_o"][:, 512 * cc:512 * cc + 512].rearrange("(c p) n -> p c n", p=128)), r=[t_wb["w_o"]], w=[t_slot[si]])

                    def comp(si):
                        W = slots[si].rearrange("p (c n) -> p c n", c=16)
                        for tt in range(4):
                            ba, bs = tt % 2, 2 + tt % 2
                            for kc in range(8):
                                T(lambda e, ba=ba, kc=kc, tt=tt: e.matmul(pb[ba][:, :], lhsT=oaT[:, kc, tokr(tt)], rhs=W[:, kc, :], start=(kc == 0), stop=(kc == 7)), r=[t_oaT, t_slot[si]], w=[tb[ba]])
                            for kc in range(8):
                                T(lambda e, bs=bs, kc=kc, tt=tt: e.matmul(pb[bs][:, :], lhsT=osT[:, kc, tokr(tt)], rhs=W[:, 8 + kc, :], start=(kc == 0), stop=(kc == 7)), r=[t_osT, t_slot[si]], w=[tb[bs]])
                            xs = X[:, tt, 512 * cc:512 * cc + 512]
                            ti = 4 * sI + tt
                            V(lambda e, ba=ba, xs=xs, ti=ti: e.scalar_tensor_tensor(out=xs, in0=pb[ba][:, :], scalar=statC[:, ti, 7:8], in1=xs, op0=ALU.mult, op1=ALU.add), r=[tb[ba], t_statC, t_X], w=[t_X])
                            V(lambda e, bs=bs, xs=xs: e.tensor_tensor(out=xs, in0=xs, in1=pb[bs][:, :], op=ALU.add), r=[tb[bs], t_X], w=[t_X])
                    return load, comp

                for cc in range(4):
                    pend.append(mk_wo(cc))

                pend.append((None, lambda si: norm_to_hT(gffn, t_gf)))

                def mk_gu(grp, b2):
                    ffcs = [j for j in (2 * b2, 2 * b2 + 1) if j < 11]
                    c0 = (grp * 11 + ffcs[0]) * 128
                    ncol = 128 * len(ffcs)

                    def load(si):
                        Wv = slots[si].rearrange("p (w c n) -> p w c n", w=2, c=16)
                        DS(lambda e: e.dma_start(out=Wv[:, 0, :, 0:ncol], in_=wb["w_gate"][:, c0:c0 + ncol].rearrange("(c p) n -> p c n", p=128)), r=[t_wb["w_gate"]], w=[t_slot[si]])
                        DS(lambda e: e.dma_start(out=Wv[:, 1, :, 0:ncol], in_=wb["w_up"][:, c0:c0 + ncol].rearrange("(c p) n -> p c n", p=128)), r=[t_wb["w_up"]], w=[t_slot[si]])

                    def comp(si):
                        Wv = slots[si].rearrange("p (w c n) -> p w c n", w=2, c=16)
                        for jj, j in enumerate(ffcs):
                            bg, bu = j % 2, 2 + j % 2
                            for kc in range(16):
                                T(lambda e, bg=bg, kc=kc, jj=jj: e.matmul(pb[bg][:, :], lhsT=Wv[:, 0, kc, 128 * jj:128 * jj + 128], rhs=hT[:, kc, :], start=(kc == 0), stop=(kc == 15)), r=[t_slot[si], t_hT], w=[tb[bg]])
                            for kc in range(16):
                                T(lambda e, bu=bu, kc=kc, jj=jj: e.matmul(pb[bu][:, :], lhsT=Wv[:, 1, kc, 128 * jj:128 * jj + 128], rhs=hT[:, kc, :], start=(kc == 0), stop=(kc == 15)), r=[t_slot[si], t_hT], w=[tb[bu]])
                            sgi = j % 2
                            A(lambda e, bg=bg, sgi=sgi: e.activation(out=sg[sgi], in_=pb[bg][:, :], func=AF.Silu), r=[tb[bg]], w=[t_sg[sgi]])
                            V(lambda e, bu=bu, sgi=sgi, j=j: e.tensor_tensor(out=actT[:, j, :], in0=sg[sgi], in1=pb[bu][:, :], op=ALU.mult), r=[t_sg[sgi], tb[bu]], w=[t_act])
                    return load, comp

                def mk_down(grp, cc):
                    def load(si):
                        Wv = slots[si][:, 0:11 * 512].rearrange("p (j n) -> p j n", j=11)
                        DS(lambda e: e.dma_start(out=Wv, in_=wb["w_down"][1408 * grp:1408 * grp + 1408, 512 * cc:512 * cc + 512].rearrange("(j p) n -> p j n", p=128)), r=[t_wb["w_down"]], w=[t_slot[si]])

                    def comp(si):
                        Wv = slots[si][:, 0:11 * 512].rearrange("p (j n) -> p j n", j=11)
                        for tt in range(4):
                            bd = 4 + tt % 2
                            for j in range(11):
                                T(lambda e, bd=bd, j=j, tt=tt: e.matmul(pb[bd][:, :], lhsT=actT[:, j, 128 * tt:128 * tt + 128], rhs=Wv[:, j, :], start=(j == 0), stop=(j == 10)), r=[t_act, t_slot[si]], w=[tb[bd]])
                            xs = X[:, tt, 512 * cc:512 * cc + 512]
                            V(lambda e, bd=bd, xs=xs: e.tensor_tensor(out=xs, in0=xs, in1=pb[bd][:, :], op=ALU.add), r=[tb[bd], t_X], w=[t_X])
                    return load, comp

                for grp in range(4):
                    for b2 in range(6):
                        pend.append(mk_gu(grp, b2))
                    for cc in range(4):
                        pend.append(mk_down(grp, cc))

                WPv = actT[:, 0:8, :].rearrange("p a b -> p (a b)").rearrange("p (c n) -> p c n", c=2)

                def ple_prep(si_unused):
                    DS(lambda e: e.dma_start(out=WPv, in_=wb["w_ple_proj"].rearrange("(c p) n -> p c n", p=128)), r=[t_wb["w_ple_proj"]], w=[t_act])
                    norm_to_hT(gple, t_gp)
                    for tt in range(4):
                        DG(lambda e, tt=tt: e.dma_start(out=pf, in_=dr["p"][t0_ + 128 * tt:t0_ + 128 * tt + 128, :]), w=[t_pf])
                        V(lambda e: e.tensor_copy(out=pbf, in_=pf), r=[t_pf], w=[t_hb])
                        for kc in range(2):
                            T(lambda e, kc=kc: e.transpose(out=pbb[6][:, 128 * kc:128 * kc + 128], in_=pbf[:, 128 * kc:128 * kc + 128], identity=ident_bf[:]), r=[t_hb, t_const], w=[tb[6]])
                        A(lambda e, tt=tt: e.activation(out=pT[:, :, 128 * tt:128 * tt + 128], in_=pbb[6][:, 0:256].rearrange("p (k t) -> p k t", k=2), func=AF.Copy), r=[tb[6]], w=[t_pT])

                pend.append((None, ple_prep))

                def mk_pg(cc):
                    def load(si):
                        DS(lambda e: e.dma_start(out=slots[si].rearrange("p (c n) -> p c n", c=16), in_=wb["w_ple_gate"][:, 512 * cc:512 * cc + 512].rearrange("(c p) n -> p c n", p=128)), r=[t_wb["w_ple_gate"]], w=[t_slot[si]])

                    def comp(si):
                        W = slots[si].rearrange("p (c n) -> p c n", c=16)
                        WP = WPv
                        for tt in range(4):
                            bg, bp = tt % 2, 2 + tt % 2
                            for kc in range(16):
                                T(lambda e, bg=bg, kc=kc, tt=tt: e.matmul(pb[bg][:, :], lhsT=hT[:, kc, 128 * tt:128 * tt + 128], rhs=W[:, kc, :], start=(kc == 0), stop=(kc == 15)), r=[t_hT, t_slot[si]], w=[tb[bg]])
                            for kc in range(2):
                                T(lambda e, bp=bp, kc=kc, tt=tt: e.matmul(pb[bp][:, :], lhsT=pT[:, kc, 128 * tt:128 * tt + 128], rhs=WP[:, kc, 512 * cc:512 * cc + 512], start=(kc == 0), stop=(kc == 1)), r=[t_pT, t_act], w=[tb[bp]])
                            sgi = tt % 2
                            A(lambda e, bg=bg, sgi=sgi: e.activation(out=sg[sgi], in_=pb[bg][:, :], func=AF.Sigmoid), r=[tb[bg]], w=[t_sg[sgi]])
                            V(lambda e, bp=bp, sgi=sgi: e.tensor_tensor(out=sg[sgi], in0=sg[sgi], in1=pb[bp][:, :], op=ALU.mult), r=[t_sg[sgi], tb[bp]], w=[t_sg[sgi]])
                            xs = X[:, tt, 512 * cc:512 * cc + 512]
                            V(lambda e, sgi=sgi, xs=xs: e.tensor_tensor(out=xs, in0=xs, in1=sg[sgi], op=ALU.add), r=[t_sg[sgi], t_X], w=[t_X])
                    return load, comp

                for cc in range(4):
                    pend.append(mk_pg(cc))

                loads = [(i, ld) for i, (ld, _) in enumerate(pend) if ld is not None]
                assigned = {}
                li = [0]

                def issue_loads_upto(n_ahead_idx):
                    while li[0] < len(loads) and loads[li[0]][0] <= n_ahead_idx:
                        idx, ld = loads[li[0]]
                        si = next_slot()
                        assigned[idx] = si
                        ld(si)
                        li[0] += 1

                for i, (ld, cp) in enumerate(pend):
                    cnt = 0
                    j = i
                    tgt = i
                    while j < len(pend) and cnt < 2:
                        if pend[j][0] is not None:
                            cnt += 1
                            tgt = j
                        j += 1
                    issue_loads_upto(tgt)
                    cp(assigned.get(i))
                for tt in range(4):
                    DG(lambda e, tt=tt: e.dma_start(out=out[t0_ + 128 * tt:t0_ + 128 * tt + 128, :], in_=X[:, tt, :]), r=[t_X], w=[t_out])

            for sI in range(4):
                super_tile(sI)

        if stage in ("all", "pd"):
            _phase5()

        finals = []
        if stage in ('all', 'pd'):
            finals.append(t_out)
        if dbg is not None:
            finals.append(t_dbg)
        P.emit(final_waits=finals)
        print("ops:", P.nops, {e: len(v) for e, v in P.ops.items()}, "sems:", P.nsem)
    return nc


def make_inputs(inputs, b):
    m = {
        "x": np.ascontiguousarray(inputs["x"][b]),
        "p": np.ascontiguousarray(inputs["p"][0, b]),
        "pos": np.ascontiguousarray(inputs["positions"][b].reshape(1, L).astype(np.int32)),
    }
    for n, r, c in WSPECS:
        m[n] = np.ascontiguousarray(inputs[n][0])
    for n, shp in SMALL:
        m[n] = np.ascontiguousarray(inputs[n][0].reshape(shp))
    return m


def kernel(**inputs):
    nc = build("all")
    in_maps = [make_inputs(inputs, b) for b in range(8)]
    res = run_bass_kernel_spmd(nc, in_maps, core_ids=list(range(8)))
    return np.stack([r["out"] for r in res.results], axis=0).astype(np.float32)
```

```python
import bisect
import math
from contextlib import ExitStack
import numpy as np
import concourse.bass as bass
import concourse.mybir as mybir
from concourse.bass_utils import run_bass_kernel_spmd

F32 = mybir.dt.float32
BF16 = mybir.dt.bfloat16
I32 = mybir.dt.int32
AF = mybir.ActivationFunctionType
ALU = mybir.AluOpType
AX = mybir.AxisListType

COMPUTE = ("tensor", "vector", "scalar", "gpsimd")
ALLENG = ("tensor", "vector", "scalar", "gpsimd", "sync")

L = 2048
D = 2048
NT = 16
DFF = 5632
EPS = 1e-6


class Tok:
    __slots__ = ("name", "last_writer", "readers", "sem", "dma_count")

    def __init__(self, name="t"):
        self.name = name
        self.last_writer = None
        self.readers = []
        self.sem = None
        self.dma_count = 0


class Op:
    __slots__ = ("eng", "fn", "edeps", "ddeps", "is_dma", "ordinal", "dma_tok")

    def __init__(self, eng, fn, is_dma):
        self.eng = eng
        self.fn = fn
        self.edeps = {}
        self.ddeps = {}
        self.is_dma = is_dma
        self.ordinal = None
        self.dma_tok = None


class Prog:
    def __init__(self, nc, stack):
        self.nc = nc
        self.stack = stack
        self.ops = {e: [] for e in ALLENG}
        self.eng_sem = {}
        self.eng_count = {e: 0 for e in COMPUTE}
        for e in COMPUTE:
            self.eng_sem[e] = stack.enter_context(nc.semaphore("s_" + e))
        self.nsem = 4
        self.ALL = Tok("ALL")
        self.nops = 0

    def tok(self, name="t"):
        return Tok(name)

    def toks(self, n, name="t"):
        return [Tok("%s%d" % (name, i)) for i in range(n)]

    def _dep_on(self, op, prev):
        if prev is None or prev is op:
            return
        if prev.is_dma:
            tok = prev.dma_tok
            cur = op.ddeps.get(id(tok))
            if cur is None or cur[1] < tok.dma_count:
                op.ddeps[id(tok)] = (tok, tok.dma_count)
            return
        if prev.eng == "tensor" and op.eng == "tensor" and not op.is_dma:
            return
        cur = op.edeps.get(prev.eng)
        if cur is None or cur < prev.ordinal:
            op.edeps[prev.eng] = prev.ordinal

    def add(self, eng, fn, reads=(), writes=(), dma=False, _barrier=False):
        op = Op(eng, fn, dma)
        self.nops += 1
        if not dma:
            self.eng_count[eng] += 1
            op.ordinal = self.eng_count[eng]
        reads = list(reads)
        writes = list(writes)
        if _barrier:
            writes.append(self.ALL)
        else:
            reads.append(self.ALL)
        for t in reads:
            self._dep_on(op, t.last_writer)
        for t in writes:
            self._dep_on(op, t.last_writer)
            for r in t.readers:
                self._dep_on(op, r)
        if dma:
            tok = writes[0]
            if tok.sem is None:
                tok.sem = self.stack.enter_context(self.nc.semaphore("d%d" % self.nsem))
                self.nsem += 1
            tok.dma_count += 16
            op.dma_tok = tok
        for t in writes:
            t.last_writer = op
            t.readers = []
        for t in reads:
            t.readers.append(op)
        self.ops[eng].append(op)
        return op

    def emit(self, final_waits=()):
        nc = self.nc
        needed = {e: set() for e in COMPUTE}
        for e, lst in self.ops.items():
            for op in lst:
                for pe, o in op.edeps.items():
                    needed[pe].add(o)
        ranks = {e: sorted(needed[e]) for e in COMPUTE}

        def rank(e, o):
            return bisect.bisect_right(ranks[e], o)

        with nc.Block() as block:
            def make(e):
                def body(eng):
                    waited_e = {}
                    waited_d = {}
                    for op in self.ops[e]:
                        for pe, o in op.edeps.items():
                            val = rank(pe, o)
                            if waited_e.get(pe, 0) >= val:
                                continue
                            eng.wait_ge(self.eng_sem[pe], val)
                            waited_e[pe] = val
                        for k, (tok, cnt) in op.ddeps.items():
                            if waited_d.get(k, 0) >= cnt:
                                continue
                            eng.wait_ge(tok.sem, cnt)
                            waited_d[k] = cnt
                        ins = op.fn(eng)
                        if op.is_dma:
                            ins.then_inc(op.dma_tok.sem, 16)
                        elif op.ordinal in needed[e]:
                            ins.then_inc(self.eng_sem[e], 1)
                    if e == "sync":
                        for tok in final_waits:
                            eng.wait_ge(tok.sem, tok.dma_count)
                return body
            block.tensor(make("tensor"))
            block.vector(make("vector"))
            block.scalar(make("scalar"))
            block.gpsimd(make("gpsimd"))
            block.sync(make("sync"))


WSPECS = [
    ("w_in", 2048, 1856), ("w_glu", 1024, 1024), ("w_uq", 512, 1536), ("w_ukv", 256, 2048),
    ("w_o", 2048, 2048), ("w_gate", 2048, 5632), ("w_up", 2048, 5632), ("w_down", 5632, 2048),
    ("w_ple_gate", 2048, 2048), ("w_ple_proj", 256, 2048),
]
SMALL = [
    ("g_mix_norm", [1, 2048]), ("g_q_lora", [1, 512]), ("g_kv_lora", [1, 256]),
    ("g_q_head", [1, 192]), ("g_k_head", [1, 192]), ("lam_re", [64, 64]), ("lam_im", [64, 64]),
    ("log_dt", [1, 64]), ("b_re", [64, 64, 16]), ("b_im", [64, 64, 16]), ("c_re", [64, 16, 64]),
    ("c_im", [64, 16, 64]), ("d_skip", [1, 1024]), ("b_glu", [1, 1024]), ("g_out_attn", [1, 1024]),
    ("g_out_ssm", [1, 1024]), ("g_ffn_norm", [1, 2048]), ("g_ple_norm", [1, 2048]),
]

ARENA_BYTES = 200 * 1024


def build(stage="all"):
    nc = bass.Bass("TRN2", target_bir_lowering=False)
    dr = {}
    dr["x"] = nc.dram_tensor("x", [L, D], F32, kind="ExternalInput").ap()
    dr["p"] = nc.dram_tensor("p", [L, 256], F32, kind="ExternalInput").ap()
    dr["pos"] = nc.dram_tensor("pos", [1, L], I32, kind="ExternalInput").ap()
    for n, r, c in WSPECS:
        dr[n] = nc.dram_tensor(n, [r, c], F32, kind="ExternalInput").ap()
    for n, shp in SMALL:
        dr[n] = nc.dram_tensor(n, shp, F32, kind="ExternalInput").ap()
    out = nc.dram_tensor("out", [L, D], F32, kind="ExternalOutput").ap()
    wb = {}
    for n, r, c in WSPECS:
        wb[n] = nc.dram_tensor(n + "_bf", [r, c], BF16).ap()
    kscr = nc.dram_tensor("kscr", [64, 16, 15, 16], BF16).ap()
    dbg = None
    if stage != "all":
        dbg = nc.dram_tensor("dbg", [128, 16384], F32, kind="ExternalOutput").ap()

    with ExitStack() as st:
        P = Prog(nc, st)
        sbt = lambda n, s, d: st.enter_context(nc.sbuf_tensor(n, s, d))
        arena = sbt("arena", [128, ARENA_BYTES // 2], BF16)

        def view(off, shape, dt, parts=(0, 128)):
            esz = 4 if dt in (F32, I32) else 2
            n = int(np.prod(shape))
            nb = n * esz
            assert off % 4 == 0 and off + nb <= ARENA_BYTES, (off, nb)
            a = arena[parts[0]:parts[1], off // 2:(off + nb) // 2]
            if dt != BF16:
                a = a.bitcast(dt)
            if len(shape) > 1:
                names = " ".join("d%d" % i for i in range(len(shape)))
                kw = {"d%d" % i: int(s) for i, s in enumerate(shape)}
                a = a.rearrange("p (%s) -> p %s" % (names, names), **kw)
            return a

        KB = 1024
        pb = [st.enter_context(nc.psum_tensor("pb%d" % i, [128, 512], F32)) for i in range(8)]
        tb = P.toks(8, "pb")
        pbb = [b[:].bitcast(BF16) for b in pb]

        ident_bf = sbt("ident_bf", [128, 128], BF16)
        ident_f = sbt("ident_f", [128, 128], F32)
        mask_bf = sbt("mask_bf", [128, 128], BF16)
        ones_bf = sbt("ones_bf", [128, 128], BF16)
        scr1 = sbt("scr1", [128, 4], F32)
        t_const = P.tok("const")

        def barrier():
            P.add("vector", lambda e: e.memset(scr1[:, 0:1], 0.0), _barrier=True)

        V = lambda fn, r=(), w=(): P.add("vector", fn, reads=r, writes=w)
        A = lambda fn, r=(), w=(): P.add("scalar", fn, reads=r, writes=w)
        G = lambda fn, r=(), w=(): P.add("gpsimd", fn, reads=r, writes=w)
        T = lambda fn, r=(), w=(): P.add("tensor", fn, reads=r, writes=w)
        DS = lambda fn, r=(), w=(): P.add("sync", fn, reads=r, writes=w, dma=True)
        DG = lambda fn, r=(), w=(): P.add("gpsimd", fn, reads=r, writes=w, dma=True)

        G(lambda e: e.memset(ident_bf[:], 0.0), w=[t_const])
        G(lambda e: e.affine_select(out=ident_bf[:], in_=ident_bf[:], pattern=[[-1, 128]], compare_op=ALU.not_equal, fill=1.0, base=0, channel_multiplier=1), r=[t_const], w=[t_const])
        G(lambda e: e.memset(ident_f[:], 0.0), w=[t_const])
        G(lambda e: e.affine_select(out=ident_f[:], in_=ident_f[:], pattern=[[-1, 128]], compare_op=ALU.not_equal, fill=1.0, base=0, channel_multiplier=1), r=[t_const], w=[t_const])
        G(lambda e: e.memset(mask_bf[:], 1.0), w=[t_const])
        G(lambda e: e.affine_select(out=mask_bf[:], in_=mask_bf[:], pattern=[[1, 128]], compare_op=ALU.is_ge, fill=0.0, base=0, channel_multiplier=-1), r=[t_const], w=[t_const])
        G(lambda e: e.memset(ones_bf[:], 1.0), w=[t_const])

        colv = sbt("colv", [128, 4, 8], F32)
        t_col = P.tok("col")
        for idx, nm in enumerate(("b_glu", "g_out_ssm", "g_out_ssm", "g_out_attn")):
            DS(lambda e, idx=idx, nm=nm: e.dma_start(out=colv[:, idx, :], in_=dr[nm][0, :].rearrange("(c p) -> p c", p=128), allow_slow_non_contiguous=True), w=[t_col])
        V(lambda e: e.reciprocal(out=colv[:, 2, :], in_=colv[:, 2, :]), r=[t_col], w=[t_col])

        t_wb = {n: P.tok("wb_" + n) for n, _, _ in WSPECS}

        def precast(n):
            r, c = [(rr, cc) for nn, rr, cc in WSPECS if nn == n][0]
            step = 256
            for r0 in range(0, r, step):
                r1 = min(r, r0 + step)
                DG(lambda e, r0=r0, r1=r1: e.dma_start(out=wb[n][r0:r1, :], in_=dr[n][r0:r1, :]), w=[t_wb[n]])

        need_w = {"ssm": ["w_in", "w_glu"], "pa": ["w_in"], "attn": ["w_in", "w_uq", "w_ukv"], "pb": ["w_in"]}.get(stage, [n for n, _, _ in WSPECS])
        for n in need_w:
            precast(n)

        def rstd_act(dst, src, n):
            return [lambda e: e.activation(out=dst, in_=src, func=AF.Ln, scale=1.0 / n, bias=EPS),
                    lambda e: e.activation(out=dst, in_=dst, func=AF.Exp, scale=-0.5)]

        o_toep, o_mt, o_ca, o_a8, o_uT = 0, 16 * KB, 32 * KB, 50 * KB, 52 * KB
        Toep = view(o_toep, [64, 128], BF16)
        MT = view(o_mt, [64, 2, 64], BF16)
        CAall = view(o_ca, [32, 2, 9, 16], BF16)
        A8dup = view(o_a8, [32, 2], F32)
        A8sw = view(o_a8 + 256, [32, 2], F32)
        uT = view(o_uT, [64, 256], BF16)
        t_toep, t_mt, t_ca, t_a8, t_uT = P.toks(5, "tab")

        def _phase0():
            base = 84 * KB
            cur = [base]

            def al(shape, dt, parts=(0, 128)):
                esz = 4 if dt in (F32, I32) else 2
                nb = int(np.prod(shape)) * esz
                nb = (nb + 31) // 32 * 32
                v = view(cur[0], shape, dt, parts)
                cur[0] += nb
                return v

            s32 = lambda: al([32], F32)
            LR, LI, LDT, DT, XX, TH, MAG, CS, SN, AR, AI, DEN, NR, CFR, CFI, T1, T2, T3 = [s32() for _ in range(18)]
            TI = al([32], I32)
            BRE = al([32, 16], F32); BIM = al([32, 16], F32)
            BBR = al([32, 16], F32); BBI = al([32, 16], F32)
            CST = [al([4, 128], F32), al([4, 128], F32)]
            CT = [al([32, 16], F32), al([32, 16], F32)]
            APr = al([9, 32], F32); APi = al([9, 32], F32)
            AB = al([32, 2, 128], BF16)
            BBbf = al([32, 2, 16], BF16)
            Kflat = al([64, 128], BF16, (0, 16))
            Dbc = al([64, 16], F32, (0, 16))
            Dd = al([64, 16], F32, (0, 16))
            W1 = al([32, 16], F32); W2 = al([32, 16], F32); W3 = al([32, 16], F32)
            Z0 = al([64, 112], BF16, (0, 16))
            t0 = P.tok("p0")
            tS = [t0]

            for gh in range(2):
                ps_ = slice(64 * gh, 64 * gh + 64)
                gs = slice(32 * gh, 32 * gh + 32)
                DS(lambda e, ps_=ps_, gs=gs: e.dma_start(out=LR[ps_, :], in_=dr["lam_re"][gs, :].rearrange("g p -> p g"), allow_slow_non_contiguous=True), w=[t0])
                DS(lambda e, ps_=ps_, gs=gs: e.dma_start(out=LI[ps_, :], in_=dr["lam_im"][gs, :].rearrange("g p -> p g"), allow_slow_non_contiguous=True), w=[t0])
                DS(lambda e, ps_=ps_, gs=gs: e.dma_start(out=LDT[ps_, :], in_=dr["log_dt"][0, gs].partition_broadcast(64)), w=[t0])
                DS(lambda e, ps_=ps_, gs=gs: e.dma_start(out=BRE[ps_, :, :], in_=dr["b_re"][gs, :, :].rearrange("g p h -> p g h")), w=[t0])
                DS(lambda e, ps_=ps_, gs=gs: e.dma_start(out=BIM[ps_, :, :], in_=dr["b_im"][gs, :, :].rearrange("g p h -> p g h")), w=[t0])
                for gt in range(4):
                    g0 = 32 * gh + 8 * gt
                    for ri, nm in enumerate(("c_re", "c_im")):
                        DS(lambda e, g0=g0, gt=gt, gh=gh, ri=ri, nm=nm: e.dma_start(out=CST[ri][:, gt, 64 * gh:64 * gh + 64], in_=dr[nm][g0:g0 + 8, :, :].rearrange("g h p -> (g h) p")), w=[t0])
            DS(lambda e: e.dma_start(out=Dbc[:, :, :].rearrange("p g h -> p (g h)"), in_=dr["d_skip"][0, :].partition_broadcast(16)), w=[t0])
            V(lambda e: e.memset(Z0[:], 0.0), w=[t0])

            def vv(fn):
                V(fn, r=[t0, t_const], w=[t0])

            def aa(fn):
                A(fn, r=[t0], w=[t0])

            vv(lambda e: e.tensor_scalar(out=LR[:], in0=LR[:], scalar1=-1e-4, scalar2=None, op0=ALU.min))
            aa(lambda e: e.activation(out=DT[:], in_=LDT[:], func=AF.Exp))
            vv(lambda e: e.tensor_tensor(out=XX[:], in0=LR[:], in1=DT[:], op=ALU.mult))
            vv(lambda e: e.tensor_tensor(out=TH[:], in0=LI[:], in1=DT[:], op=ALU.mult))
            vv(lambda e: e.tensor_scalar(out=MAG[:], in0=XX[:], scalar1=1.0 / 8, scalar2=1.0, op0=ALU.mult, op1=ALU.add))
            for k in range(7, 0, -1):
                vv(lambda e, k=k: e.scalar_tensor_tensor(out=MAG[:], in0=XX[:], scalar=1.0 / k, in1=MAG[:], op0=ALU.mult, op1=ALU.mult))
                vv(lambda e: e.tensor_scalar(out=MAG[:], in0=MAG[:], scalar1=1.0, scalar2=None, op0=ALU.add))

            def trig(dst, shift):
                vv(lambda e: e.tensor_scalar(out=T1[:], in0=TH[:], scalar1=float(shift), scalar2=1.0 / (2 * math.pi), op0=ALU.add, op1=ALU.mult))
                vv(lambda e: e.tensor_copy(out=TI[:], in_=T1[:]))
                vv(lambda e: e.tensor_copy(out=T2[:], in_=TI[:]))
                vv(lambda e: e.tensor_scalar(out=T1[:], in0=TH[:], scalar1=float(shift), scalar2=None, op0=ALU.add))
                vv(lambda e: e.scalar_tensor_tensor(out=T1[:], in0=T2[:], scalar=-2 * math.pi, in1=T1[:], op0=ALU.mult, op1=ALU.add))
                vv(lambda e: e.tensor_scalar(out=T1[:], in0=T1[:], scalar1=-math.pi, scalar2=math.pi, op0=ALU.max, op1=ALU.min))
                aa(lambda e: e.activation(out=dst, in_=T1[:], func=AF.Sin))

            trig(SN[:], 0.0)
            trig(CS[:], math.pi / 2)
            vv(lambda e: e.tensor_tensor(out=AR[:], in0=MAG[:], in1=CS[:], op=ALU.mult))
            vv(lambda e: e.tensor_tensor(out=AI[:], in0=MAG[:], in1=SN[:], op=ALU.mult))
            vv(lambda e: e.tensor_tensor(out=DEN[:], in0=LR[:], in1=LR[:], op=ALU.mult))
            vv(lambda e: e.tensor_tensor(out=T1[:], in0=LI[:], in1=LI[:], op=ALU.mult))
            vv(lambda e: e.tensor_tensor(out=DEN[:], in0=DEN[:], in1=T1[:], op=ALU.add))
            vv(lambda e: e.reciprocal(out=DEN[:], in_=DEN[:]))
            vv(lambda e: e.tensor_scalar(out=NR[:], in0=AR[:], scalar1=-1.0, scalar2=None, op0=ALU.add))
            vv(lambda e: e.tensor_tensor(out=T1[:], in0=NR[:], in1=LR[:], op=ALU.mult))
            vv(lambda e: e.tensor_tensor(out=T2[:], in0=AI[:], in1=LI[:], op=ALU.mult))
            vv(lambda e: e.tensor_tensor(out=T1[:], in0=T1[:], in1=T2[:], op=ALU.add))
            vv(lambda e: e.tensor_tensor(out=CFR[:], in0=T1[:], in1=DEN[:], op=ALU.mult))
            vv(lambda e: e.tensor_tensor(out=T1[:], in0=AI[:], in1=LR[:], op=ALU.mult))
            vv(lambda e: e.tensor_tensor(out=T2[:], in0=NR[:], in1=LI[:], op=ALU.mult))
            vv(lambda e: e.tensor_tensor(out=T1[:], in0=T1[:], in1=T2[:], op=ALU.subtract))
            vv(lambda e: e.tensor_tensor(out=CFI[:], in0=T1[:], in1=DEN[:], op=ALU.mult))
            vv(lambda e: e.memset(APr[:, 0, :], 1.0))
            vv(lambda e: e.memset(APi[:, 0, :], 0.0))
            vv(lambda e: e.tensor_copy(out=APr[:, 1, :], in_=AR[:]))
            vv(lambda e: e.tensor_copy(out=APi[:, 1, :], in_=AI[:]))
            for k in range(2, 9):
                vv(lambda e, k=k: e.tensor_tensor(out=T1[:], in0=APr[:, k - 1, :], in1=AR[:], op=ALU.mult))
                vv(lambda e, k=k: e.tensor_tensor(out=T2[:], in0=APi[:, k - 1, :], in1=AI[:], op=ALU.mult))
                vv(lambda e, k=k: e.tensor_tensor(out=APr[:, k, :], in0=T1[:], in1=T2[:], op=ALU.subtract))
                vv(lambda e, k=k: e.tensor_tensor(out=T1[:], in0=APr[:, k - 1, :], in1=AI[:], op=ALU.mult))
                vv(lambda e, k=k: e.tensor_tensor(out=T2[:], in0=APi[:, k - 1, :], in1=AR[:], op=ALU.mult))
                vv(lambda e, k=k: e.tensor_tensor(out=APi[:, k, :], in0=T1[:], in1=T2[:], op=ALU.add))
            V(lambda e: e.tensor_copy(out=A8dup[:, :, 0], in_=APr[:, 8, :]), r=[t0], w=[t_a8])
            V(lambda e: e.tensor_copy(out=A8dup[:, :, 1], in_=APr[:, 8, :]), r=[t0], w=[t_a8])
            V(lambda e: e.tensor_scalar(out=A8sw[:, :, 0], in0=APi[:, 8, :], scalar1=-1.0, scalar2=None, op0=ALU.mult), r=[t0], w=[t_a8])
            V(lambda e: e.tensor_copy(out=A8sw[:, :, 1], in_=APi[:, 8, :]), r=[t0], w=[t_a8])

            def bc16(a):
                return a.unsqueeze(2).to_broadcast([128, 32, 16])

            vv(lambda e: e.tensor_tensor(out=W1[:], in0=BRE[:], in1=bc16(CFR[:]), op=ALU.mult))
            vv(lambda e: e.tensor_tensor(out=W2[:], in0=BIM[:], in1=bc16(CFI[:]), op=ALU.mult))
            vv(lambda e: e.tensor_tensor(out=BBR[:], in0=W1[:], in1=W2[:], op=ALU.subtract))
            vv(lambda e: e.tensor_tensor(out=W1[:], in0=BIM[:], in1=bc16(CFR[:]), op=ALU.mult))
            vv(lambda e: e.tensor_tensor(out=W2[:], in0=BRE[:], in1=bc16(CFI[:]), op=ALU.mult))
            vv(lambda e: e.tensor_tensor(out=BBI[:], in0=W1[:], in1=W2[:], op=ALU.add))
            vv(lambda e: e.tensor_copy(out=BBbf[:, :, 0, :], in_=BBR[:]))
            vv(lambda e: e.tensor_copy(out=BBbf[:, :, 1, :], in_=BBI[:]))
            for ri in range(2):
                for gt in range(4):
                    T(lambda e, ri=ri, gt=gt: e.transpose(out=pb[ri][:, 128 * gt:128 * gt + 128], in_=CST[ri][:, gt, :], identity=ident_f[:]), r=[t0, t_const], w=[tb[ri]])
                V(lambda e, ri=ri: e.tensor_copy(out=CT[ri][:, :, :].rearrange("p g h -> p (g h)"), in_=pb[ri][:, :]), r=[tb[ri], t0], w=[t0])
            for d in range(9):
                vv(lambda e, d=d: e.tensor_tensor(out=W1[:], in0=CT[0][:], in1=bc16(APr[:, d, :]), op=ALU.mult))
                vv(lambda e, d=d: e.tensor_tensor(out=W2[:], in0=CT[1][:], in1=bc16(APi[:, d, :]), op=ALU.mult))
                V(lambda e, d=d: e.tensor_tensor(out=CAall[:, :, 0, d, :], in0=W1[:], in1=W2[:], op=ALU.subtract), r=[t0], w=[t_ca])
                vv(lambda e, d=d: e.tensor_tensor(out=W1[:], in0=CT[0][:], in1=bc16(APi[:, d, :]), op=ALU.mult))
                vv(lambda e, d=d: e.tensor_tensor(out=W2[:], in0=CT[1][:], in1=bc16(APr[:, d, :]), op=ALU.mult))
                vv(lambda e, d=d: e.tensor_tensor(out=W1[:], in0=W1[:], in1=W2[:], op=ALU.add))
                V(lambda e, d=d: e.tensor_scalar(out=CAall[:, :, 1, d, :], in0=W1[:], scalar1=-1.0, scalar2=None, op0=ALU.mult), r=[t0], w=[t_ca])
            ABv = AB[:, :, :, :].rearrange("p g r (j h) -> p g r j h", j=8)
            for j in range(8):
                k = 7 - j
                vv(lambda e, k=k: e.tensor_tensor(out=W1[:], in0=BBR[:], in1=bc16(APr[:, k, :]), op=ALU.mult))
                vv(lambda e, k=k: e.tensor_tensor(out=W2[:], in0=BBI[:], in1=bc16(APi[:, k, :]), op=ALU.mult))
                vv(lambda e, j=j: e.tensor_tensor(out=ABv[:, :, 0, j, :], in0=W1[:], in1=W2[:], op=ALU.subtract))
                vv(lambda e, k=k: e.tensor_tensor(out=W1[:], in0=BBR[:], in1=bc16(APi[:, k, :]), op=ALU.mult))
                vv(lambda e, k=k: e.tensor_tensor(out=W2[:], in0=BBI[:], in1=bc16(APr[:, k, :]), op=ALU.mult))
                vv(lambda e, j=j: e.tensor_tensor(out=ABv[:, :, 1, j, :], in0=W1[:], in1=W2[:], op=ALU.add))
            for blk in range(8):
                bk = 2 + (blk % 2)
                for s in range(8):
                    gg = blk * 4 + s // 2
                    ri = s % 2
                    T(lambda e, bk=bk, s=s, gg=gg, ri=ri: e.transpose(out=pbb[bk][:, 128 * s:128 * s + 128], in_=AB[:, gg, ri, :], identity=ident_bf[:]), r=[t0, t_const], w=[tb[bk]])
                for gh in range(2):
                    src = pbb[bk][:, :].rearrange("p (g r h q) -> p g r h q", g=4, r=2, h=2)[:, :, :, gh, :]
                    dst = MT[:, 32 * gh + blk * 4:32 * gh + blk * 4 + 4, :, :]
                    V(lambda e, src=src, dst=dst: e.tensor_copy(out=dst, in_=src), r=[tb[bk]], w=[t_mt])
            vv(lambda e: e.tensor_tensor(out=Dd[:], in0=Dbc[:], in1=ident_f[0:16, 0:16].unsqueeze(1).to_broadcast([16, 64, 16]), op=ALU.mult))
            for blk in range(16):
                bk = 4 + (blk % 2)
                for s in range(4):
                    g = blk * 4 + s
                    gh, gg = g // 32, g % 32
                    prt = slice(64 * gh, 64 * gh + 64)
                    for ri in range(2):
                        T(lambda e, bk=bk, s=s, prt=prt, gg=gg, ri=ri: e.matmul(pb[bk][0:16, 128 * s:128 * s + 128], lhsT=BBbf[prt, gg, ri, :], rhs=CAall[prt, gg, ri, 0:8, :].rearrange("p d h -> p (d h)"), start=(ri == 0), stop=(ri == 1)), r=[t0, t_ca], w=[tb[bk]])
                V(lambda e, bk=bk, blk=blk: e.tensor_copy(out=Kflat[:, 4 * blk:4 * blk + 4, :], in_=pb[bk][0:16, :].rearrange("p (g c) -> p g c", g=4)), r=[tb[bk], t0], w=[t0])
                V(lambda e, bk=bk, blk=blk: e.tensor_tensor(out=Kflat[:, 4 * blk:4 * blk + 4, 0:16], in0=pb[bk][0:16, :].rearrange("p (g c) -> p g c", g=4)[:, :, 0:16], in1=Dd[:, 4 * blk:4 * blk + 4, :], op=ALU.add), r=[tb[bk], t0], w=[t0])
            t_ks = P.tok("kscr")
            DS(lambda e: e.dma_start(out=kscr[:, :, 7:15, :].rearrange("g h d e -> h g (d e)"), in_=Kflat[:, :, :]), r=[t0], w=[t_ks])
            DS(lambda e: e.dma_start(out=kscr[:, :, 0:7, :].rearrange("g h d e -> h g (d e)"), in_=Z0[:, :, :]), r=[t0], w=[t_ks])
            kflat_d = kscr.rearrange("g h d e -> h g (d e)")
            for j in range(8):
                DS(lambda e, j=j: e.dma_start(out=Toep[16 * j:16 * j + 16, :, :], in_=kflat_d[:, :, (7 - j) * 16:(7 - j) * 16 + 128]), r=[t_ks], w=[t_toep])
            barrier()

        if stage in ("all", "ssm", "p0"):
            _phase0()

        def load_gain_bc(dst, name, tok):
            DS(lambda e: e.dma_start(out=dst, in_=dr[name][0, :].partition_broadcast(128)), w=[tok])

        def _phase1():
            Wu = view(84 * KB, [16, 1024], BF16)
            hT = view(116 * KB, [16, 1024], BF16)
            xbuf = [view(148 * KB, [2048], F32), view(156 * KB, [2048], F32)]
            hb = [view(164 * KB, [2048], BF16), view(168 * KB, [2048], BF16)]
            junk = view(172 * KB, [2048], BF16)
            gmix = view(176 * KB, [2048], F32)
            ublk = view(184 * KB, [64, 8, 16], BF16)
            stat = sbt("statA", [128, 32], F32)
            t_Wu, t_hT, t_junk, t_gm, t_ublk, t_stat = P.toks(6, "pa")
            t_x = P.toks(2, "x"); t_hb = P.toks(2, "hb")
            load_gain_bc(gmix, "g_mix_norm", t_gm)
            DS(lambda e: e.dma_start(out=Wu, in_=wb["w_in"][:, 832:1856].rearrange("(c p) n -> p c n", p=128)), r=[t_wb["w_in"]], w=[t_Wu])
            for hf in range(2):
                for i in range(8):
                    ti = hf * 8 + i
                    b = ti % 2
                    DS(lambda e, ti=ti, b=b: e.dma_start(out=xbuf[b], in_=dr["x"][128 * ti:128 * ti + 128, :]), w=[t_x[b]])
                    A(lambda e, b=b, ti=ti: e.activation(out=junk, in_=xbuf[b], func=AF.Square, accum_out=stat[:, ti:ti + 1]), r=[t_x[b]], w=[t_junk, t_stat])
                    for f in rstd_act(stat[:, 16 + ti:17 + ti], stat[:, ti:ti + 1], 2048):
                        A(f, r=[t_stat], w=[t_stat])
                    V(lambda e, b=b, ti=ti: e.scalar_tensor_tensor(out=hb[b], in0=xbuf[b], scalar=stat[:, 16 + ti:17 + ti], in1=gmix, op0=ALU.mult, op1=ALU.mult), r=[t_x[b], t_stat, t_gm], w=[t_hb[b]])
                    bk = [0, 1, 2, 3][(2 * ti) % 4:(2 * ti) % 4 + 2]
                    for kc in range(16):
                        bb = bk[kc // 8]
                        T(lambda e, b=b, kc=kc, bb=bb: e.transpose(out=pbb[bb][:, 128 * (kc % 8):128 * (kc % 8) + 128], in_=hb[b][:, 128 * kc:128 * kc + 128], identity=ident_bf[:]), r=[t_hb[b], t_const], w=[tb[bb]])
                    srcs = [pbb[bk[q]][:, :].rearrange("p (k t) -> p k t", k=8) for q in range(2)]
                    dsts = [hT[:, 8 * q:8 * q + 8, 128 * i:128 * i + 128] for q in range(2)]
                    A(lambda e, s_=srcs[0], d_=dsts[0]: e.activation(out=d_, in_=s_, func=AF.Copy), r=[tb[bk[0]]], w=[t_hT])
                    V(lambda e, s_=srcs[1], d_=dsts[1]: e.tensor_copy(out=d_, in_=s_), r=[tb[bk[1]]], w=[t_hT])
                for tau in range(8):
                    for cc in range(2):
                        bk = 4 + (tau * 2 + cc) % 4
                        for kc in range(16):
                            T(lambda e, bk=bk, kc=kc, tau=tau, cc=cc: e.matmul(pb[bk][:, :], lhsT=hT[:, kc, tau:1024:8], rhs=Wu[:, kc, 512 * cc:512 * cc + 512], start=(kc == 0), stop=(kc == 15)), r=[t_hT, t_Wu], w=[tb[bk]])
                        src = pb[bk][:, :].rearrange("p (g h) -> p g h", g=32)
                        dst = ublk[:, 32 * cc:32 * cc + 32, tau, :]
                        if cc == 0:
                            A(lambda e, src=src, dst=dst: e.activation(out=dst, in_=src, func=AF.Copy), r=[tb[bk]], w=[t_ublk])
                        else:
                            V(lambda e, src=src, dst=dst: e.tensor_copy(out=dst, in_=src), r=[tb[bk]], w=[t_ublk])
                for g8 in range(8):
                    bk = g8 % 2
                    for s in range(8):
                        g = g8 * 8 + s
                        T(lambda e, bk=bk, s=s, g=g: e.transpose(out=pbb[bk][:, 128 * s:128 * s + 128], in_=ublk[:, g, :, :].rearrange("p t h -> p (t h)"), identity=ident_bf[:]), r=[t_ublk, t_const], w=[tb[bk]])
                    src = pbb[bk][:, :].rearrange("p (g c) -> p g c", g=8)
                    dst = uT[:, 8 * g8:8 * g8 + 8, 128 * hf:128 * hf + 128]
                    if g8 % 2 == 0:
                        A(lambda e, src=src, dst=dst: e.activation(out=dst, in_=src, func=AF.Copy), r=[tb[bk]], w=[t_uT])
                    else:
                        V(lambda e, src=src, dst=dst: e.tensor_copy(out=dst, in_=src), r=[tb[bk]], w=[t_uT])
            barrier()

        if stage in ("all", "ssm", "pa"):
            _phase1()

        t_dbg = P.tok("dbg")

        def dump(src, col0, ncols, parts=128, rd=(), dst=None):
            d_ = dbg[0:parts, col0:col0 + ncols] if dst is None else dst
            DS(lambda e: e.dma_start(out=d_, in_=src, allow_slow_non_contiguous=True), r=list(rd), w=[t_dbg])

        osT = view(0, [8, 2048], BF16)
        t_osT = P.tok("osT")
        def _phase2():
            dS = view(84 * KB, [32, 2, 256], F32)
            tmp1 = view(148 * KB, [32, 2], F32)
            tmp2 = view(148 * KB + 256, [32, 2], F32)
            Sb = view(149 * KB, [32, 2, 257], BF16)
            t_dS, t_tmp1, t_tmp2, t_Sb = P.toks(4, "ps")
            for gg in range(32):
                bk = gg % 4
                for gh in range(2):
                    g = 32 * gh + gg
                    for ri in range(2):
                        T(lambda e, bk=bk, gh=gh, g=g, ri=ri: e.matmul(pb[bk][64 * gh:64 * gh + 64, 256 * ri:256 * ri + 256], lhsT=MT[:, g, ri, :], rhs=uT[:, g, :], start=True, stop=True), r=[t_mt, t_uT], w=[tb[bk]])
                if gg % 2 == 0:
                    V(lambda e, bk=bk, gg=gg: e.tensor_copy(out=dS[:, gg, :, :].rearrange("p r c -> p (r c)"), in_=pb[bk][:, :]), r=[tb[bk]], w=[t_dS])
                else:
                    A(lambda e, bk=bk, gg=gg: e.activation(out=dS[:, gg, :, :].rearrange("p r c -> p (r c)"), in_=pb[bk][:, :], func=AF.Copy), r=[tb[bk]], w=[t_dS])
            for c in range(1, 256):
                V(lambda e, c=c: e.tensor_tensor(out=tmp1, in0=dS[:, :, :, c - 1], in1=A8dup, op=ALU.mult), r=[t_dS, t_a8], w=[t_tmp1])
                V(lambda e, c=c: e.tensor_tensor(out=tmp2, in0=dS[:, :, ::-1, c - 1], in1=A8sw, op=ALU.mult), r=[t_dS, t_a8], w=[t_tmp2])
                V(lambda e, c=c: e.tensor_tensor(out=dS[:, :, :, c], in0=dS[:, :, :, c], in1=tmp1, op=ALU.add), r=[t_tmp1, t_dS], w=[t_dS])
                V(lambda e, c=c: e.tensor_tensor(out=dS[:, :, :, c], in0=dS[:, :, :, c], in1=tmp2, op=ALU.add), r=[t_tmp2, t_dS], w=[t_dS])
            V(lambda e: e.memset(Sb[:, :, :, 0:1], 0.0), w=[t_Sb])
            V(lambda e: e.tensor_copy(out=Sb[:, 0:16, :, 1:257], in_=dS[:, 0:16, :, :]), r=[t_dS], w=[t_Sb])
            A(lambda e: e.activation(out=Sb[:, 16:32, :, 1:257], in_=dS[:, 16:32, :, :], func=AF.Copy), r=[t_dS], w=[t_Sb])
            if stage == "ssm":
                dump(dS[:, :, :, 7::8], 8192, 2048, rd=[t_dS], dst=dbg[:, 8192:10240].rearrange("p (g r c) -> p g r c", g=32, r=2))
            barrier()
            gB = view(84 * KB, [2, 8, 1024], BF16)
            gT = view(116 * KB, [8, 2048], BF16)
            sqt = view(182 * KB, [512], F32)
            wv = view(184 * KB, [512], F32)
            sg = view(186 * KB, [512], F32)
            t_gB, t_gT, t_sqt, t_wv, t_sg = P.toks(5, "ps2")
            for ct in range(2):
                for g4 in range(16):
                    bk = g4 % 4
                    for s in range(4):
                        g = 4 * g4 + s
                        gh, gg = g // 32, g % 32
                        prt = slice(64 * gh, 64 * gh + 64)
                        o_ = pb[bk][:, 128 * s:128 * s + 128]
                        T(lambda e, o_=o_, g=g, ct=ct: e.matmul(o_, lhsT=uT[:, g, 128 * ct:128 * ct + 128], rhs=Toep[:, g, :], start=True, stop=False), r=[t_uT, t_toep], w=[tb[bk]])
                        for ri in range(2):
                            T(lambda e, o_=o_, prt=prt, gg=gg, ri=ri, ct=ct: e.matmul(o_, lhsT=Sb[prt, gg, ri, 128 * ct:128 * ct + 128], rhs=CAall[prt, gg, ri, 1:9, :].rearrange("p d h -> p (d h)"), start=False, stop=(ri == 1)), r=[t_Sb, t_ca], w=[tb[bk]])
                    A(lambda e, bk=bk: e.activation(out=sqt, in_=pb[bk][:, :], func=AF.Square), r=[tb[bk]], w=[t_sqt])
                    V(lambda e: e.tensor_scalar(out=wv, in0=sqt, scalar1=0.044715, scalar2=1.0, op0=ALU.mult, op1=ALU.add), r=[t_sqt], w=[t_wv])
                    V(lambda e, bk=bk: e.tensor_tensor(out=wv, in0=wv, in1=pb[bk][:, :], op=ALU.mult), r=[t_wv, tb[bk]], w=[t_wv])
                    A(lambda e: e.activation(out=sg, in_=wv, func=AF.Sigmoid, scale=1.5957691216057308), r=[t_wv], w=[t_sg])
                    dst = gB[:, ct, :, 64 * g4:64 * g4 + 64].rearrange("p t (g h) -> p g t h", g=4)
                    V(lambda e, bk=bk, dst=dst: e.tensor_tensor(out=dst, in0=sg.rearrange("p (g t h) -> p g t h", g=4, t=8), in1=pb[bk][:, :].rearrange("p (g t h) -> p g t h", g=4, t=8), op=ALU.mult), r=[t_sg, tb[bk]], w=[t_gB])
            for ct in range(2):
                for tau in range(8):
                    bk = 4 + (ct * 8 + tau) % 4
                    for kc in range(8):
                        T(lambda e, bk=bk, kc=kc, ct=ct, tau=tau: e.transpose(out=pbb[bk][:, 128 * kc:128 * kc + 128], in_=gB[:, ct, tau, 128 * kc:128 * kc + 128], identity=ident_bf[:]), r=[t_gB, t_const], w=[tb[bk]])
                    src = pbb[bk][:, :].rearrange("p (k c) -> p k c", k=8)
                    dst = gT[:, :, 1024 * ct + tau:1024 * ct + 1024:8]
                    if tau % 2 == 0:
                        V(lambda e, src=src, dst=dst: e.tensor_copy(out=dst, in_=src), r=[tb[bk]], w=[t_gT])
                    else:
                        A(lambda e, src=src, dst=dst: e.activation(out=dst, in_=src, func=AF.Copy), r=[tb[bk]], w=[t_gT])
            if stage == "ssm":
                gdump = view(149 * KB, [2048], F32)
                t_gd = P.tok("gd")
                V(lambda e: e.tensor_copy(out=gdump, in_=gT[:, 3, :]), r=[t_gT], w=[t_gd])
                dump(gdump, 2048, 2048, rd=[t_gd])
            barrier()
            Wglu = view(148 * KB, [8, 1024], BF16)
            sqs = view(164 * KB, [8, 512], BF16)
            sgb = view(172 * KB, [512], BF16)
            rb = view(173 * KB, [512], BF16)
            rbf = view(174 * KB, [512], F32)
            t_Wglu, t_sqs, t_sgb, t_rb = P.toks(4, "ps3")
            DS(lambda e: e.dma_start(out=Wglu, in_=wb["w_glu"].rearrange("(c p) n -> p c n", p=128)), r=[t_wb["w_glu"]], w=[t_Wglu])
            for tq in range(4):
                tsl = slice(512 * tq, 512 * tq + 512)
                for oc in range(8):
                    bk = oc % 4
                    for kc in range(8):
                        T(lambda e, bk=bk, kc=kc, oc=oc, tsl=tsl: e.matmul(pb[bk][:, :], lhsT=Wglu[:, kc, 128 * oc:128 * oc + 128], rhs=gT[:, kc, tsl], start=(kc == 0), stop=(kc == 7)), r=[t_Wglu, t_gT], w=[tb[bk]])
                    A(lambda e, bk=bk, oc=oc: e.activation(out=sgb, in_=pb[bk][:, :], func=AF.Sigmoid, bias=colv[:, 0, oc:oc + 1]), r=[tb[bk], t_col], w=[t_sgb])
                    V(lambda e, oc=oc, tsl=tsl: e.scalar_tensor_tensor(out=osT[:, oc, tsl], in0=gT[:, oc, tsl], scalar=colv[:, 1, oc:oc + 1], in1=sgb, op0=ALU.mult, op1=ALU.mult), r=[t_gT, t_col, t_sgb], w=[t_osT])
                    A(lambda e, oc=oc, tsl=tsl: e.activation(out=sqs[:, oc, :], in_=osT[:, oc, tsl], func=AF.Square, scale=colv[:, 2, oc:oc + 1]), r=[t_osT, t_col], w=[t_sqs])
                bq = 4 + tq % 2
                for oc in range(8):
                    T(lambda e, bq=bq, oc=oc: e.matmul(pb[bq][:, :], lhsT=ones_bf[:], rhs=sqs[:, oc, :], start=(oc == 0), stop=(oc == 7)), r=[t_sqs, t_const], w=[tb[bq]])
                A(lambda e, bq=bq: e.activation(out=rbf, in_=pb[bq][:, :], func=AF.Ln, scale=1.0 / 1024, bias=EPS), r=[tb[bq]], w=[t_rb])
                A(lambda e: e.activation(out=rbf, in_=rbf, func=AF.Exp, scale=-0.5), r=[t_rb], w=[t_rb])
                for oc in range(8):
                    V(lambda e, oc=oc, tsl=tsl: e.tensor_tensor(out=osT[:, oc, tsl], in0=osT[:, oc, tsl], in1=rbf, op=ALU.mult), r=[t_rb, t_osT], w=[t_osT])
            if stage == "ssm":
                od = view(84 * KB, [2048], F32)
                t_od = P.tok("od")
                V(lambda e: e.tensor_copy(out=od, in_=osT[:, 5, :]), r=[t_osT], w=[t_od])
                dump(od, 0, 2048, rd=[t_od])
            barrier()


        if stage in ("all", "ssm"):
            _phase2()

        cqT = view(32 * KB, [4, 2048], BF16)
        ckvT = view(48 * KB, [2, 2048], BF16)
        Rk = view(56 * KB, [16, 64], F32)
        CS2 = view(72 * KB, [16, 2, 32], F32)
        SN2 = view(76 * KB, [16, 2, 32], F32)
        oaT = view(80 * KB, [8, 2048], BF16)
        t_cqT, t_ckvT, t_Rk, t_trig, t_oaT = P.toks(5, "pb")
        statB = sbt("statB", [128, 16, 8], F32)
        t_statB = P.tok("statB")
        gsm = sbt("gsm", [128, 1152], F32)
        t_gsm = P.tok("gsm")
        statC = sbt("statC", [128, 16, 8], F32)
        ssqA = sbt("ssqA", [128, 16, 8], F32)
        t_statC, t_ssqA = P.toks(2, "pc2")
        def _phase3():
            Wq = view(80 * KB, [16, 832], BF16)
            xbuf = [view(112 * KB, [2048], F32), view(120 * KB, [2048], F32)]
            hb = [view(128 * KB, [2048], BF16), view(132 * KB, [2048], BF16)]
            junk = view(136 * KB, [2048], BF16)
            gmix = view(140 * KB, [2048], F32)
            hTt = [view(148 * KB, [16, 128], BF16), view(152 * KB, [16, 128], BF16)]
            cqb = view(156 * KB, [512], BF16)
            ckvb = view(157 * KB, [256], BF16)
            krg = view(158 * KB, [2, 32], F32)
            kA = view(158 * KB + 256, [2, 32], F32)
            kB_ = view(158 * KB + 512, [2, 32], F32)
            posi = view(159 * KB, [16], I32)
            posf = view(159 * KB + 64, [16], F32)
            invf = view(159 * KB + 128, [32], F32)
            ANG = view(160 * KB, [16, 32], F32)
            AN2 = view(162 * KB, [16, 32], F32)
            AN3 = view(164 * KB, [16, 32], F32)
            ANI = view(166 * KB, [16, 32], I32)
            t_Wq, t_junk, t_gm, t_cqb, t_ckvb, t_krg, t_ang = P.toks(7, "pb2")
            t_x = P.toks(2, "x"); t_hb = P.toks(2, "hb"); t_hTt = P.toks(2, "hTt")
            load_gain_bc(gmix, "g_mix_norm", t_gm)
            for nm, a, b_ in (("g_q_lora", 0, 512), ("g_kv_lora", 512, 768), ("g_q_head", 768, 960), ("g_k_head", 960, 1152)):
                DS(lambda e, nm=nm, a=a, b_=b_: e.dma_start(out=gsm[:, a:b_], in_=dr[nm][0, :].partition_broadcast(128)), w=[t_gsm])
            V(lambda e: e.tensor_scalar(out=gsm[:, 768:960], in0=gsm[:, 768:960], scalar1=192 ** -0.5, scalar2=None, op0=ALU.mult), r=[t_gsm], w=[t_gsm])
            DS(lambda e: e.dma_start(out=Wq, in_=wb["w_in"][:, 0:832].rearrange("(c p) n -> p c n", p=128)), r=[t_wb["w_in"]], w=[t_Wq])
            DS(lambda e: e.dma_start(out=posi, in_=dr["pos"][0, :].rearrange("(t p) -> p t", p=128), allow_slow_non_contiguous=True), w=[t_ang])
            V(lambda e: e.tensor_copy(out=posf, in_=posi), r=[t_ang], w=[t_ang])
            for i in range(32):
                val = float(np.float32(1.0) / np.float32(np.float32(10000.0) ** np.float32(np.float32(2 * i) / np.float32(64.0))))
                G(lambda e, i=i, val=val: e.memset(invf[:, i:i + 1], val), w=[t_ang])
            V(lambda e: e.tensor_tensor(out=ANG, in0=posf.unsqueeze(2).to_broadcast([128, 16, 32]), in1=invf.unsqueeze(1).to_broadcast([128, 16, 32]), op=ALU.mult), r=[t_ang], w=[t_ang])

            def trig2(dst, shift, sgn):
                V(lambda e: e.tensor_scalar(out=AN2, in0=ANG, scalar1=float(shift), scalar2=1.0 / (2 * math.pi), op0=ALU.add, op1=ALU.mult), r=[t_ang], w=[t_ang])
                V(lambda e: e.tensor_copy(out=ANI, in_=AN2), r=[t_ang], w=[t_ang])
                V(lambda e: e.tensor_copy(out=AN3, in_=ANI), r=[t_ang], w=[t_ang])
                V(lambda e: e.tensor_scalar(out=AN2, in0=ANG, scalar1=float(shift), scalar2=None, op0=ALU.add), r=[t_ang], w=[t_ang])
                V(lambda e: e.scalar_tensor_tensor(out=AN2, in0=AN3, scalar=-2 * math.pi, in1=AN2, op0=ALU.mult, op1=ALU.add), r=[t_ang], w=[t_ang])
                V(lambda e: e.tensor_scalar(out=AN2, in0=AN2, scalar1=-math.pi, scalar2=math.pi, op0=ALU.max, op1=ALU.min), r=[t_ang], w=[t_ang])
                A(lambda e: e.activation(out=dst, in_=AN2, func=AF.Sin, scale=float(sgn)), r=[t_ang], w=[t_trig])

            trig2(CS2[:, :, 0, :], math.pi / 2, 1.0)
            trig2(CS2[:, :, 1, :], math.pi / 2, 1.0)
            trig2(SN2[:, :, 0, :], 0.0, -1.0)
            trig2(SN2[:, :, 1, :], 0.0, 1.0)

            for ti in range(16):
                b = ti % 2
                tsl = slice(128 * ti, 128 * ti + 128)
                DS(lambda e, ti=ti, b=b: e.dma_start(out=xbuf[b], in_=dr["x"][128 * ti:128 * ti + 128, :]), w=[t_x[b]])
                A(lambda e, b=b, ti=ti: e.activation(out=junk, in_=xbuf[b], func=AF.Square, accum_out=statB[:, ti, 0:1]), r=[t_x[b]], w=[t_junk, t_statB])
                for f in rstd_act(statB[:, ti, 1:2], statB[:, ti, 0:1], 2048):
                    A(f, r=[t_statB], w=[t_statB])
                V(lambda e, b=b, ti=ti: e.scalar_tensor_tensor(out=hb[b], in0=xbuf[b], scalar=statB[:, ti, 1:2], in1=gmix, op0=ALU.mult, op1=ALU.mult), r=[t_x[b], t_statB, t_gm], w=[t_hb[b]])
                bk = [0, 1] if ti % 2 == 0 else [2, 3]
                for kc in range(16):
                    bb = bk[kc // 8]
                    T(lambda e, b=b, kc=kc, bb=bb: e.transpose(out=pbb[bb][:, 128 * (kc % 8):128 * (kc % 8) + 128], in_=hb[b][:, 128 * kc:128 * kc + 128], identity=ident_bf[:]), r=[t_hb[b], t_const], w=[tb[bb]])
                A(lambda e, b=b, bb=bk[0]: e.activation(out=hTt[b][:, 0:8, :], in_=pbb[bb][:, :].rearrange("p (k t) -> p k t", k=8), func=AF.Copy), r=[tb[bk[0]]], w=[t_hTt[b]])
                V(lambda e, b=b, bb=bk[1]: e.tensor_copy(out=hTt[b][:, 8:16, :], in_=pbb[bb][:, :].rearrange("p (k t) -> p k t", k=8)), r=[tb[bk[1]]], w=[t_hTt[b]])
                bq, bkv = (4, 5) if ti % 2 == 0 else (6, 7)
                for kc in range(16):
                    T(lambda e, b=b, kc=kc, bq=bq: e.matmul(pb[bq][:, :], lhsT=hTt[b][:, kc, :], rhs=Wq[:, kc, 0:512], start=(kc == 0), stop=(kc == 15)), r=[t_hTt[b], t_Wq], w=[tb[bq]])
                for kc in range(16):
                    T(lambda e, b=b, kc=kc, bkv=bkv: e.matmul(pb[bkv][:, 0:320], lhsT=hTt[b][:, kc, :], rhs=Wq[:, kc, 512:832], start=(kc == 0), stop=(kc == 15)), r=[t_hTt[b], t_Wq], w=[tb[bkv]])
                A(lambda e, bq=bq, ti=ti: e.activation(out=junk[:, 0:512], in_=pb[bq][:, :], func=AF.Square, accum_out=statB[:, ti, 2:3]), r=[tb[bq]], w=[t_junk, t_statB])
                for f in rstd_act(statB[:, ti, 3:4], statB[:, ti, 2:3], 512):
                    A(f, r=[t_statB], w=[t_statB])
                V(lambda e, bq=bq, ti=ti: e.scalar_tensor_tensor(out=cqb, in0=pb[bq][:, :], scalar=statB[:, ti, 3:4], in1=gsm[:, 0:512], op0=ALU.mult, op1=ALU.mult), r=[tb[bq], t_statB, t_gsm], w=[t_cqb])
                A(lambda e, bkv=bkv, ti=ti: e.activation(out=junk[:, 0:256], in_=pb[bkv][:, 0:256], func=AF.Square, accum_out=statB[:, ti, 4:5]), r=[tb[bkv]], w=[t_junk, t_statB])
                for f in rstd_act(statB[:, ti, 5:6], statB[:, ti, 4:5], 256):
                    A(f, r=[t_statB], w=[t_statB])
                V(lambda e, bkv=bkv, ti=ti: e.scalar_tensor_tensor(out=ckvb, in0=pb[bkv][:, 0:256], scalar=statB[:, ti, 5:6], in1=gsm[:, 512:768], op0=ALU.mult, op1=ALU.mult), r=[tb[bkv], t_statB, t_gsm], w=[t_ckvb])
                A(lambda e, bkv=bkv, ti=ti: e.activation(out=junk[:, 0:64], in_=pb[bkv][:, 256:320], func=AF.Square, accum_out=statB[:, ti, 6:7]), r=[tb[bkv]], w=[t_junk, t_statB])
                V(lambda e, bkv=bkv: e.tensor_tensor(out=krg.rearrange("p a b -> p (a b)"), in0=pb[bkv][:, 256:320], in1=gsm[:, 1088:1152], op=ALU.mult), r=[tb[bkv], t_gsm], w=[t_krg])
                V(lambda e, ti=ti: e.tensor_tensor(out=kA, in0=krg, in1=CS2[:, ti, :, :], op=ALU.mult), r=[t_krg, t_trig], w=[t_krg])
                V(lambda e, ti=ti: e.tensor_tensor(out=kB_, in0=krg[:, ::-1, :], in1=SN2[:, ti, :, :], op=ALU.mult), r=[t_krg, t_trig], w=[t_krg])
                V(lambda e, ti=ti: e.tensor_tensor(out=Rk[:, ti, :].rearrange("p (a b) -> p a b", a=2), in0=kA, in1=kB_, op=ALU.add), r=[t_krg], w=[t_Rk])
                for kc in range(4):
                    T(lambda e, kc=kc, bq=bq: e.transpose(out=pbb[bq][:, 128 * kc:128 * kc + 128], in_=cqb[:, 128 * kc:128 * kc + 128], identity=ident_bf[:]), r=[t_cqb, t_const], w=[tb[bq]])
                for kc in range(2):
                    T(lambda e, kc=kc, bq=bq: e.transpose(out=pbb[bq][:, 512 + 128 * kc:512 + 128 * kc + 128], in_=ckvb[:, 128 * kc:128 * kc + 128], identity=ident_bf[:]), r=[t_ckvb, t_const], w=[tb[bq]])
                A(lambda e, bq=bq, tsl=tsl: e.activation(out=cqT[:, :, tsl], in_=pbb[bq][:, 0:512].rearrange("p (k t) -> p k t", k=4), func=AF.Copy), r=[tb[bq]], w=[t_cqT])
                V(lambda e, bq=bq, tsl=tsl: e.tensor_copy(out=ckvT[:, :, tsl], in_=pbb[bq][:, 512:768].rearrange("p (k t) -> p k t", k=2)), r=[tb[bq]], w=[t_ckvT])
            if stage == "pb":
                od = view(170 * KB, [2048], F32)
                t_od = P.tok("od")
                V(lambda e: e.tensor_copy(out=od, in_=cqT[:, 1, :]), r=[t_cqT], w=[t_od])
                dump(od, 0, 2048, rd=[t_od])
                od2 = view(178 * KB, [2048], F32)
                V(lambda e: e.tensor_copy(out=od2, in_=ckvT[:, 1, :]), r=[t_ckvT], w=[t_od])
                dump(od2, 2048, 2048, rd=[t_od])
                dump(Rk.rearrange("p a b -> p (a b)"), 4096, 1024, rd=[t_Rk])
                dump(CS2.rearrange("p a b c -> p (a b c)"), 5120, 1024, rd=[t_trig])
                dump(SN2.rearrange("p a b c -> p (a b c)"), 6144, 1024, rd=[t_trig])
            barrier()

        if stage in ("all", "attn", "pb"):
            _phase3()

        def _phase4():
            Wuq = view(112 * KB, [4, 1536], BF16)
            Wukv = view(124 * KB, [2, 2048], BF16)
            t_Wuq, t_Wukv = P.toks(2, "pcw")
            DS(lambda e: e.dma_start(out=Wuq, in_=wb["w_uq"].rearrange("(c p) n -> p c n", p=128)), r=[t_wb["w_uq"]], w=[t_Wuq])
            DS(lambda e: e.dma_start(out=Wukv, in_=wb["w_ukv"].rearrange("(c p) n -> p c n", p=128)), r=[t_wb["w_ukv"]], w=[t_Wukv])
            hbuf = []
            for s_ in range(2):
                o0 = (132 + 21 * s_) * KB
                hbuf.append(dict(
                    QK=view(o0, [2, 2048], BF16), QKr=view(o0 + 8 * KB, [2, 2048], BF16),
                    Vh=view(o0 + 16 * KB, [16, 130], BF16), t=P.toks(3, "hb%d" % s_)))
            Pt = [view((174 + i) * KB, [512], BF16) for i in range(3)]
            t_Pt = P.toks(3, "Pt")
            qr = [view(177 * KB + 768 * i, [2, 32], F32) for i in range(2)]
            qA = [view(177 * KB + 768 * i + 256, [2, 32], F32) for i in range(2)]
            qB = [view(177 * KB + 768 * i + 512, [2, 32], F32) for i in range(2)]
            qb_ = [view(179 * KB, [256], BF16), view(179 * KB + 512, [256], BF16)]
            kb_ = [view(180 * KB, [256], BF16), view(180 * KB + 512, [256], BF16)]
            oh = [view(181 * KB, [128], BF16), view(181 * KB + 256, [128], BF16)]
            junkp = view(182 * KB, [256], BF16)
            junkc = view(183 * KB, [128], BF16)
            t_junkp, t_junkc = P.toks(2, "pcj")
            t_qr = P.toks(2, "qr")
            t_qb = P.toks(2, "qb"); t_kb = P.toks(2, "kb"); t_oh = P.toks(2, "oh")
            statE = sbt("statE", [128, 16, 2], F32)
            t_statE = P.tok("statE")
            tS = P.toks(2, "statCt")
            V(lambda e: e.memset(ssqA[:, :, :], 0.0), w=[t_ssqA])
            for s_ in range(2):
                V(lambda e, s_=s_: e.memset(hbuf[s_]["Vh"][:, :, 128:130], 1.0), w=[hbuf[s_]["t"][2]])
                V(lambda e, s_=s_: e.memset(qb_[s_][:, 192:256], 0.0), w=[t_qb[s_]])
                V(lambda e, s_=s_: e.memset(kb_[s_][:, 192:256], 0.0), w=[t_kb[s_]])

            def proj_tile(h, ti):
                HB = hbuf[h % 2]
                tQK, tQKr, tVh = HB["t"]
                tsl = slice(128 * ti, 128 * ti + 128)
                bk = 6 + ti % 2
                pr = ti % 2
                tSt = tS[pr]
                for kc in range(4):
                    T(lambda e, kc=kc: e.matmul(pb[bk][:, 0:192], lhsT=cqT[:, kc, tsl], rhs=Wuq[:, kc, 192 * h:192 * h + 192], start=(kc == 0), stop=(kc == 3)), r=[t_cqT, t_Wuq], w=[tb[bk]])
                for kc in range(2):
                    T(lambda e, kc=kc: e.matmul(pb[bk][:, 192:448], lhsT=ckvT[:, kc, tsl], rhs=Wukv[:, kc, 256 * h:256 * h + 256], start=(kc == 0), stop=(kc == 1)), r=[t_ckvT, t_Wukv], w=[tb[bk]])
                A(lambda e: e.activation(out=junkp[:, 0:192], in_=pb[bk][:, 0:192], func=AF.Square, accum_out=statC[:, ti, 0:1]), r=[tb[bk]], w=[t_junkp, tSt])
                A(lambda e: e.activation(out=junkp[:, 0:128], in_=pb[bk][:, 192:320], func=AF.Square, accum_out=statC[:, ti, 1:2]), r=[tb[bk]], w=[t_junkp, tSt])
                V(lambda e: e.tensor_tensor(out=statC[:, ti, 1:2], in0=statC[:, ti, 1:2], in1=statB[:, ti, 6:7], op=ALU.add), r=[tSt, t_statB], w=[tSt])
                A(lambda e: e.activation(out=statC[:, ti, 2:4], in_=statC[:, ti, 0:2], func=AF.Ln, scale=1.0 / 192, bias=EPS), r=[tSt], w=[tSt])
                A(lambda e: e.activation(out=statC[:, ti, 2:4], in_=statC[:, ti, 2:4], func=AF.Exp, scale=-0.5), r=[tSt], w=[tSt])
                V(lambda e: e.scalar_tensor_tensor(out=qb_[pr][:, 0:128], in0=pb[bk][:, 0:128], scalar=statC[:, ti, 2:3], in1=gsm[:, 768:896], op0=ALU.mult, op1=ALU.mult), r=[tb[bk], tSt, t_gsm], w=[t_qb[pr]])
                V(lambda e: e.scalar_tensor_tensor(out=qr[pr].rearrange("p a b -> p (a b)"), in0=pb[bk][:, 128:192], scalar=statC[:, ti, 2:3], in1=gsm[:, 896:960], op0=ALU.mult, op1=ALU.mult), r=[tb[bk], tSt, t_gsm], w=[t_qr[pr]])
                V(lambda e: e.tensor_tensor(out=qA[pr], in0=qr[pr], in1=CS2[:, ti, :, :], op=ALU.mult), r=[t_qr[pr], t_trig], w=[t_qr[pr]])
                V(lambda e: e.tensor_tensor(out=qB[pr], in0=qr[pr][:, ::-1, :], in1=SN2[:, ti, :, :], op=ALU.mult), r=[t_qr[pr], t_trig], w=[t_qr[pr]])
                V(lambda e: e.tensor_tensor(out=qb_[pr][:, 128:192].rearrange("p (a b) -> p a b", a=2), in0=qA[pr], in1=qB[pr], op=ALU.add), r=[t_qr[pr]], w=[t_qb[pr]])
                V(lambda e: e.scalar_tensor_tensor(out=kb_[pr][:, 0:128], in0=pb[bk][:, 192:320], scalar=statC[:, ti, 3:4], in1=gsm[:, 960:1088], op0=ALU.mult, op1=ALU.mult), r=[tb[bk], tSt, t_gsm], w=[t_kb[pr]])
                A(lambda e: e.activation(out=kb_[pr][:, 128:192], in_=Rk[:, ti, :], func=AF.Copy, scale=statC[:, ti, 3:4]), r=[t_Rk, tSt], w=[t_kb[pr]])
                A(lambda e: e.activation(out=HB["Vh"][:, ti, 0:128], in_=pb[bk][:, 320:448], func=AF.Copy), r=[tb[bk]], w=[tVh])
                T(lambda e: e.transpose(out=pbb[bk][:, 0:128], in_=qb_[pr][:, 0:128], identity=ident_bf[:]), r=[t_qb[pr], t_const], w=[tb[bk]])
                T(lambda e: e.transpose(out=pbb[bk][:, 128:256], in_=kb_[pr][:, 0:128], identity=ident_bf[:]), r=[t_kb[pr], t_const], w=[tb[bk]])
                T(lambda e: e.transpose(out=pbb[bk][:, 256:384], in_=qb_[pr][:, 128:256], identity=ident_bf[:]), r=[t_qb[pr], t_const], w=[tb[bk]])
                T(lambda e: e.transpose(out=pbb[bk][:, 384:512], in_=kb_[pr][:, 128:256], identity=ident_bf[:]), r=[t_kb[pr], t_const], w=[tb[bk]])
                A(lambda e: e.activation(out=HB["QK"][:, :, tsl], in_=pbb[bk][:, 0:256].rearrange("p (a t) -> p a t", a=2), func=AF.Copy), r=[tb[bk]], w=[tQK])
                A(lambda e: e.activation(out=HB["QKr"][:, :, tsl], in_=pbb[bk][:, 256:512].rearrange("p (a t) -> p a t", a=2), func=AF.Copy), r=[tb[bk]], w=[tQKr])

            def head_iters():
                its = []
                for qsb in range(4):
                    for kb in range(4 * qsb + 4):
                        its.append((qsb, kb))
                return its

            ITS = head_iters()

            def score_S(h, i):
                HB = hbuf[h % 2]
                tQK, tQKr, tVh = HB["t"]
                qsb, kb = ITS[i]
                q0 = max(512 * qsb, 128 * kb)
                q1 = 512 * qsb + 512
                nq = q1 - q0
                sb_ = 4 + (i % 2)
                pt = i % 3
                ksl = slice(128 * kb, 128 * kb + 128)
                T(lambda e: e.matmul(pb[sb_][:, 0:nq], lhsT=HB["QK"][:, 1, ksl], rhs=HB["QK"][:, 0, q0:q1], start=True, stop=False), r=[tQK], w=[tb[sb_]])
                T(lambda e: e.matmul(pb[sb_][:, 0:nq], lhsT=HB["QKr"][0:64, 1, ksl], rhs=HB["QKr"][0:64, 0, q0:q1], start=False, stop=True), r=[tQKr], w=[tb[sb_]])
                A(lambda e: e.activation(out=Pt[pt][:, 0:nq], in_=pb[sb_][:, 0:nq], func=AF.Exp), r=[tb[sb_]], w=[t_Pt[pt]])
                if 128 * kb >= 512 * qsb:
                    V(lambda e: e.tensor_tensor(out=Pt[pt][:, 0:128], in0=Pt[pt][:, 0:128], in1=mask_bf[:], op=ALU.mult), r=[t_Pt[pt], t_const], w=[t_Pt[pt]])

            def score_PV(h, i):
                HB = hbuf[h % 2]
                tQK, tQKr, tVh = HB["t"]
                qsb, kb = ITS[i]
                q0 = max(512 * qsb, 128 * kb)
                q1 = 512 * qsb + 512
                nq = q1 - q0
                pt = i % 3
                for j in range(nq // 128):
                    qi = (q0 + 128 * j) // 128
                    ob = qi % 4
                    T(lambda e, ob=ob, j=j, qi=qi: e.matmul(pb[ob][:, 0:130], lhsT=Pt[pt][:, 128 * j:128 * j + 128], rhs=HB["Vh"][:, kb, 0:130], start=(kb == 0), stop=(kb == qi)), r=[t_Pt[pt], tVh], w=[tb[ob]])
                if kb == 4 * qsb + 3:
                    for j in range(4):
                        qi = 4 * qsb + j
                        ob = qi % 4
                        pr = qi % 2
                        tsl = slice(128 * qi, 128 * qi + 128)
                        V(lambda e, ob=ob, qi=qi: e.reciprocal(out=statE[:, qi, 0:1], in_=pb[ob][:, 128:129]), r=[tb[ob]], w=[t_statE])
                        A(lambda e, ob=ob, qi=qi, pr=pr: e.activation(out=oh[pr], in_=pb[ob][:, 0:128], func=AF.Copy, scale=statE[:, qi, 0:1]), r=[tb[ob], t_statE], w=[t_oh[pr]])
                        A(lambda e, ob=ob, qi=qi: e.activation(out=junkc, in_=pb[ob][:, 0:128], func=AF.Square, scale=statE[:, qi, 0:1], accum_out=ssqA[:, qi, h:h + 1]), r=[tb[ob], t_statE], w=[t_junkc, t_ssqA])
                        tbk = 6 + qi % 2
                        T(lambda e, tbk=tbk, pr=pr: e.transpose(out=pbb[tbk][:, 512:640], in_=oh[pr], identity=ident_bf[:]), r=[t_oh[pr], t_const], w=[tb[tbk]])
                        A(lambda e, tbk=tbk, tsl=tsl: e.activation(out=oaT[:, h, tsl], in_=pbb[tbk][:, 512:640], func=AF.Copy, scale=colv[:, 3, h:h + 1]), r=[tb[tbk], t_col], w=[t_oaT])

            for ti in range(16):
                proj_tile(0, ti)
            NI = len(ITS)
            for h in range(8):
                nxt = 0
                score_S(h, 0)
                for i in range(NI):
                    if i + 1 < NI:
                        score_S(h, i + 1)
                    score_PV(h, i)
                    if h + 1 < 8 and i % 2 == 1 and nxt < 16:
                        proj_tile(h + 1, nxt)
                        nxt += 1
                while h + 1 < 8 and nxt < 16:
                    proj_tile(h + 1, nxt)
                    nxt += 1
            V(lambda e: e.tensor_reduce(out=statC[:, :, 6:7], in_=ssqA[:, :, :], axis=AX.X, op=ALU.add), r=[t_ssqA, tS[0], tS[1]], w=[t_statC])
            A(lambda e: e.activation(out=statC[:, :, 7:8], in_=statC[:, :, 6:7], func=AF.Ln, scale=1.0 / 1024, bias=EPS), r=[t_statC], w=[t_statC])
            A(lambda e: e.activation(out=statC[:, :, 7:8], in_=statC[:, :, 7:8], func=AF.Exp, scale=-0.5), r=[t_statC], w=[t_statC])
            if stage == "attn":
                od = view(132 * KB, [2048], F32)
                t_od = P.tok("od")
                for ii, kc in enumerate((0, 5)):
                    V(lambda e, kc=kc: e.tensor_copy(out=od, in_=oaT[:, kc, :]), r=[t_oaT], w=[t_od])
                    dump(od, 2048 * ii, 2048, rd=[t_od])
                od2 = view(140 * KB, [2048], F32)
                V(lambda e: e.tensor_copy(out=od2, in_=cqT[:, 1, :]), r=[t_cqT], w=[t_od])
                dump(od2, 4096, 2048, rd=[t_od])
            barrier()
        if stage in ("all", "attn"):
            _phase4()

        t_out = P.tok("out")
        def _phase5():
            X = view(32 * KB, [4, 2048], F32)
            hT = view(64 * KB, [16, 512], BF16)
            actT = view(112 * KB, [11, 512], BF16)
            slots = [view((123 + 16 * i) * KB, [8192], BF16) for i in range(3)]
            t_slot = P.toks(3, "slot")
            gffn = view(171 * KB, [2048], F32)
            gple = view(179 * KB, [2048], F32)
            hb = view(187 * KB, [2048], BF16)
            junk = view(191 * KB, [2048], BF16)
            sg = [view(195 * KB, [512], F32), view(197 * KB, [512], F32)]
            pf = view(199 * KB, [256], F32)
            pT = view(160 * KB + 0, [2, 512], BF16)
            statD = sbt("statD", [128, 8], F32)
            t_X, t_hT, t_act, t_gf, t_gp, t_hb, t_junk, t_pf, t_pT, t_statD = P.toks(10, "pd")
            t_sg = P.toks(2, "sg")
            pbf = gsm[:, 0:128].bitcast(BF16)
            pT = gsm[:, 128:640].bitcast(BF16).rearrange("p (k t) -> p k t", k=2)
            load_gain_bc(gffn, "g_ffn_norm", t_gf)
            load_gain_bc(gple, "g_ple_norm", t_gp)
            slot_i = [0]

            def next_slot():
                i = slot_i[0] % 3
                slot_i[0] += 1
                return i

            def super_tile(sI):
                t0_ = 512 * sI
                pend = []

                def tokr(tt):
                    return slice(t0_ + 128 * tt, t0_ + 128 * tt + 128)

                for tt in range(4):
                    DG(lambda e, tt=tt: e.dma_start(out=X[:, tt, :], in_=dr["x"][t0_ + 128 * tt:t0_ + 128 * tt + 128, :]), w=[t_X])

                def norm_to_hT(gb, t_g):
                    for tt in range(4):
                        A(lambda e, tt=tt: e.activation(out=junk, in_=X[:, tt, :], func=AF.Square, accum_out=statD[:, 0:1]), r=[t_X], w=[t_junk, t_statD])
                        for f in rstd_act(statD[:, 1:2], statD[:, 0:1], 2048):
                            A(f, r=[t_statD], w=[t_statD])
                        V(lambda e, tt=tt: e.scalar_tensor_tensor(out=hb, in0=X[:, tt, :], scalar=statD[:, 1:2], in1=gb, op0=ALU.mult, op1=ALU.mult), r=[t_X, t_statD, t_g], w=[t_hb])
                        for kc in range(16):
                            bb = 6 + kc // 8
                            T(lambda e, kc=kc, bb=bb: e.transpose(out=pbb[bb][:, 128 * (kc % 8):128 * (kc % 8) + 128], in_=hb[:, 128 * kc:128 * kc + 128], identity=ident_bf[:]), r=[t_hb, t_const], w=[tb[bb]])
                        for q in range(2):
                            A(lambda e, q=q, tt=tt: e.activation(out=hT[:, 8 * q:8 * q + 8, 128 * tt:128 * tt + 128], in_=pbb[6 + q][:, :].rearrange("p (k t) -> p k t", k=8), func=AF.Copy), r=[tb[6 + q]], w=[t_hT])

                def mk_wo(cc):
                    def load(si):
                        DS(lambda e: e.dma_start(out=slots[si].rearrange("p (c n) -> p c n", c=16), in_=wb["w_o"][:, 512 * cc:512 * cc + 512].rearrange("(c p) n -> p c n", p=128)), r=[t_wb["w_o"]], w=[t_slot[si]])

                    def comp(si):
                        W = slots[si].rearrange("p (c n) -> p c n", c=16)
                        for tt in range(4):
                            ba, bs = tt % 2, 2 + tt % 2
                            for kc in range(8):
                                T(lambda e, ba=ba, kc=kc, tt=tt: e.matmul(pb[ba][:, :], lhsT=oaT[:, kc, tokr(tt)], rhs=W[:, kc, :], start=(kc == 0), stop=(kc == 7)), r=[t_oaT, t_slot[si]], w=[tb[ba]])
                            for kc in range(8):
                                T(lambda e, bs=bs, kc=kc, tt=tt: e.matmul(pb[bs][:, :], lhsT=osT[:, kc, tokr(tt)], rhs=W[:, 8 + kc, :], start=(kc == 0), stop=(kc == 7)), r=[t_osT, t_slot[si]], w=[tb[bs]])
                            xs = X[:, tt, 512 * cc:512 * cc + 512]
                            ti = 4 * sI + tt
                            V(lambda e, ba=ba, xs=xs, ti=ti: e.scalar_tensor_tensor(out=xs, in0=pb[ba][:, :], scalar=statC[:, ti, 7:8], in1=xs, op0=ALU.mult, op1=ALU.add), r=[tb[ba], t_statC, t_X], w=[t_X])
                            V(lambda e, bs=bs, xs=xs: e.tensor_tensor(out=xs, in0=xs, in1=pb[bs][:, :], op=ALU.add), r=[tb[bs], t_X], w=[t_X])
                    return load, comp

                for cc in range(4):
                    pend.append(mk_wo(cc))

                pend.append((None, lambda si: norm_to_hT(gffn, t_gf)))

                def mk_gu(grp, b2):
                    ffcs = [j for j in (2 * b2, 2 * b2 + 1) if j < 11]
                    c0 = (grp * 11 + ffcs[0]) * 128
                    ncol = 128 * len(ffcs)

                    def load(si):
                        Wv = slots[si].rearrange("p (w c n) -> p w c n", w=2, c=16)
                        DS(lambda e: e.dma_start(out=Wv[:, 0, :, 0:ncol], in_=wb["w_gate"][:, c0:c0 + ncol].rearrange("(c p) n -> p c n", p=128)), r=[t_wb["w_gate"]], w=[t_slot[si]])
                        DS(lambda e: e.dma_start(out=Wv[:, 1, :, 0:ncol], in_=wb["w_up"][:, c0:c0 + ncol].rearrange("(c p) n -> p c n", p=128)), r=[t_wb["w_up"]], w=[t_slot[si]])

                    def comp(si):
                        Wv = slots[si].rearrange("p (w c n) -> p w c n", w=2, c=16)
                        for jj, j in enumerate(ffcs):
                            bg, bu = j % 2, 2 + j % 2
                            for kc in range(16):
                                T(lambda e, bg=bg, kc=kc, jj=jj: e.matmul(pb[bg][:, :], lhsT=Wv[:, 0, kc, 128 * jj:128 * jj + 128], rhs=hT[:, kc, :], start=(kc == 0), stop=(kc == 15)), r=[t_slot[si], t_hT], w=[tb[bg]])
                            for kc in range(16):
                                T(lambda e, bu=bu, kc=kc, jj=jj: e.matmul(pb[bu][:, :], lhsT=Wv[:, 1, kc, 128 * jj:128 * jj + 128], rhs=hT[:, kc, :], start=(kc == 0), stop=(kc == 15)), r=[t_slot[si], t_hT], w=[tb[bu]])
                            sgi = j % 2
                            A(lambda e, bg=bg, sgi=sgi: e.activation(out=sg[sgi], in_=pb[bg][:, :], func=AF.Silu), r=[tb[bg]], w=[t_sg[sgi]])
                            V(lambda e, bu=bu, sgi=sgi, j=j: e.tensor_tensor(out=actT[:, j, :], in0=sg[sgi], in1=pb[bu][:, :], op=ALU.mult), r=[t_sg[sgi], tb[bu]], w=[t_act])
                    return load, comp

                def mk_down(grp, cc):
                    def load(si):
                        Wv = slots[si][:, 0:11 * 512].rearrange("p (j n) -> p j n", j=11)
                        DS(lambda e: e.dma_start(out=Wv, in_=wb["w_down"][1408 * grp:1408 * grp + 1408, 512 * cc:512 * cc + 512].rearrange("(j p) n -> p j n", p=128)), r=[t_wb["w_down"]], w=[t_slot[si]])

                    def comp(si):
                        Wv = slots[si][:, 0:11 * 512].rearrange("p (j n) -> p j n", j=11)
                        for tt in range(4):
                            bd = 4 + tt % 2
                            for j in range(11):
                                T(lambda e, bd=bd, j=j, tt=tt: e.matmul(pb[bd][:, :], lhsT=actT[:, j, 128 * tt:128 * tt + 128], rhs=Wv[:, j, :], start=(j == 0), stop=(j == 10)), r=[t_act, t_slot[si]], w=[tb[bd]])
                            xs = X[:, tt, 512 * cc:512 * cc + 512]
                            V(lambda e, bd=bd, xs=xs: e.tensor_tensor(out=xs, in0=xs, in1=pb[bd][:, :], op=ALU.add), r=[tb[bd], t_X], w=[t_X])
                    return load, comp

                for grp in range(4):
                    for b2 in range(6):
                        pend.append(mk_gu(grp, b2))
                    for cc in range(4):
                        pend.append(mk_down(grp, cc))

                WPv = actT[:, 0:8, :].rearrange("p a b -> p (a b)").rearrange("p (c n) -> p c n", c=2)

                def ple_prep(si_unused):
                    DS(lambda e: e.dma_start(out=WPv, in_=wb["w_ple_proj"].rearrange("(c p) n -> p c n", p=128)), r=[t_wb["w_ple_proj"]], w=[t_act])
                    norm_to_hT(gple, t_gp)
                    for tt in range(4):
                        DG(lambda e, tt=tt: e.dma_start(out=pf, in_=dr["p"][t0_ + 128 * tt:t0_ + 128 * tt + 128, :]), w=[t_pf])
                        V(lambda e: e.tensor_copy(out=pbf, in_=pf), r=[t_pf], w=[t_hb])
                        for kc in range(2):
                            T(lambda e, kc=kc: e.transpose(out=pbb[6][:, 128 * kc:128 * kc + 128], in_=pbf[:, 128 * kc:128 * kc + 128], identity=ident_bf[:]), r=[t_hb, t_const], w=[tb[6]])
                        A(lambda e, tt=tt: e.activation(out=pT[:, :, 128 * tt:128 * tt + 128], in_=pbb[6][:, 0:256].rearrange("p (k t) -> p k t", k=2), func=AF.Copy), r=[tb[6]], w=[t_pT])

                pend.append((None, ple_prep))

                def mk_pg(cc):
                    def load(si):
                        DS(lambda e: e.dma_start(out=slots[si].rearrange("p (c n) -> p c n", c=16), in_=wb["w_ple_gate"][:, 512 * cc:512 * cc + 512].rearrange("(c p) n -> p c n", p=128)), r=[t_wb["w_ple_gate"]], w=[t_slot[si]])

                    def comp(si):
                        W = slots[si].rearrange("p (c n) -> p c n", c=16)
                        WP = WPv
                        for tt in range(4):
                            bg, bp = tt % 2, 2 + tt % 2
                            for kc in range(16):
                                T(lambda e, bg=bg, kc=kc, tt=tt: e.matmul(pb[bg][:, :], lhsT=hT[:, kc, 128 * tt:128 * tt + 128], rhs=W[:, kc, :], start=(kc == 0), stop=(kc == 15)), r=[t_hT, t_slot[si]], w=[tb[bg]])
                            for kc in range(2):
                                T(lambda e, bp=bp, kc=kc, tt=tt: e.matmul(pb[bp][:, :], lhsT=pT[:, kc, 128 * tt:128 * tt + 128], rhs=WP[:, kc, 512 * cc:512 * cc + 512], start=(kc == 0), stop=(kc == 1)), r=[t_pT, t_act], w=[tb[bp]])
                            sgi = tt % 2
                            A(lambda e, bg=bg, sgi=sgi: e.activation(out=sg[sgi], in_=pb[bg][:, :], func=AF.Sigmoid), r=[tb[bg]], w=[t_sg[sgi]])
                            V(lambda e, bp=bp, sgi=sgi: e.tensor_tensor(out=sg[sgi], in0=sg[sgi], in1=pb[bp][:, :], op=ALU.mult), r=[t_sg[sgi], tb[bp]], w=[t_sg[sgi]])
                            xs = X[:, tt, 512 * cc:512 * cc + 512]
                            V(lambda e, sgi=sgi, xs=xs: e.tensor_tensor(out=xs, in0=xs, in1=sg[sgi], op=ALU.add), r=[t_sg[sgi], t_X], w=[t_X])
                    return load, comp

                for cc in range(4):
                    pend.append(mk_pg(cc))

                loads = [(i, ld) for i, (ld, _) in enumerate(pend) if ld is not None]
                assigned = {}
                li = [0]

                def issue_loads_upto(n_ahead_idx):
                    while li[0] < len(loads) and loads[li[0]][0] <= n_ahead_idx:
                        idx, ld = loads[li[0]]
                        si = next_slot()
                        assigned[idx] = si
                        ld(si)
                        li[0] += 1

                for i, (ld, cp) in enumerate(pend):
                    cnt = 0
                    j = i
                    tgt = i
                    while j < len(pend) and cnt < 2:
                        if pend[j][0] is not None:
                            cnt += 1
                            tgt = j
                        j += 1
                    issue_loads_upto(tgt)
                    cp(assigned.get(i))
                for tt in range(4):
                    DG(lambda e, tt=tt: e.dma_start(out=out[t0_ + 128 * tt:t0_ + 128 * tt + 128, :], in_=X[:, tt, :]), r=[t_X], w=[t_out])

            for sI in range(4):
                super_tile(sI)

        if stage in ("all", "pd"):
            _phase5()

        finals = []
        if stage in ('all', 'pd'):
            finals.append(t_out)
        if dbg is not None:
            finals.append(t_dbg)
        P.emit(final_waits=finals)
        print("ops:", P.nops, {e: len(v) for e, v in P.ops.items()}, "sems:", P.nsem)
    return nc


def make_inputs(inputs, b):
    m = {
        "x": np.ascontiguousarray(inputs["x"][b]),
        "p": np.ascontiguousarray(inputs["p"][0, b]),
        "pos": np.ascontiguousarray(inputs["positions"][b].reshape(1, L).astype(np.int32)),
    }
    for n, r, c in WSPECS:
        m[n] = np.ascontiguousarray(inputs[n][0])
    for n, shp in SMALL:
        m[n] = np.ascontiguousarray(inputs[n][0].reshape(shp))
    return m


def kernel(**inputs):
    nc = build("all")
    in_maps = [make_inputs(inputs, b) for b in range(8)]
    res = run_bass_kernel_spmd(nc, in_maps, core_ids=list(range(8)))
    return np.stack([r["out"] for r in res.results], axis=0).astype(np.float32)
```

```python
import bisect
import math
from contextlib import ExitStack
import numpy as np
import concourse.bass as bass
import concourse.mybir as mybir
from concourse.bass_utils import run_bass_kernel_spmd

F32 = mybir.dt.float32
BF16 = mybir.dt.bfloat16
I32 = mybir.dt.int32
AF = mybir.ActivationFunctionType
ALU = mybir.AluOpType
AX = mybir.AxisListType

COMPUTE = ("tensor", "vector", "scalar", "gpsimd")
ALLENG = ("tensor", "vector", "scalar", "gpsimd", "sync")

L = 2048
D = 2048
NT = 16
DFF = 5632
EPS = 1e-6


class Tok:
    __slots__ = ("name", "last_writer", "readers", "sem", "dma_count")

    def __init__(self, name="t"):
        self.name = name
        self.last_writer = None
        self.readers = []
        self.sem = None
        self.dma_count = 0


class Op:
    __slots__ = ("eng", "fn", "edeps", "ddeps", "is_dma", "ordinal", "dma_tok")

    def __init__(self, eng, fn, is_dma):
        self.eng = eng
        self.fn = fn
        self.edeps = {}
        self.ddeps = {}
        self.is_dma = is_dma
        self.ordinal = None
        self.dma_tok = None


class Prog:
    def __init__(self, nc, stack):
        self.nc = nc
        self.stack = stack
        self.ops = {e: [] for e in ALLENG}
        self.eng_sem = {}
        self.eng_count = {e: 0 for e in COMPUTE}
        for e in COMPUTE:
            self.eng_sem[e] = stack.enter_context(nc.semaphore("s_" + e))
        self.nsem = 4
        self.ALL = Tok("ALL")
        self.nops = 0

    def tok(self, name="t"):
        return Tok(name)

    def toks(self, n, name="t"):
        return [Tok("%s%d" % (name, i)) for i in range(n)]

    def _dep_on(self, op, prev):
        if prev is None or prev is op:
            return
        if prev.is_dma:
            tok = prev.dma_tok
            cur = op.ddeps.get(id(tok))
            if cur is None or cur[1] < tok.dma_count:
                op.ddeps[id(tok)] = (tok, tok.dma_count)
            return
        if prev.eng == "tensor" and op.eng == "tensor" and not op.is_dma:
            return
        cur = op.edeps.get(prev.eng)
        if cur is None or cur < prev.ordinal:
            op.edeps[prev.eng] = prev.ordinal

    def add(self, eng, fn, reads=(), writes=(), dma=False, _barrier=False):
        op = Op(eng, fn, dma)
        self.nops += 1
        if not dma:
            self.eng_count[eng] += 1
            op.ordinal = self.eng_count[eng]
        reads = list(reads)
        writes = list(writes)
        if _barrier:
            writes.append(self.ALL)
        else:
            reads.append(self.ALL)
        for t in reads:
            self._dep_on(op, t.last_writer)
        for t in writes:
            self._dep_on(op, t.last_writer)
            for r in t.readers:
                self._dep_on(op, r)
        if dma:
            tok = writes[0]
            if tok.sem is None:
                tok.sem = self.stack.enter_context(self.nc.semaphore("d%d" % self.nsem))
                self.nsem += 1
            tok.dma_count += 16
            op.dma_tok = tok
        for t in writes:
            t.last_writer = op
            t.readers = []
        for t in reads:
            t.readers.append(op)
        self.ops[eng].append(op)
        return op

    def emit(self, final_waits=()):
        nc = self.nc
        needed = {e: set() for e in COMPUTE}
        for e, lst in self.ops.items():
            for op in lst:
                for pe, o in op.edeps.items():
                    needed[pe].add(o)
        ranks = {e: sorted(needed[e]) for e in COMPUTE}

        def rank(e, o):
            return bisect.bisect_right(ranks[e], o)

        with nc.Block() as block:
            def make(e):
                def body(eng):
                    waited_e = {}
                    waited_d = {}
                    for op in self.ops[e]:
                        for pe, o in op.edeps.items():
                            val = rank(pe, o)
                            if waited_e.get(pe, 0) >= val:
                                continue
                            eng.wait_ge(self.eng_sem[pe], val)
                            waited_e[pe] = val
                        for k, (tok, cnt) in op.ddeps.items():
                            if waited_d.get(k, 0) >= cnt:
                                continue
                            eng.wait_ge(tok.sem, cnt)
                            waited_d[k] = cnt
                        ins = op.fn(eng)
                        if op.is_dma:
                            ins.then_inc(op.dma_tok.sem, 16)
                        elif op.ordinal in needed[e]:
                            ins.then_inc(self.eng_sem[e], 1)
                    if e == "sync":
                        for tok in final_waits:
                            eng.wait_ge(tok.sem, tok.dma_count)
                return body
            block.tensor(make("tensor"))
            block.vector(make("vector"))
            block.scalar(make("scalar"))
            block.gpsimd(make("gpsimd"))
            block.sync(make("sync"))


WSPECS = [
    ("w_in", 2048, 1856), ("w_glu", 1024, 1024), ("w_uq", 512, 1536), ("w_ukv", 256, 2048),
    ("w_o", 2048, 2048), ("w_gate", 2048, 5632), ("w_up", 2048, 5632), ("w_down", 5632, 2048),
    ("w_ple_gate", 2048, 2048), ("w_ple_proj", 256, 2048),
]
SMALL = [
    ("g_mix_norm", [1, 2048]), ("g_q_lora", [1, 512]), ("g_kv_lora", [1, 256]),
    ("g_q_head", [1, 192]), ("g_k_head", [1, 192]), ("lam_re", [64, 64]), ("lam_im", [64, 64]),
    ("log_dt", [1, 64]), ("b_re", [64, 64, 16]), ("b_im", [64, 64, 16]), ("c_re", [64, 16, 64]),
    ("c_im", [64, 16, 64]), ("d_skip", [1, 1024]), ("b_glu", [1, 1024]), ("g_out_attn", [1, 1024]),
    ("g_out_ssm", [1, 1024]), ("g_ffn_norm", [1, 2048]), ("g_ple_norm", [1, 2048]),
]

ARENA_BYTES = 200 * 1024


def build(stage="all"):
    nc = bass.Bass("TRN2", target_bir_lowering=False)
    dr = {}
    dr["x"] = nc.dram_tensor("x", [L, D], F32, kind="ExternalInput").ap()
    dr["p"] = nc.dram_tensor("p", [L, 256], F32, kind="ExternalInput").ap()
    dr["pos"] = nc.dram_tensor("pos", [1, L], I32, kind="ExternalInput").ap()
    for n, r, c in WSPECS:
        dr[n] = nc.dram_tensor(n, [r, c], F32, kind="ExternalInput").ap()
    for n, shp in SMALL:
        dr[n] = nc.dram_tensor(n, shp, F32, kind="ExternalInput").ap()
    out = nc.dram_tensor("out", [L, D], F32, kind="ExternalOutput").ap()
    wb = {}
    for n, r, c in WSPECS:
        wb[n] = nc.dram_tensor(n + "_bf", [r, c], BF16).ap()
    kscr = nc.dram_tensor("kscr", [64, 16, 15, 16], BF16).ap()
    dbg = None
    if stage != "all":
        dbg = nc.dram_tensor("dbg", [128, 16384], F32, kind="ExternalOutput").ap()

    with ExitStack() as st:
        P = Prog(nc, st)
        sbt = lambda n, s, d: st.enter_context(nc.sbuf_tensor(n, s, d))
        arena = sbt("arena", [128, ARENA_BYTES // 2], BF16)

        def view(off, shape, dt, parts=(0, 128)):
            esz = 4 if dt in (F32, I32) else 2
            n = int(np.prod(shape))
            nb = n * esz
            assert off % 4 == 0 and off + nb <= ARENA_BYTES, (off, nb)
            a = arena[parts[0]:parts[1], off // 2:(off + nb) // 2]
            if dt != BF16:
                a = a.bitcast(dt)
            if len(shape) > 1:
                names = " ".join("d%d" % i for i in range(len(shape)))
                kw = {"d%d" % i: int(s) for i, s in enumerate(shape)}
                a = a.rearrange("p (%s) -> p %s" % (names, names), **kw)
            return a

        KB = 1024
        pb = [st.enter_context(nc.psum_tensor("pb%d" % i, [128, 512], F32)) for i in range(8)]
        tb = P.toks(8, "pb")
        pbb = [b[:].bitcast(BF16) for b in pb]

        ident_bf = sbt("ident_bf", [128, 128], BF16)
        ident_f = sbt("ident_f", [128, 128], F32)
        mask_bf = sbt("mask_bf", [128, 128], BF16)
        ones_bf = sbt("ones_bf", [128, 128], BF16)
        scr1 = sbt("scr1", [128, 4], F32)
        t_const = P.tok("const")

        def barrier():
            P.add("vector", lambda e: e.memset(scr1[:, 0:1], 0.0), _barrier=True)

        V = lambda fn, r=(), w=(): P.add("vector", fn, reads=r, writes=w)
        A = lambda fn, r=(), w=(): P.add("scalar", fn, reads=r, writes=w)
        G = lambda fn, r=(), w=(): P.add("gpsimd", fn, reads=r, writes=w)
        T = lambda fn, r=(), w=(): P.add("tensor", fn, reads=r, writes=w)
        DS = lambda fn, r=(), w=(): P.add("sync", fn, reads=r, writes=w, dma=True)
        DG = lambda fn, r=(), w=(): P.add("gpsimd", fn, reads=r, writes=w, dma=True)

        G(lambda e: e.memset(ident_bf[:], 0.0), w=[t_const])
        G(lambda e: e.affine_select(out=ident_bf[:], in_=ident_bf[:], pattern=[[-1, 128]], compare_op=ALU.not_equal, fill=1.0, base=0, channel_multiplier=1), r=[t_const], w=[t_const])
        G(lambda e: e.memset(ident_f[:], 0.0), w=[t_const])
        G(lambda e: e.affine_select(out=ident_f[:], in_=ident_f[:], pattern=[[-1, 128]], compare_op=ALU.not_equal, fill=1.0, base=0, channel_multiplier=1), r=[t_const], w=[t_const])
        G(lambda e: e.memset(mask_bf[:], 1.0), w=[t_const])
        G(lambda e: e.affine_select(out=mask_bf[:], in_=mask_bf[:], pattern=[[1, 128]], compare_op=ALU.is_ge, fill=0.0, base=0, channel_multiplier=-1), r=[t_const], w=[t_const])
        G(lambda e: e.memset(ones_bf[:], 1.0), w=[t_const])

        colv = sbt("colv", [128, 4, 8], F32)
        t_col = P.tok("col")
        for idx, nm in enumerate(("b_glu", "g_out_ssm", "g_out_ssm", "g_out_attn")):
            DS(lambda e, idx=idx, nm=nm: e.dma_start(out=colv[:, idx, :], in_=dr[nm][0, :].rearrange("(c p) -> p c", p=128), allow_slow_non_contiguous=True), w=[t_col])
        V(lambda e: e.reciprocal(out=colv[:, 2, :], in_=colv[:, 2, :]), r=[t_col], w=[t_col])

        t_wb = {n: P.tok("wb_" + n) for n, _, _ in WSPECS}

        def precast(n):
            r, c = [(rr, cc) for nn, rr, cc in WSPECS if nn == n][0]
            step = 256
            for r0 in range(0, r, step):
                r1 = min(r, r0 + step)
                DG(lambda e, r0=r0, r1=r1: e.dma_start(out=wb[n][r0:r1, :], in_=dr[n][r0:r1, :]), w=[t_wb[n]])

        need_w = {"ssm": ["w_in", "w_glu"], "pa": ["w_in"], "attn": ["w_in", "w_uq", "w_ukv"], "pb": ["w_in"]}.get(stage, [n for n, _, _ in WSPECS])
        for n in need_w:
            precast(n)

        def rstd_act(dst, src, n):
            return [lambda e: e.activation(out=dst, in_=src, func=AF.Ln, scale=1.0 / n, bias=EPS),
                    lambda e: e.activation(out=dst, in_=dst, func=AF.Exp, scale=-0.5)]

        o_toep, o_mt, o_ca, o_a8, o_uT = 0, 16 * KB, 32 * KB, 50 * KB, 52 * KB
        Toep = view(o_toep, [64, 128], BF16)
        MT = view(o_mt, [64, 2, 64], BF16)
        CAall = view(o_ca, [32, 2, 9, 16], BF16)
        A8dup = view(o_a8, [32, 2], F32)
        A8sw = view(o_a8 + 256, [32, 2], F32)
        uT = view(o_uT, [64, 256], BF16)
        t_toep, t_mt, t_ca, t_a8, t_uT = P.toks(5, "tab")

        def _phase0():
            base = 84 * KB
            cur = [base]

            def al(shape, dt, parts=(0, 128)):
                esz = 4 if dt in (F32, I32) else 2
                nb = int(np.prod(shape)) * esz
                nb = (nb + 31) // 32 * 32
                v = view(cur[0], shape, dt, parts)
                cur[0] += nb
                return v

            s32 = lambda: al([32], F32)
            LR, LI, LDT, DT, XX, TH, MAG, CS, SN, AR, AI, DEN, NR, CFR, CFI, T1, T2, T3 = [s32() for _ in range(18)]
            TI = al([32], I32)
            BRE = al([32, 16], F32); BIM = al([32, 16], F32)
            BBR = al([32, 16], F32); BBI = al([32, 16], F32)
            CST = [al([4, 128], F32), al([4, 128], F32)]
            CT = [al([32, 16], F32), al([32, 16], F32)]
            APr = al([9, 32], F32); APi = al([9, 32], F32)
            AB = al([32, 2, 128], BF16)
            BBbf = al([32, 2, 16], BF16)
            Kflat = al([64, 128], BF16, (0, 16))
            Dbc = al([64, 16], F32, (0, 16))
            Dd = al([64, 16], F32, (0, 16))
            W1 = al([32, 16], F32); W2 = al([32, 16], F32); W3 = al([32, 16], F32)
            Z0 = al([64, 112], BF16, (0, 16))
            t0 = P.tok("p0")
            tS = [t0]

            for gh in range(2):
                ps_ = slice(64 * gh, 64 * gh + 64)
                gs = slice(32 * gh, 32 * gh + 32)
                DS(lambda e, ps_=ps_, gs=gs: e.dma_start(out=LR[ps_, :], in_=dr["lam_re"][gs, :].rearrange("g p -> p g"), allow_slow_non_contiguous=True), w=[t0])
                DS(lambda e, ps_=ps_, gs=gs: e.dma_start(out=LI[ps_, :], in_=dr["lam_im"][gs, :].rearrange("g p -> p g"), allow_slow_non_contiguous=True), w=[t0])
                DS(lambda e, ps_=ps_, gs=gs: e.dma_start(out=LDT[ps_, :], in_=dr["log_dt"][0, gs].partition_broadcast(64)), w=[t0])
                DS(lambda e, ps_=ps_, gs=gs: e.dma_start(out=BRE[ps_, :, :], in_=dr["b_re"][gs, :, :].rearrange("g p h -> p g h")), w=[t0])
                DS(lambda e, ps_=ps_, gs=gs: e.dma_start(out=BIM[ps_, :, :], in_=dr["b_im"][gs, :, :].rearrange("g p h -> p g h")), w=[t0])
                for gt in range(4):
                    g0 = 32 * gh + 8 * gt
                    for ri, nm in enumerate(("c_re", "c_im")):
                        DS(lambda e, g0=g0, gt=gt, gh=gh, ri=ri, nm=nm: e.dma_start(out=CST[ri][:, gt, 64 * gh:64 * gh + 64], in_=dr[nm][g0:g0 + 8, :, :].rearrange("g h p -> (g h) p")), w=[t0])
            DS(lambda e: e.dma_start(out=Dbc[:, :, :].rearrange("p g h -> p (g h)"), in_=dr["d_skip"][0, :].partition_broadcast(16)), w=[t0])
            V(lambda e: e.memset(Z0[:], 0.0), w=[t0])

            def vv(fn):
                V(fn, r=[t0, t_const], w=[t0])

            def aa(fn):
                A(fn, r=[t0], w=[t0])

            vv(lambda e: e.tensor_scalar(out=LR[:], in0=LR[:], scalar1=-1e-4, scalar2=None, op0=ALU.min))
            aa(lambda e: e.activation(out=DT[:], in_=LDT[:], func=AF.Exp))
            vv(lambda e: e.tensor_tensor(out=XX[:], in0=LR[:], in1=DT[:], op=ALU.mult))
            vv(lambda e: e.tensor_tensor(out=TH[:], in0=LI[:], in1=DT[:], op=ALU.mult))
            vv(lambda e: e.tensor_scalar(out=MAG[:], in0=XX[:], scalar1=1.0 / 8, scalar2=1.0, op0=ALU.mult, op1=ALU.add))
            for k in range(7, 0, -1):
                vv(lambda e, k=k: e.scalar_tensor_tensor(out=MAG[:], in0=XX[:], scalar=1.0 / k, in1=MAG[:], op0=ALU.mult, op1=ALU.mult))
                vv(lambda e: e.tensor_scalar(out=MAG[:], in0=MAG[:], scalar1=1.0, scalar2=None, op0=ALU.add))

            def trig(dst, shift):
                vv(lambda e: e.tensor_scalar(out=T1[:], in0=TH[:], scalar1=float(shift), scalar2=1.0 / (2 * math.pi), op0=ALU.add, op1=ALU.mult))
                vv(lambda e: e.tensor_copy(out=TI[:], in_=T1[:]))
                vv(lambda e: e.tensor_copy(out=T2[:], in_=TI[:]))
                vv(lambda e: e.tensor_scalar(out=T1[:], in0=TH[:], scalar1=float(shift), scalar2=None, op0=ALU.add))
                vv(lambda e: e.scalar_tensor_tensor(out=T1[:], in0=T2[:], scalar=-2 * math.pi, in1=T1[:], op0=ALU.mult, op1=ALU.add))
                vv(lambda e: e.tensor_scalar(out=T1[:], in0=T1[:], scalar1=-math.pi, scalar2=math.pi, op0=ALU.max, op1=ALU.min))
                aa(lambda e: e.activation(out=dst, in_=T1[:], func=AF.Sin))

            trig(SN[:], 0.0)
            trig(CS[:], math.pi / 2)
            vv(lambda e: e.tensor_tensor(out=AR[:], in0=MAG[:], in1=CS[:], op=ALU.mult))
            vv(lambda e: e.tensor_tensor(out=AI[:], in0=MAG[:], in1=SN[:], op=ALU.mult))
            vv(lambda e: e.tensor_tensor(out=DEN[:], in0=LR[:], in1=LR[:], op=ALU.mult))
            vv(lambda e: e.tensor_tensor(out=T1[:], in0=LI[:], in1=LI[:], op=ALU.mult))
            vv(lambda e: e.tensor_tensor(out=DEN[:], in0=DEN[:], in1=T1[:], op=ALU.add))
            vv(lambda e: e.reciprocal(out=DEN[:], in_=DEN[:]))
            vv(lambda e: e.tensor_scalar(out=NR[:], in0=AR[:], scalar1=-1.0, scalar2=None, op0=ALU.add))
            vv(lambda e: e.tensor_tensor(out=T1[:], in0=NR[:], in1=LR[:], op=ALU.mult))
            vv(lambda e: e.tensor_tensor(out=T2[:], in0=AI[:], in1=LI[:], op=ALU.mult))
            vv(lambda e: e.tensor_tensor(out=T1[:], in0=T1[:], in1=T2[:], op=ALU.add))
            vv(lambda e: e.tensor_tensor(out=CFR[:], in0=T1[:], in1=DEN[:], op=ALU.mult))
            vv(lambda e: e.tensor_tensor(out=T1[:], in0=AI[:], in1=LR[:], op=ALU.mult))
            vv(lambda e: e.tensor_tensor(out=T2[:], in0=NR[:], in1=LI[:], op=ALU.mult))
            vv(lambda e: e.tensor_tensor(out=T1[:], in0=T1[:], in1=T2[:], op=ALU.subtract))
            vv(lambda e: e.tensor_tensor(out=CFI[:], in0=T1[:], in1=DEN[:], op=ALU.mult))
            vv(lambda e: e.memset(APr[:, 0, :], 1.0))
            vv(lambda e: e.memset(APi[:, 0, :], 0.0))
            vv(lambda e: e.tensor_copy(out=APr[:, 1, :], in_=AR[:]))
            vv(lambda e: e.tensor_copy(out=APi[:, 1, :], in_=AI[:]))
            for k in range(2, 9):
                vv(lambda e, k=k: e.tensor_tensor(out=T1[:], in0=APr[:, k - 1, :], in1=AR[:], op=ALU.mult))
                vv(lambda e, k=k: e.tensor_tensor(out=T2[:], in0=APi[:, k - 1, :], in1=AI[:], op=ALU.mult))
                vv(lambda e, k=k: e.tensor_tensor(out=APr[:, k, :], in0=T1[:], in1=T2[:], op=ALU.subtract))
                vv(lambda e, k=k: e.tensor_tensor(out=T1[:], in0=APr[:, k - 1, :], in1=AI[:], op=ALU.mult))
                vv(lambda e, k=k: e.tensor_tensor(out=T2[:], in0=APi[:, k - 1, :], in1=AR[:], op=ALU.mult))
                vv(lambda e, k=k: e.tensor_tensor(out=APi[:, k, :], in0=T1[:], in1=T2[:], op=ALU.add))
            V(lambda e: e.tensor_copy(out=A8dup[:, :, 0], in_=APr[:, 8, :]), r=[t0], w=[t_a8])
            V(lambda e: e.tensor_copy(out=A8dup[:, :, 1], in_=APr[:, 8, :]), r=[t0], w=[t_a8])
            V(lambda e: e.tensor_scalar(out=A8sw[:, :, 0], in0=APi[:, 8, :], scalar1=-1.0, scalar2=None, op0=ALU.mult), r=[t0], w=[t_a8])
            V(lambda e: e.tensor_copy(out=A8sw[:, :, 1], in_=APi[:, 8, :]), r=[t0], w=[t_a8])

            def bc16(a):
                return a.unsqueeze(2).to_broadcast([128, 32, 16])

            vv(lambda e: e.tensor_tensor(out=W1[:], in0=BRE[:], in1=bc16(CFR[:]), op=ALU.mult))
            vv(lambda e: e.tensor_tensor(out=W2[:], in0=BIM[:], in1=bc16(CFI[:]), op=ALU.mult))
            vv(lambda e: e.tensor_tensor(out=BBR[:], in0=W1[:], in1=W2[:], op=ALU.subtract))
            vv(lambda e: e.tensor_tensor(out=W1[:], in0=BIM[:], in1=bc16(CFR[:]), op=ALU.mult))
            vv(lambda e: e.tensor_tensor(out=W2[:], in0=BRE[:], in1=bc16(CFI[:]), op=ALU.mult))
            vv(lambda e: e.tensor_tensor(out=BBI[:], in0=W1[:], in1=W2[:], op=ALU.add))
            vv(lambda e: e.tensor_copy(out=BBbf[:, :, 0, :], in_=BBR[:]))
            vv(lambda e: e.tensor_copy(out=BBbf[:, :, 1, :], in_=BBI[:]))
            for ri in range(2):
                for gt in range(4):
                    T(lambda e, ri=ri, gt=gt: e.transpose(out=pb[ri][:, 128 * gt:128 * gt + 128], in_=CST[ri][:, gt, :], identity=ident_f[:]), r=[t0, t_const], w=[tb[ri]])
                V(lambda e, ri=ri: e.tensor_copy(out=CT[ri][:, :, :].rearrange("p g h -> p (g h)"), in_=pb[ri][:, :]), r=[tb[ri], t0], w=[t0])
            for d in range(9):
                vv(lambda e, d=d: e.tensor_tensor(out=W1[:], in0=CT[0][:], in1=bc16(APr[:, d, :]), op=ALU.mult))
                vv(lambda e, d=d: e.tensor_tensor(out=W2[:], in0=CT[1][:], in1=bc16(APi[:, d, :]), op=ALU.mult))
                V(lambda e, d=d: e.tensor_tensor(out=CAall[:, :, 0, d, :], in0=W1[:], in1=W2[:], op=ALU.subtract), r=[t0], w=[t_ca])
                vv(lambda e, d=d: e.tensor_tensor(out=W1[:], in0=CT[0][:], in1=bc16(APi[:, d, :]), op=ALU.mult))
                vv(lambda e, d=d: e.tensor_tensor(out=W2[:], in0=CT[1][:], in1=bc16(APr[:, d, :]), op=ALU.mult))
                vv(lambda e, d=d: e.tensor_tensor(out=W1[:], in0=W1[:], in1=W2[:], op=ALU.add))
                V(lambda e, d=d: e.tensor_scalar(out=CAall[:, :, 1, d, :], in0=W1[:], scalar1=-1.0, scalar2=None, op0=ALU.mult), r=[t0], w=[t_ca])
            ABv = AB[:, :, :, :].rearrange("p g r (j h) -> p g r j h", j=8)
            for j in range(8):
                k = 7 - j
                vv(lambda e, k=k: e.tensor_tensor(out=W1[:], in0=BBR[:], in1=bc16(APr[:, k, :]), op=ALU.mult))
                vv(lambda e, k=k: e.tensor_tensor(out=W2[:], in0=BBI[:], in1=bc16(APi[:, k, :]), op=ALU.mult))
                vv(lambda e, j=j: e.tensor_tensor(out=ABv[:, :, 0, j, :], in0=W1[:], in1=W2[:], op=ALU.subtract))
                vv(lambda e, k=k: e.tensor_tensor(out=W1[:], in0=BBR[:], in1=bc16(APi[:, k, :]), op=ALU.mult))
                vv(lambda e, k=k: e.tensor_tensor(out=W2[:], in0=BBI[:], in1=bc16(APr[:, k, :]), op=ALU.mult))
                vv(lambda e, j=j: e.tensor_tensor(out=ABv[:, :, 1, j, :], in0=W1[:], in1=W2[:], op=ALU.add))
            for blk in range(8):
                bk = 2 + (blk % 2)
                for s in range(8):
                    gg = blk * 4 + s // 2
                    ri = s % 2
                    T(lambda e, bk=bk, s=s, gg=gg, ri=ri: e.transpose(out=pbb[bk][:, 128 * s:128 * s + 128], in_=AB[:, gg, ri, :], identity=ident_bf[:]), r=[t0, t_const], w=[tb[bk]])
                for gh in range(2):
                    src = pbb[bk][:, :].rearrange("p (g r h q) -> p g r h q", g=4, r=2, h=2)[:, :, :, gh, :]
                    dst = MT[:, 32 * gh + blk * 4:32 * gh + blk * 4 + 4, :, :]
                    V(lambda e, src=src, dst=dst: e.tensor_copy(out=dst, in_=src), r=[tb[bk]], w=[t_mt])
            vv(lambda e: e.tensor_tensor(out=Dd[:], in0=Dbc[:], in1=ident_f[0:16, 0:16].unsqueeze(1).to_broadcast([16, 64, 16]), op=ALU.mult))
            for blk in range(16):
                bk = 4 + (blk % 2)
                for s in range(4):
                    g = blk * 4 + s
                    gh, gg = g // 32, g % 32
                    prt = slice(64 * gh, 64 * gh + 64)
                    for ri in range(2):
                        T(lambda e, bk=bk, s=s, prt=prt, gg=gg, ri=ri: e.matmul(pb[bk][0:16, 128 * s:128 * s + 128], lhsT=BBbf[prt, gg, ri, :], rhs=CAall[prt, gg, ri, 0:8, :].rearrange("p d h -> p (d h)"), start=(ri == 0), stop=(ri == 1)), r=[t0, t_ca], w=[tb[bk]])
                V(lambda e, bk=bk, blk=blk: e.tensor_copy(out=Kflat[:, 4 * blk:4 * blk + 4, :], in_=pb[bk][0:16, :].rearrange("p (g c) -> p g c", g=4)), r=[tb[bk], t0], w=[t0])
                V(lambda e, bk=bk, blk=blk: e.tensor_tensor(out=Kflat[:, 4 * blk:4 * blk + 4, 0:16], in0=pb[bk][0:16, :].rearrange("p (g c) -> p g c", g=4)[:, :, 0:16], in1=Dd[:, 4 * blk:4 * blk + 4, :], op=ALU.add), r=[tb[bk], t0], w=[t0])
            t_ks = P.tok("kscr")
            DS(lambda e: e.dma_start(out=kscr[:, :, 7:15, :].rearrange("g h d e -> h g (d e)"), in_=Kflat[:, :, :]), r=[t0], w=[t_ks])
            DS(lambda e: e.dma_start(out=kscr[:, :, 0:7, :].rearrange("g h d e -> h g (d e)"), in_=Z0[:, :, :]), r=[t0], w=[t_ks])
            kflat_d = kscr.rearrange("g h d e -> h g (d e)")
            for j in range(8):
                DS(lambda e, j=j: e.dma_start(out=Toep[16 * j:16 * j + 16, :, :], in_=kflat_d[:, :, (7 - j) * 16:(7 - j) * 16 + 128]), r=[t_ks], w=[t_toep])
            barrier()

        if stage in ("all", "ssm", "p0"):
            _phase0()

        def load_gain_bc(dst, name, tok):
            DS(lambda e: e.dma_start(out=dst, in_=dr[name][0, :].partition_broadcast(128)), w=[tok])

        def _phase1():
            Wu = view(84 * KB, [16, 1024], BF16)
            hT = view(116 * KB, [16, 1024], BF16)
            xbuf = [view(148 * KB, [2048], F32), view(156 * KB, [2048], F32)]
            hb = [view(164 * KB, [2048], BF16), view(168 * KB, [2048], BF16)]
            junk = view(172 * KB, [2048], BF16)
            gmix = view(176 * KB, [2048], F32)
            ublk = view(184 * KB, [64, 8, 16], BF16)
            stat = sbt("statA", [128, 32], F32)
            t_Wu, t_hT, t_junk, t_gm, t_ublk, t_stat = P.toks(6, "pa")
            t_x = P.toks(2, "x"); t_hb = P.toks(2, "hb")
            load_gain_bc(gmix, "g_mix_norm", t_gm)
            DS(lambda e: e.dma_start(out=Wu, in_=wb["w_in"][:, 832:1856].rearrange("(c p) n -> p c n", p=128)), r=[t_wb["w_in"]], w=[t_Wu])
            for hf in range(2):
                for i in range(8):
                    ti = hf * 8 + i
                    b = ti % 2
                    DS(lambda e, ti=ti, b=b: e.dma_start(out=xbuf[b], in_=dr["x"][128 * ti:128 * ti + 128, :]), w=[t_x[b]])
                    A(lambda e, b=b, ti=ti: e.activation(out=junk, in_=xbuf[b], func=AF.Square, accum_out=stat[:, ti:ti + 1]), r=[t_x[b]], w=[t_junk, t_stat])
                    for f in rstd_act(stat[:, 16 + ti:17 + ti], stat[:, ti:ti + 1], 2048):
                        A(f, r=[t_stat], w=[t_stat])
                    V(lambda e, b=b, ti=ti: e.scalar_tensor_tensor(out=hb[b], in0=xbuf[b], scalar=stat[:, 16 + ti:17 + ti], in1=gmix, op0=ALU.mult, op1=ALU.mult), r=[t_x[b], t_stat, t_gm], w=[t_hb[b]])
                    bk = [0, 1, 2, 3][(2 * ti) % 4:(2 * ti) % 4 + 2]
                    for kc in range(16):
                        bb = bk[kc // 8]
                        T(lambda e, b=b, kc=kc, bb=bb: e.transpose(out=pbb[bb][:, 128 * (kc % 8):128 * (kc % 8) + 128], in_=hb[b][:, 128 * kc:128 * kc + 128], identity=ident_bf[:]), r=[t_hb[b], t_const], w=[tb[bb]])
                    srcs = [pbb[bk[q]][:, :].rearrange("p (k t) -> p k t", k=8) for q in range(2)]
                    dsts = [hT[:, 8 * q:8 * q + 8, 128 * i:128 * i + 128] for q in range(2)]
                    A(lambda e, s_=srcs[0], d_=dsts[0]: e.activation(out=d_, in_=s_, func=AF.Copy), r=[tb[bk[0]]], w=[t_hT])
                    V(lambda e, s_=srcs[1], d_=dsts[1]: e.tensor_copy(out=d_, in_=s_), r=[tb[bk[1]]], w=[t_hT])
                for tau in range(8):
                    for cc in range(2):
                        bk = 4 + (tau * 2 + cc) % 4
                        for kc in range(16):
                            T(lambda e, bk=bk, kc=kc, tau=tau, cc=cc: e.matmul(pb[bk][:, :], lhsT=hT[:, kc, tau:1024:8], rhs=Wu[:, kc, 512 * cc:512 * cc + 512], start=(kc == 0), stop=(kc == 15)), r=[t_hT, t_Wu], w=[tb[bk]])
                        src = pb[bk][:, :].rearrange("p (g h) -> p g h", g=32)
                        dst = ublk[:, 32 * cc:32 * cc + 32, tau, :]
                        if cc == 0:
                            A(lambda e, src=src, dst=dst: e.activation(out=dst, in_=src, func=AF.Copy), r=[tb[bk]], w=[t_ublk])
                        else:
                            V(lambda e, src=src, dst=dst: e.tensor_copy(out=dst, in_=src), r=[tb[bk]], w=[t_ublk])
                for g8 in range(8):
                    bk = g8 % 2
                    for s in range(8):
                        g = g8 * 8 + s
                        T(lambda e, bk=bk, s=s, g=g: e.transpose(out=pbb[bk][:, 128 * s:128 * s + 128], in_=ublk[:, g, :, :].rearrange("p t h -> p (t h)"), identity=ident_bf[:]), r=[t_ublk, t_const], w=[tb[bk]])
                    src = pbb[bk][:, :].rearrange("p (g c) -> p g c", g=8)
                    dst = uT[:, 8 * g8:8 * g8 + 8, 128 * hf:128 * hf + 128]
                    if g8 % 2 == 0:
                        A(lambda e, src=src, dst=dst: e.activation(out=dst, in_=src, func=AF.Copy), r=[tb[bk]], w=[t_uT])
                    else:
                        V(lambda e, src=src, dst=dst: e.tensor_copy(out=dst, in_=src), r=[tb[bk]], w=[t_uT])
            barrier()

        if stage in ("all", "ssm", "pa"):
            _phase1()

        t_dbg = P.tok("dbg")

        def dump(src, col0, ncols, parts=128, rd=(), dst=None):
            d_ = dbg[0:parts, col0:col0 + ncols] if dst is None else dst
            DS(lambda e: e.dma_start(out=d_, in_=src, allow_slow_non_contiguous=True), r=list(rd), w=[t_dbg])

        osT = view(0, [8, 2048], BF16)
        t_osT = P.tok("osT")
        def _phase2():
            dS = view(84 * KB, [32, 2, 256], F32)
            tmp1 = view(148 * KB, [32, 2], F32)
            tmp2 = view(148 * KB + 256, [32, 2], F32)
            Sb = view(149 * KB, [32, 2, 257], BF16)
            t_dS, t_tmp1, t_tmp2, t_Sb = P.toks(4, "ps")
            for gg in range(32):
                bk = gg % 4
                for gh in range(2):
                    g = 32 * gh + gg
                    for ri in range(2):
                        T(lambda e, bk=bk, gh=gh, g=g, ri=ri: e.matmul(pb[bk][64 * gh:64 * gh + 64, 256 * ri:256 * ri + 256], lhsT=MT[:, g, ri, :], rhs=uT[:, g, :], start=True, stop=True), r=[t_mt, t_uT], w=[tb[bk]])
                if gg % 2 == 0:
                    V(lambda e, bk=bk, gg=gg: e.tensor_copy(out=dS[:, gg, :, :].rearrange("p r c -> p (r c)"), in_=pb[bk][:, :]), r=[tb[bk]], w=[t_dS])
                else:
                    A(lambda e, bk=bk, gg=gg: e.activation(out=dS[:, gg, :, :].rearrange("p r c -> p (r c)"), in_=pb[bk][:, :], func=AF.Copy), r=[tb[bk]], w=[t_dS])
            for c in range(1, 256):
                V(lambda e, c=c: e.tensor_tensor(out=tmp1, in0=dS[:, :, :, c - 1], in1=A8dup, op=ALU.mult), r=[t_dS, t_a8], w=[t_tmp1])
                V(lambda e, c=c: e.tensor_tensor(out=tmp2, in0=dS[:, :, ::-1, c - 1], in1=A8sw, op=ALU.mult), r=[t_dS, t_a8], w=[t_tmp2])
                V(lambda e, c=c: e.tensor_tensor(out=dS[:, :, :, c], in0=dS[:, :, :, c], in1=tmp1, op=ALU.add), r=[t_tmp1, t_dS], w=[t_dS])
                V(lambda e, c=c: e.tensor_tensor(out=dS[:, :, :, c], in0=dS[:, :, :, c], in1=tmp2, op=ALU.add), r=[t_tmp2, t_dS], w=[t_dS])
            V(lambda e: e.memset(Sb[:, :, :, 0:1], 0.0), w=[t_Sb])
            V(lambda e: e.tensor_copy(out=Sb[:, 0:16, :, 1:257], in_=dS[:, 0:16, :, :]), r=[t_dS], w=[t_Sb])
            A(lambda e: e.activation(out=Sb[:, 16:32, :, 1:257], in_=dS[:, 16:32, :, :], func=AF.Copy), r=[t_dS], w=[t_Sb])
            if stage == "ssm":
                dump(dS[:, :, :, 7::8], 8192, 2048, rd=[t_dS], dst=dbg[:, 8192:10240].rearrange("p (g r c) -> p g r c", g=32, r=2))
            barrier()
            gB = view(84 * KB, [2, 8, 1024], BF16)
            gT = view(116 * KB, [8, 2048], BF16)
            sqt = view(182 * KB, [512], F32)
            wv = view(184 * KB, [512], F32)
            sg = view(186 * KB, [512], F32)
            t_gB, t_gT, t_sqt, t_wv, t_sg = P.toks(5, "ps2")
            for ct in range(2):
                for g4 in range(16):
                    bk = g4 % 4
                    for s in range(4):
                        g = 4 * g4 + s
                        gh, gg = g // 32, g % 32
                        prt = slice(64 * gh, 64 * gh + 64)
                        o_ = pb[bk][:, 128 * s:128 * s + 128]
                        T(lambda e, o_=o_, g=g, ct=ct: e.matmul(o_, lhsT=uT[:, g, 128 * ct:128 * ct + 128], rhs=Toep[:, g, :], start=True, stop=False), r=[t_uT, t_toep], w=[tb[bk]])
                        for ri in range(2):
                            T(lambda e, o_=o_, prt=prt, gg=gg, ri=ri, ct=ct: e.matmul(o_, lhsT=Sb[prt, gg, ri, 128 * ct:128 * ct + 128], rhs=CAall[prt, gg, ri, 1:9, :].rearrange("p d h -> p (d h)"), start=False, stop=(ri == 1)), r=[t_Sb, t_ca], w=[tb[bk]])
                    A(lambda e, bk=bk: e.activation(out=sqt, in_=pb[bk][:, :], func=AF.Square), r=[tb[bk]], w=[t_sqt])
                    V(lambda e: e.tensor_scalar(out=wv, in0=sqt, scalar1=0.044715, scalar2=1.0, op0=ALU.mult, op1=ALU.add), r=[t_sqt], w=[t_wv])
                    V(lambda e, bk=bk: e.tensor_tensor(out=wv, in0=wv, in1=pb[bk][:, :], op=ALU.mult), r=[t_wv, tb[bk]], w=[t_wv])
                    A(lambda e: e.activation(out=sg, in_=wv, func=AF.Sigmoid, scale=1.5957691216057308), r=[t_wv], w=[t_sg])
                    dst = gB[:, ct, :, 64 * g4:64 * g4 + 64].rearrange("p t (g h) -> p g t h", g=4)
                    V(lambda e, bk=bk, dst=dst: e.tensor_tensor(out=dst, in0=sg.rearrange("p (g t h) -> p g t h", g=4, t=8), in1=pb[bk][:, :].rearrange("p (g t h) -> p g t h", g=4, t=8), op=ALU.mult), r=[t_sg, tb[bk]], w=[t_gB])
            for ct in range(2):
                for tau in range(8):
                    bk = 4 + (ct * 8 + tau) % 4
                    for kc in range(8):
                        T(lambda e, bk=bk, kc=kc, ct=ct, tau=tau: e.transpose(out=pbb[bk][:, 128 * kc:128 * kc + 128], in_=gB[:, ct, tau, 128 * kc:128 * kc + 128], identity=ident_bf[:]), r=[t_gB, t_const], w=[tb[bk]])
                    src = pbb[bk][:, :].rearrange("p (k c) -> p k c", k=8)
                    dst = gT[:, :, 1024 * ct + tau:1024 * ct + 1024:8]
                    if tau % 2 == 0:
                        V(lambda e, src=src, dst=dst: e.tensor_copy(out=dst, in_=src), r=[tb[bk]], w=[t_gT])
                    else:
                        A(lambda e, src=src, dst=dst: e.activation(out=dst, in_=src, func=AF.Copy), r=[tb[bk]], w=[t_gT])
            if stage == "ssm":
                gdump = view(149 * KB, [2048], F32)
                t_gd = P.tok("gd")
                V(lambda e: e.tensor_copy(out=gdump, in_=gT[:, 3, :]), r=[t_gT], w=[t_gd])
                dump(gdump, 2048, 2048, rd=[t_gd])
            barrier()
            Wglu = view(148 * KB, [8, 1024], BF16)
            sqs = view(164 * KB, [8, 512], BF16)
            sgb = view(172 * KB, [512], BF16)
            rb = view(173 * KB, [512], BF16)
            rbf = view(174 * KB, [512], F32)
            t_Wglu, t_sqs, t_sgb, t_rb = P.toks(4, "ps3")
            DS(lambda e: e.dma_start(out=Wglu, in_=wb["w_glu"].rearrange("(c p) n -> p c n", p=128)), r=[t_wb["w_glu"]], w=[t_Wglu])
            for tq in range(4):
                tsl = slice(512 * tq, 512 * tq + 512)
                for oc in range(8):
                    bk = oc % 4
                    for kc in range(8):
                        T(lambda e, bk=bk, kc=kc, oc=oc, tsl=tsl: e.matmul(pb[bk][:, :], lhsT=Wglu[:, kc, 128 * oc:128 * oc + 128], rhs=gT[:, kc, tsl], start=(kc == 0), stop=(kc == 7)), r=[t_Wglu, t_gT], w=[tb[bk]])
                    A(lambda e, bk=bk, oc=oc: e.activation(out=sgb, in_=pb[bk][:, :], func=AF.Sigmoid, bias=colv[:, 0, oc:oc + 1]), r=[tb[bk], t_col], w=[t_sgb])
                    V(lambda e, oc=oc, tsl=tsl: e.scalar_tensor_tensor(out=osT[:, oc, tsl], in0=gT[:, oc, tsl], scalar=colv[:, 1, oc:oc + 1], in1=sgb, op0=ALU.mult, op1=ALU.mult), r=[t_gT, t_col, t_sgb], w=[t_osT])
                    A(lambda e, oc=oc, tsl=tsl: e.activation(out=sqs[:, oc, :], in_=osT[:, oc, tsl], func=AF.Square, scale=colv[:, 2, oc:oc + 1]), r=[t_osT, t_col], w=[t_sqs])
                bq = 4 + tq % 2
                for oc in range(8):
                    T(lambda e, bq=bq, oc=oc: e.matmul(pb[bq][:, :], lhsT=ones_bf[:], rhs=sqs[:, oc, :], start=(oc == 0), stop=(oc == 7)), r=[t_sqs, t_const], w=[tb[bq]])
                A(lambda e, bq=bq: e.activation(out=rbf, in_=pb[bq][:, :], func=AF.Ln, scale=1.0 / 1024, bias=EPS), r=[tb[bq]], w=[t_rb])
                A(lambda e: e.activation(out=rbf, in_=rbf, func=AF.Exp, scale=-0.5), r=[t_rb], w=[t_rb])
                for oc in range(8):
                    V(lambda e, oc=oc, tsl=tsl: e.tensor_tensor(out=osT[:, oc, tsl], in0=osT[:, oc, tsl], in1=rbf, op=ALU.mult), r=[t_rb, t_osT], w=[t_osT])
            if stage == "ssm":
                od = view(84 * KB, [2048], F32)
                t_od = P.tok("od")
                V(lambda e: e.tensor_copy(out=od, in_=osT[:, 5, :]), r=[t_osT], w=[t_od])
                dump(od, 0, 2048, rd=[t_od])
            barrier()


        if stage in ("all", "ssm"):
            _phase2()

        cqT = view(32 * KB, [4, 2048], BF16)
        ckvT = view(48 * KB, [2, 2048], BF16)
        Rk = view(56 * KB, [16, 64], F32)
        CS2 = view(72 * KB, [16, 2, 32], F32)
        SN2 = view(76 * KB, [16, 2, 32], F32)
        oaT = view(80 * KB, [8, 2048], BF16)
        t_cqT, t_ckvT, t_Rk, t_trig, t_oaT = P.toks(5, "pb")
        statB = sbt("statB", [128, 16, 8], F32)
        t_statB = P.tok("statB")
        gsm = sbt("gsm", [128, 1152], F32)
        t_gsm = P.tok("gsm")
        statC = sbt("statC", [128, 16, 8], F32)
        ssqA = sbt("ssqA", [128, 16, 8], F32)
        t_statC, t_ssqA = P.toks(2, "pc2")
        def _phase3():
            Wq = view(80 * KB, [16, 832], BF16)
            xbuf = [view(112 * KB, [2048], F32), view(120 * KB, [2048], F32)]
            hb = [view(128 * KB, [2048], BF16), view(132 * KB, [2048], BF16)]
            junk = view(136 * KB, [2048], BF16)
            gmix = view(140 * KB, [2048], F32)
            hTt = [view(148 * KB, [16, 128], BF16), view(152 * KB, [16, 128], BF16)]
            cqb = view(156 * KB, [512], BF16)
            ckvb = view(157 * KB, [256], BF16)
            krg = view(158 * KB, [2, 32], F32)
            kA = view(158 * KB + 256, [2, 32], F32)
            kB_ = view(158 * KB + 512, [2, 32], F32)
            posi = view(159 * KB, [16], I32)
            posf = view(159 * KB + 64, [16], F32)
            invf = view(159 * KB + 128, [32], F32)
            ANG = view(160 * KB, [16, 32], F32)
            AN2 = view(162 * KB, [16, 32], F32)
            AN3 = view(164 * KB, [16, 32], F32)
            ANI = view(166 * KB, [16, 32], I32)
            t_Wq, t_junk, t_gm, t_cqb, t_ckvb, t_krg, t_ang = P.toks(7, "pb2")
            t_x = P.toks(2, "x"); t_hb = P.toks(2, "hb"); t_hTt = P.toks(2, "hTt")
            load_gain_bc(gmix, "g_mix_norm", t_gm)
            for nm, a, b_ in (("g_q_lora", 0, 512), ("g_kv_lora", 512, 768), ("g_q_head", 768, 960), ("g_k_head", 960, 1152)):
                DS(lambda e, nm=nm, a=a, b_=b_: e.dma_start(out=gsm[:, a:b_], in_=dr[nm][0, :].partition_broadcast(128)), w=[t_gsm])
            V(lambda e: e.tensor_scalar(out=gsm[:, 768:960], in0=gsm[:, 768:960], scalar1=192 ** -0.5, scalar2=None, op0=ALU.mult), r=[t_gsm], w=[t_gsm])
            DS(lambda e: e.dma_start(out=Wq, in_=wb["w_in"][:, 0:832].rearrange("(c p) n -> p c n", p=128)), r=[t_wb["w_in"]], w=[t_Wq])
            DS(lambda e: e.dma_start(out=posi, in_=dr["pos"][0, :].rearrange("(t p) -> p t", p=128), allow_slow_non_contiguous=True), w=[t_ang])
            V(lambda e: e.tensor_copy(out=posf, in_=posi), r=[t_ang], w=[t_ang])
            for i in range(32):
                val = float(np.float32(1.0) / np.float32(np.float32(10000.0) ** np.float32(np.float32(2 * i) / np.float32(64.0))))
                G(lambda e, i=i, val=val: e.memset(invf[:, i:i + 1], val), w=[t_ang])
            V(lambda e: e.tensor_tensor(out=ANG, in0=posf.unsqueeze(2).to_broadcast([128, 16, 32]), in1=invf.unsqueeze(1).to_broadcast([128, 16, 32]), op=ALU.mult), r=[t_ang], w=[t_ang])

            def trig2(dst, shift, sgn):
                V(lambda e: e.tensor_scalar(out=AN2, in0=ANG, scalar1=float(shift), scalar2=1.0 / (2 * math.pi), op0=ALU.add, op1=ALU.mult), r=[t_ang], w=[t_ang])
                V(lambda e: e.tensor_copy(out=ANI, in_=AN2), r=[t_ang], w=[t_ang])
                V(lambda e: e.tensor_copy(out=AN3, in_=ANI), r=[t_ang], w=[t_ang])
                V(lambda e: e.tensor_scalar(out=AN2, in0=ANG, scalar1=float(shift), scalar2=None, op0=ALU.add), r=[t_ang], w=[t_ang])
                V(lambda e: e.scalar_tensor_tensor(out=AN2, in0=AN3, scalar=-2 * math.pi, in1=AN2, op0=ALU.mult, op1=ALU.add), r=[t_ang], w=[t_ang])
                V(lambda e: e.tensor_scalar(out=AN2, in0=AN2, scalar1=-math.pi, scalar2=math.pi, op0=ALU.max, op1=ALU.min), r=[t_ang], w=[t_ang])
                A(lambda e: e.activation(out=dst, in_=AN2, func=AF.Sin, scale=float(sgn)), r=[t_ang], w=[t_trig])

            trig2(CS2[:, :, 0, :], math.pi / 2, 1.0)
            trig2(CS2[:, :, 1, :], math.pi / 2, 1.0)
            trig2(SN2[:, :, 0, :], 0.0, -1.0)
            trig2(SN2[:, :, 1, :], 0.0, 1.0)

            for ti in range(16):
                b = ti % 2
                tsl = slice(128 * ti, 128 * ti + 128)
                DS(lambda e, ti=ti, b=b: e.dma_start(out=xbuf[b], in_=dr["x"][128 * ti:128 * ti + 128, :]), w=[t_x[b]])
                A(lambda e, b=b, ti=ti: e.activation(out=junk, in_=xbuf[b], func=AF.Square, accum_out=statB[:, ti, 0:1]), r=[t_x[b]], w=[t_junk, t_statB])
                for f in rstd_act(statB[:, ti, 1:2], statB[:, ti, 0:1], 2048):
                    A(f, r=[t_statB], w=[t_statB])
                V(lambda e, b=b, ti=ti: e.scalar_tensor_tensor(out=hb[b], in0=xbuf[b], scalar=statB[:, ti, 1:2], in1=gmix, op0=ALU.mult, op1=ALU.mult), r=[t_x[b], t_statB, t_gm], w=[t_hb[b]])
                bk = [0, 1] if ti % 2 == 0 else [2, 3]
                for kc in range(16):
                    bb = bk[kc // 8]
                    T(lambda e, b=b, kc=kc, bb=bb: e.transpose(out=pbb[bb][:, 128 * (kc % 8):128 * (kc % 8) + 128], in_=hb[b][:, 128 * kc:128 * kc + 128], identity=ident_bf[:]), r=[t_hb[b], t_const], w=[tb[bb]])
                A(lambda e, b=b, bb=bk[0]: e.activation(out=hTt[b][:, 0:8, :], in_=pbb[bb][:, :].rearrange("p (k t) -> p k t", k=8), func=AF.Copy), r=[tb[bk[0]]], w=[t_hTt[b]])
                V(lambda e, b=b, bb=bk[1]: e.tensor_copy(out=hTt[b][:, 8:16, :], in_=pbb[bb][:, :].rearrange("p (k t) -> p k t", k=8)), r=[tb[bk[1]]], w=[t_hTt[b]])
                bq, bkv = (4, 5) if ti % 2 == 0 else (6, 7)
                for kc in range(16):
                    T(lambda e, b=b, kc=kc, bq=bq: e.matmul(pb[bq][:, :], lhsT=hTt[b][:, kc, :], rhs=Wq[:, kc, 0:512], start=(kc == 0), stop=(kc == 15)), r=[t_hTt[b], t_Wq], w=[tb[bq]])
                for kc in range(16):
                    T(lambda e, b=b, kc=kc, bkv=bkv: e.matmul(pb[bkv][:, 0:320], lhsT=hTt[b][:, kc, :], rhs=Wq[:, kc, 512:832], start=(kc == 0), stop=(kc == 15)), r=[t_hTt[b], t_Wq], w=[tb[bkv]])
                A(lambda e, bq=bq, ti=ti: e.activation(out=junk[:, 0:512], in_=pb[bq][:, :], func=AF.Square, accum_out=statB[:, ti, 2:3]), r=[tb[bq]], w=[t_junk, t_statB])
                for f in rstd_act(statB[:, ti, 3:4], statB[:, ti, 2:3], 512):
                    A(f, r=[t_statB], w=[t_statB])
                V(lambda e, bq=bq, ti=ti: e.scalar_tensor_tensor(out=cqb, in0=pb[bq][:, :], scalar=statB[:, ti, 3:4], in1=gsm[:, 0:512], op0=ALU.mult, op1=ALU.mult), r=[tb[bq], t_statB, t_gsm], w=[t_cqb])
                A(lambda e, bkv=bkv, ti=ti: e.activation(out=junk[:, 0:256], in_=pb[bkv][:, 0:256], func=AF.Square, accum_out=statB[:, ti, 4:5]), r=[tb[bkv]], w=[t_junk, t_statB])
                for f in rstd_act(statB[:, ti, 5:6], statB[:, ti, 4:5], 256):
                    A(f, r=[t_statB], w=[t_statB])
                V(lambda e, bkv=bkv, ti=ti: e.scalar_tensor_tensor(out=ckvb, in0=pb[bkv][:, 0:256], scalar=statB[:, ti, 5:6], in1=gsm[:, 512:768], op0=ALU.mult, op1=ALU.mult), r=[tb[bkv], t_statB, t_gsm], w=[t_ckvb])
                A(lambda e, bkv=bkv, ti=ti: e.activation(out=junk[:, 0:64], in_=pb[bkv][:, 256:320], func=AF.Square, accum_out=statB[:, ti, 6:7]), r=[tb[bkv]], w=[t_junk, t_statB])
                V(lambda e, ti=ti: e.tensor_scalar(out=statB[:, ti, 7:8], in0=statB[:, ti, 6:7], scalar1=1.0 / 192, scalar2=EPS, op0=ALU.mult, op1=ALU.add), r=[t_statB], w=[t_statB])
                V(lambda e, bkv=bkv: e.tensor_tensor(out=krg.rearrange("p a b -> p (a b)"), in0=pb[bkv][:, 256:320], in1=gsm[:, 1088:1152], op=ALU.mult), r=[tb[bkv], t_gsm], w=[t_krg])
                V(lambda e, ti=ti: e.tensor_tensor(out=kA, in0=krg, in1=CS2[:, ti, :, :], op=ALU.mult), r=[t_krg, t_trig], w=[t_krg])
                V(lambda e, ti=ti: e.tensor_tensor(out=kB_, in0=krg[:, ::-1, :], in1=SN2[:, ti, :, :], op=ALU.mult), r=[t_krg, t_trig], w=[t_krg])
                V(lambda e, ti=ti: e.tensor_tensor(out=Rk[:, ti, :].rearrange("p (a b) -> p a b", a=2), in0=kA, in1=kB_, op=ALU.add), r=[t_krg], w=[t_Rk])
                for kc in range(4):
                    T(lambda e, kc=kc, bq=bq: e.transpose(out=pbb[bq][:, 128 * kc:128 * kc + 128], in_=cqb[:, 128 * kc:128 * kc + 128], identity=ident_bf[:]), r=[t_cqb, t_const], w=[tb[bq]])
                for kc in range(2):
                    T(lambda e, kc=kc, bq=bq: e.transpose(out=pbb[bq][:, 512 + 128 * kc:512 + 128 * kc + 128], in_=ckvb[:, 128 * kc:128 * kc + 128], identity=ident_bf[:]), r=[t_ckvb, t_const], w=[tb[bq]])
                A(lambda e, bq=bq, tsl=tsl: e.activation(out=cqT[:, :, tsl], in_=pbb[bq][:, 0:512].rearrange("p (k t) -> p k t", k=4), func=AF.Copy), r=[tb[bq]], w=[t_cqT])
                V(lambda e, bq=bq, tsl=tsl: e.tensor_copy(out=ckvT[:, :, tsl], in_=pbb[bq][:, 512:768].rearrange("p (k t) -> p k t", k=2)), r=[tb[bq]], w=[t_ckvT])
            if stage == "pb":
                od = view(170 * KB, [2048], F32)
                t_od = P.tok("od")
                V(lambda e: e.tensor_copy(out=od, in_=cqT[:, 1, :]), r=[t_cqT], w=[t_od])
                dump(od, 0, 2048, rd=[t_od])
                od2 = view(178 * KB, [2048], F32)
                V(lambda e: e.tensor_copy(out=od2, in_=ckvT[:, 1, :]), r=[t_ckvT], w=[t_od])
                dump(od2, 2048, 2048, rd=[t_od])
                dump(Rk.rearrange("p a b -> p (a b)"), 4096, 1024, rd=[t_Rk])
                dump(CS2.rearrange("p a b c -> p (a b c)"), 5120, 1024, rd=[t_trig])
                dump(SN2.rearrange("p a b c -> p (a b c)"), 6144, 1024, rd=[t_trig])
            barrier()

        if stage in ("all", "attn", "pb"):
            _phase3()

        def _phase4():
            Wuq = view(112 * KB, [4, 1536], BF16)
            Wukv = view(124 * KB, [2, 2048], BF16)
            t_Wuq, t_Wukv = P.toks(2, "pcw")
            DS(lambda e: e.dma_start(out=Wuq, in_=wb["w_uq"].rearrange("(c p) n -> p c n", p=128)), r=[t_wb["w_uq"]], w=[t_Wuq])
            DS(lambda e: e.dma_start(out=Wukv, in_=wb["w_ukv"].rearrange("(c p) n -> p c n", p=128)), r=[t_wb["w_ukv"]], w=[t_Wukv])
            hbuf = []
            for s_ in range(2):
                o0 = (132 + 21 * s_) * KB
                hbuf.append(dict(
                    QK=view(o0, [2, 2048], BF16), QKr=view(o0 + 8 * KB, [2, 2048], BF16),
                    Vh=view(o0 + 16 * KB, [16, 130], BF16), t=P.toks(3, "hb%d" % s_)))
            Pt = [view((174 + i) * KB, [512], BF16) for i in range(3)]
            t_Pt = P.toks(3, "Pt")
            qr = [view(177 * KB + 768 * i, [2, 32], F32) for i in range(2)]
            qA = [view(177 * KB + 768 * i + 256, [2, 32], F32) for i in range(2)]
            qB = [view(177 * KB + 768 * i + 512, [2, 32], F32) for i in range(2)]
            qb_ = [view(179 * KB, [256], BF16), view(179 * KB + 512, [256], BF16)]
            kb_ = [view(180 * KB, [256], BF16), view(180 * KB + 512, [256], BF16)]
            oh = [view(181 * KB, [128], BF16), view(181 * KB + 256, [128], BF16)]
            junkp = view(182 * KB, [256], BF16)
            junkc = view(183 * KB, [128], BF16)
            t_junkp, t_junkc = P.toks(2, "pcj")
            t_qr = P.toks(2, "qr")
            t_qb = P.toks(2, "qb"); t_kb = P.toks(2, "kb"); t_oh = P.toks(2, "oh")
            statE = sbt("statE", [128, 16, 2], F32)
            t_statE = P.tok("statE")
            tS = P.toks(2, "statCt")
            V(lambda e: e.memset(ssqA[:, :, :], 0.0), w=[t_ssqA])
            for s_ in range(2):
                V(lambda e, s_=s_: e.memset(hbuf[s_]["Vh"][:, :, 128:130], 1.0), w=[hbuf[s_]["t"][2]])
                V(lambda e, s_=s_: e.memset(qb_[s_][:, 192:256], 0.0), w=[t_qb[s_]])
                V(lambda e, s_=s_: e.memset(kb_[s_][:, 192:256], 0.0), w=[t_kb[s_]])

            def proj_tile(h, ti):
                HB = hbuf[h % 2]
                tQK, tQKr, tVh = HB["t"]
                tsl = slice(128 * ti, 128 * ti + 128)
                bk = 6 + ti % 2
                pr = ti % 2
                tSt = tS[pr]
                for kc in range(4):
                    T(lambda e, kc=kc: e.matmul(pb[bk][:, 0:192], lhsT=cqT[:, kc, tsl], rhs=Wuq[:, kc, 192 * h:192 * h + 192], start=(kc == 0), stop=(kc == 3)), r=[t_cqT, t_Wuq], w=[tb[bk]])
                for kc in range(2):
                    T(lambda e, kc=kc: e.matmul(pb[bk][:, 192:448], lhsT=ckvT[:, kc, tsl], rhs=Wukv[:, kc, 256 * h:256 * h + 256], start=(kc == 0), stop=(kc == 1)), r=[t_ckvT, t_Wukv], w=[tb[bk]])
                A(lambda e: e.activation(out=junkp[:, 0:192], in_=pb[bk][:, 0:192], func=AF.Square, accum_out=statC[:, ti, 0:1]), r=[tb[bk]], w=[t_junkp, tSt])
                A(lambda e: e.activation(out=junkp[:, 0:128], in_=pb[bk][:, 192:320], func=AF.Square, accum_out=statC[:, ti, 1:2]), r=[tb[bk]], w=[t_junkp, tSt])
                A(lambda e: e.activation(out=statC[:, ti, 2:3], in_=statC[:, ti, 0:1], func=AF.Ln, scale=1.0 / 192, bias=EPS), r=[tSt], w=[tSt])
                A(lambda e: e.activation(out=statC[:, ti, 3:4], in_=statC[:, ti, 1:2], func=AF.Ln, scale=1.0 / 192, bias=statB[:, ti, 7:8]), r=[tSt, t_statB], w=[tSt])
                A(lambda e: e.activation(out=statC[:, ti, 2:4], in_=statC[:, ti, 2:4], func=AF.Exp, scale=-0.5), r=[tSt], w=[tSt])
                V(lambda e: e.scalar_tensor_tensor(out=qb_[pr][:, 0:128], in0=pb[bk][:, 0:128], scalar=statC[:, ti, 2:3], in1=gsm[:, 768:896], op0=ALU.mult, op1=ALU.mult), r=[tb[bk], tSt, t_gsm], w=[t_qb[pr]])
                V(lambda e: e.scalar_tensor_tensor(out=qr[pr].rearrange("p a b -> p (a b)"), in0=pb[bk][:, 128:192], scalar=statC[:, ti, 2:3], in1=gsm[:, 896:960], op0=ALU.mult, op1=ALU.mult), r=[tb[bk], tSt, t_gsm], w=[t_qr[pr]])
                V(lambda e: e.tensor_tensor(out=qA[pr], in0=qr[pr], in1=CS2[:, ti, :, :], op=ALU.mult), r=[t_qr[pr], t_trig], w=[t_qr[pr]])
                V(lambda e: e.tensor_tensor(out=qB[pr], in0=qr[pr][:, ::-1, :], in1=SN2[:, ti, :, :], op=ALU.mult), r=[t_qr[pr], t_trig], w=[t_qr[pr]])
                V(lambda e: e.tensor_tensor(out=qb_[pr][:, 128:192].rearrange("p (a b) -> p a b", a=2), in0=qA[pr], in1=qB[pr], op=ALU.add), r=[t_qr[pr]], w=[t_qb[pr]])
                V(lambda e: e.scalar_tensor_tensor(out=kb_[pr][:, 0:128], in0=pb[bk][:, 192:320], scalar=statC[:, ti, 3:4], in1=gsm[:, 960:1088], op0=ALU.mult, op1=ALU.mult), r=[tb[bk], tSt, t_gsm], w=[t_kb[pr]])
                A(lambda e: e.activation(out=kb_[pr][:, 128:192], in_=Rk[:, ti, :], func=AF.Copy, scale=statC[:, ti, 3:4]), r=[t_Rk, tSt], w=[t_kb[pr]])
                A(lambda e: e.activation(out=HB["Vh"][:, ti, 0:128], in_=pb[bk][:, 320:448], func=AF.Copy), r=[tb[bk]], w=[tVh])

            def proj_tile2(h, ti):
                HB = hbuf[h % 2]
                tQK, tQKr, tVh = HB["t"]
                tsl = slice(128 * ti, 128 * ti + 128)
                bk = 6 + ti % 2
                pr = ti % 2
                T(lambda e: e.transpose(out=pbb[bk][:, 0:128], in_=qb_[pr][:, 0:128], identity=ident_bf[:]), r=[t_qb[pr], t_const], w=[tb[bk]])
                T(lambda e: e.transpose(out=pbb[bk][:, 128:256], in_=kb_[pr][:, 0:128], identity=ident_bf[:]), r=[t_kb[pr], t_const], w=[tb[bk]])
                T(lambda e: e.transpose(out=pbb[bk][:, 256:384], in_=qb_[pr][:, 128:256], identity=ident_bf[:]), r=[t_qb[pr], t_const], w=[tb[bk]])
                T(lambda e: e.transpose(out=pbb[bk][:, 384:512], in_=kb_[pr][:, 128:256], identity=ident_bf[:]), r=[t_kb[pr], t_const], w=[tb[bk]])
                A(lambda e: e.activation(out=HB["QK"][:, :, tsl], in_=pbb[bk][:, 0:256].rearrange("p (a t) -> p a t", a=2), func=AF.Copy), r=[tb[bk]], w=[tQK])
                A(lambda e: e.activation(out=HB["QKr"][:, :, tsl], in_=pbb[bk][:, 256:512].rearrange("p (a t) -> p a t", a=2), func=AF.Copy), r=[tb[bk]], w=[tQKr])

            def head_iters():
                its = []
                for qsb in range(4):
                    for kb in range(4 * qsb + 4):
                        its.append((qsb, kb))
                return its

            ITS = head_iters()

            def score_S(h, i):
                HB = hbuf[h % 2]
                tQK, tQKr, tVh = HB["t"]
                qsb, kb = ITS[i]
                q0 = max(512 * qsb, 128 * kb)
                q1 = 512 * qsb + 512
                nq = q1 - q0
                sb_ = 4 + (i % 2)
                pt = i % 3
                ksl = slice(128 * kb, 128 * kb + 128)
                T(lambda e: e.matmul(pb[sb_][:, 0:nq], lhsT=HB["QK"][:, 1, ksl], rhs=HB["QK"][:, 0, q0:q1], start=True, stop=False), r=[tQK], w=[tb[sb_]])
                T(lambda e: e.matmul(pb[sb_][:, 0:nq], lhsT=HB["QKr"][0:64, 1, ksl], rhs=HB["QKr"][0:64, 0, q0:q1], start=False, stop=True), r=[tQKr], w=[tb[sb_]])
                A(lambda e: e.activation(out=Pt[pt][:, 0:nq], in_=pb[sb_][:, 0:nq], func=AF.Exp), r=[tb[sb_]], w=[t_Pt[pt]])
                if 128 * kb >= 512 * qsb:
                    V(lambda e: e.tensor_tensor(out=Pt[pt][:, 0:128], in0=Pt[pt][:, 0:128], in1=mask_bf[:], op=ALU.mult), r=[t_Pt[pt], t_const], w=[t_Pt[pt]])

            def score_PV(h, i):
                HB = hbuf[h % 2]
                tQK, tQKr, tVh = HB["t"]
                qsb, kb = ITS[i]
                q0 = max(512 * qsb, 128 * kb)
                q1 = 512 * qsb + 512
                nq = q1 - q0
                pt = i % 3
                for j in range(nq // 128):
                    qi = (q0 + 128 * j) // 128
                    ob = qi % 4
                    T(lambda e, ob=ob, j=j, qi=qi: e.matmul(pb[ob][:, 0:130], lhsT=Pt[pt][:, 128 * j:128 * j + 128], rhs=HB["Vh"][:, kb, 0:130], start=(kb == 0), stop=(kb == qi)), r=[t_Pt[pt], tVh], w=[tb[ob]])
                if kb == 4 * qsb + 3:
                    for j in range(4):
                        qi = 4 * qsb + j
                        ob = qi % 4
                        pr = qi % 2
                        tsl = slice(128 * qi, 128 * qi + 128)
                        V(lambda e, ob=ob, qi=qi: e.reciprocal(out=statE[:, qi, 0:1], in_=pb[ob][:, 128:129]), r=[tb[ob]], w=[t_statE])
                        A(lambda e, ob=ob, qi=qi, pr=pr: e.activation(out=oh[pr], in_=pb[ob][:, 0:128], func=AF.Copy, scale=statE[:, qi, 0:1]), r=[tb[ob], t_statE], w=[t_oh[pr]])
                        A(lambda e, ob=ob, qi=qi: e.activation(out=junkc, in_=pb[ob][:, 0:128], func=AF.Square, scale=statE[:, qi, 0:1], accum_out=ssqA[:, qi, h:h + 1]), r=[tb[ob], t_statE], w=[t_junkc, t_ssqA])
                        tbk = 6 + qi % 2
                        T(lambda e, tbk=tbk, pr=pr: e.transpose(out=pbb[tbk][:, 512:640], in_=oh[pr], identity=ident_bf[:]), r=[t_oh[pr], t_const], w=[tb[tbk]])
                        A(lambda e, tbk=tbk, tsl=tsl: e.activation(out=oaT[:, h, tsl], in_=pbb[tbk][:, 512:640], func=AF.Copy, scale=colv[:, 3, h:h + 1]), r=[tb[tbk], t_col], w=[t_oaT])

            def proj_step(h, n):
                if n < 16:
                    proj_tile(h, n)
                if 1 <= n <= 16:
                    proj_tile2(h, n - 1)

            for n in range(17):
                proj_step(0, n)
            NI = len(ITS)
            for h in range(8):
                nxt = 0
                score_S(h, 0)
                for i in range(NI):
                    if i + 1 < NI:
                        score_S(h, i + 1)
                    score_PV(h, i)
                    if h + 1 < 8 and i % 2 == 1 and nxt < 17:
                        proj_step(h + 1, nxt)
                        nxt += 1
                while h + 1 < 8 and nxt < 17:
                    proj_step(h + 1, nxt)
                    nxt += 1
            V(lambda e: e.tensor_reduce(out=statC[:, :, 6:7], in_=ssqA[:, :, :], axis=AX.X, op=ALU.add), r=[t_ssqA, tS[0], tS[1]], w=[t_statC])
            A(lambda e: e.activation(out=statC[:, :, 7:8], in_=statC[:, :, 6:7], func=AF.Ln, scale=1.0 / 1024, bias=EPS), r=[t_statC], w=[t_statC])
            A(lambda e: e.activation(out=statC[:, :, 7:8], in_=statC[:, :, 7:8], func=AF.Exp, scale=-0.5), r=[t_statC], w=[t_statC])
            if stage == "attn":
                od = view(132 * KB, [2048], F32)
                t_od = P.tok("od")
                for ii, kc in enumerate((0, 5)):
                    V(lambda e, kc=kc: e.tensor_copy(out=od, in_=oaT[:, kc, :]), r=[t_oaT], w=[t_od])
                    dump(od, 2048 * ii, 2048, rd=[t_od])
                od2 = view(140 * KB, [2048], F32)
                V(lambda e: e.tensor_copy(out=od2, in_=cqT[:, 1, :]), r=[t_cqT], w=[t_od])
                dump(od2, 4096, 2048, rd=[t_od])
            barrier()
        if stage in ("all", "attn"):
            _phase4()

        t_out = P.tok("out")
        def _phase5():
            X = view(32 * KB, [4, 2048], F32)
            hT = view(64 * KB, [16, 512], BF16)
            actT = view(112 * KB, [11, 512], BF16)
            slots = [view((123 + 16 * i) * KB, [8192], BF16) for i in range(3)]
            t_slot = P.toks(3, "slot")
            gffn = view(171 * KB, [2048], F32)
            gple = view(179 * KB, [2048], F32)
            hb = view(187 * KB, [2048], BF16)
            junk = view(191 * KB, [2048], BF16)
            sg = [view(195 * KB, [512], F32), view(197 * KB, [512], F32)]
            pf = view(199 * KB, [256], F32)
            pT = view(160 * KB + 0, [2, 512], BF16)
            statD = sbt("statD", [128, 8], F32)
            t_X0, t_hT, t_act, t_gf, t_gp, t_hb, t_junk, t_pf, t_pT, t_statD = P.toks(10, "pd")
            t_X = P.toks(4, "X")
            t_sg = P.toks(2, "sg")
            pbf = gsm[:, 0:128].bitcast(BF16)
            pT = gsm[:, 128:640].bitcast(BF16).rearrange("p (k t) -> p k t", k=2)
            load_gain_bc(gffn, "g_ffn_norm", t_gf)
            load_gain_bc(gple, "g_ple_norm", t_gp)
            slot_i = [0]

            def next_slot():
                i = slot_i[0] % 3
                slot_i[0] += 1
                return i

            def super_tile(sI):
                t0_ = 512 * sI
                pend = []

                def tokr(tt):
                    return slice(t0_ + 128 * tt, t0_ + 128 * tt + 128)

                for tt in range(4):
                    DG(lambda e, tt=tt: e.dma_start(out=X[:, tt, :], in_=dr["x"][t0_ + 128 * tt:t0_ + 128 * tt + 128, :]), w=[t_X[tt]])

                def norm_to_hT(gb, t_g):
                    for tt in range(4):
                        A(lambda e, tt=tt: e.activation(out=junk, in_=X[:, tt, :], func=AF.Square, accum_out=statD[:, 0:1]), r=[t_X[tt]], w=[t_junk, t_statD])
                        for f in rstd_act(statD[:, 1:2], statD[:, 0:1], 2048):
                            A(f, r=[t_statD], w=[t_statD])
                        V(lambda e, tt=tt: e.scalar_tensor_tensor(out=hb, in0=X[:, tt, :], scalar=statD[:, 1:2], in1=gb, op0=ALU.mult, op1=ALU.mult), r=[t_X[tt], t_statD, t_g], w=[t_hb])
                        for kc in range(16):
                            bb = 6 + kc // 8
                            T(lambda e, kc=kc, bb=bb: e.transpose(out=pbb[bb][:, 128 * (kc % 8):128 * (kc % 8) + 128], in_=hb[:, 128 * kc:128 * kc + 128], identity=ident_bf[:]), r=[t_hb, t_const], w=[tb[bb]])
                        for q in range(2):
                            A(lambda e, q=q, tt=tt: e.activation(out=hT[:, 8 * q:8 * q + 8, 128 * tt:128 * tt + 128], in_=pbb[6 + q][:, :].rearrange("p (k t) -> p k t", k=8), func=AF.Copy), r=[tb[6 + q]], w=[t_hT])

                def mk_wo(cc):
                    def load(si):
                        DS(lambda e: e.dma_start(out=slots[si].rearrange("p (c n) -> p c n", c=16), in_=wb["w_o"][:, 512 * cc:512 * cc + 512].rearrange("(c p) n -> p c n", p=128)), r=[t_wb["w_o"]], w=[t_slot[si]])

                    def comp(si):
                        W = slots[si].rearrange("p (c n) -> p c n", c=16)
                        for tt in range(4):
                            ba, bs = tt % 2, 2 + tt % 2
                            for kc in range(8):
                                T(lambda e, ba=ba, kc=kc, tt=tt: e.matmul(pb[ba][:, :], lhsT=oaT[:, kc, tokr(tt)], rhs=W[:, kc, :], start=(kc == 0), stop=(kc == 7)), r=[t_oaT, t_slot[si]], w=[tb[ba]])
                            for kc in range(8):
                                T(lambda e, bs=bs, kc=kc, tt=tt: e.matmul(pb[bs][:, :], lhsT=osT[:, kc, tokr(tt)], rhs=W[:, 8 + kc, :], start=(kc == 0), stop=(kc == 7)), r=[t_osT, t_slot[si]], w=[tb[bs]])
                            xs = X[:, tt, 512 * cc:512 * cc + 512]
                            ti = 4 * sI + tt
                            V(lambda e, ba=ba, xs=xs, ti=ti: e.scalar_tensor_tensor(out=xs, in0=pb[ba][:, :], scalar=statC[:, ti, 7:8], in1=xs, op0=ALU.mult, op1=ALU.add), r=[tb[ba], t_statC, t_X[tt]], w=[t_X[tt]])
                            V(lambda e, bs=bs, xs=xs: e.tensor_tensor(out=xs, in0=xs, in1=pb[bs][:, :], op=ALU.add), r=[tb[bs], t_X[tt]], w=[t_X[tt]])
                    return load, comp

                for cc in range(4):
                    pend.append(mk_wo(cc))

                pend.append((None, lambda si: norm_to_hT(gffn, t_gf)))

                def mk_gu(grp, b2):
                    ffcs = [j for j in (2 * b2, 2 * b2 + 1) if j < 11]
                    c0 = (grp * 11 + ffcs[0]) * 128
                    ncol = 128 * len(ffcs)

                    def load(si):
                        Wv = slots[si].rearrange("p (w c n) -> p w c n", w=2, c=16)
                        DS(lambda e: e.dma_start(out=Wv[:, 0, :, 0:ncol], in_=wb["w_gate"][:, c0:c0 + ncol].rearrange("(c p) n -> p c n", p=128)), r=[t_wb["w_gate"]], w=[t_slot[si]])
                        DS(lambda e: e.dma_start(out=Wv[:, 1, :, 0:ncol], in_=wb["w_up"][:, c0:c0 + ncol].rearrange("(c p) n -> p c n", p=128)), r=[t_wb["w_up"]], w=[t_slot[si]])

                    def comp(si):
                        Wv = slots[si].rearrange("p (w c n) -> p w c n", w=2, c=16)
                        for jj, j in enumerate(ffcs):
                            bg, bu = j % 2, 2 + j % 2
                            for kc in range(16):
                                T(lambda e, bg=bg, kc=kc, jj=jj: e.matmul(pb[bg][:, :], lhsT=Wv[:, 0, kc, 128 * jj:128 * jj + 128], rhs=hT[:, kc, :], start=(kc == 0), stop=(kc == 15)), r=[t_slot[si], t_hT], w=[tb[bg]])
                            for kc in range(16):
                                T(lambda e, bu=bu, kc=kc, jj=jj: e.matmul(pb[bu][:, :], lhsT=Wv[:, 1, kc, 128 * jj:128 * jj + 128], rhs=hT[:, kc, :], start=(kc == 0), stop=(kc == 15)), r=[t_slot[si], t_hT], w=[tb[bu]])
                            sgi = j % 2
                            A(lambda e, bg=bg, sgi=sgi: e.activation(out=sg[sgi], in_=pb[bg][:, :], func=AF.Silu), r=[tb[bg]], w=[t_sg[sgi]])
                            V(lambda e, bu=bu, sgi=sgi, j=j: e.tensor_tensor(out=actT[:, j, :], in0=sg[sgi], in1=pb[bu][:, :], op=ALU.mult), r=[t_sg[sgi], tb[bu]], w=[t_act])
                    return load, comp

                def mk_down(grp, cc):
                    def load(si):
                        Wv = slots[si][:, 0:11 * 512].rearrange("p (j n) -> p j n", j=11)
                        DS(lambda e: e.dma_start(out=Wv, in_=wb["w_down"][1408 * grp:1408 * grp + 1408, 512 * cc:512 * cc + 512].rearrange("(j p) n -> p j n", p=128)), r=[t_wb["w_down"]], w=[t_slot[si]])

                    def comp(si):
                        Wv = slots[si][:, 0:11 * 512].rearrange("p (j n) -> p j n", j=11)
                        for tt in range(4):
                            bd = 4 + tt % 2
                            for j in range(11):
                                T(lambda e, bd=bd, j=j, tt=tt: e.matmul(pb[bd][:, :], lhsT=actT[:, j, 128 * tt:128 * tt + 128], rhs=Wv[:, j, :], start=(j == 0), stop=(j == 10)), r=[t_act, t_slot[si]], w=[tb[bd]])
                            xs = X[:, tt, 512 * cc:512 * cc + 512]
                            V(lambda e, bd=bd, xs=xs: e.tensor_tensor(out=xs, in0=xs, in1=pb[bd][:, :], op=ALU.add), r=[tb[bd], t_X[tt]], w=[t_X[tt]])
                    return load, comp

                for grp in range(4):
                    for b2 in range(6):
                        pend.append(mk_gu(grp, b2))
                    for cc in range(4):
                        pend.append(mk_down(grp, cc))

                WPv = actT[:, 0:8, :].rearrange("p a b -> p (a b)").rearrange("p (c n) -> p c n", c=2)

                def ple_prep(si_unused):
                    DS(lambda e: e.dma_start(out=WPv, in_=wb["w_ple_proj"].rearrange("(c p) n -> p c n", p=128)), r=[t_wb["w_ple_proj"]], w=[t_act])
                    norm_to_hT(gple, t_gp)
                    for tt in range(4):
                        DG(lambda e, tt=tt: e.dma_start(out=pf, in_=dr["p"][t0_ + 128 * tt:t0_ + 128 * tt + 128, :]), w=[t_pf])
                        V(lambda e: e.tensor_copy(out=pbf, in_=pf), r=[t_pf], w=[t_hb])
                        for kc in range(2):
                            T(lambda e, kc=kc: e.transpose(out=pbb[6][:, 128 * kc:128 * kc + 128], in_=pbf[:, 128 * kc:128 * kc + 128], identity=ident_bf[:]), r=[t_hb, t_const], w=[tb[6]])
                        A(lambda e, tt=tt: e.activation(out=pT[:, :, 128 * tt:128 * tt + 128], in_=pbb[6][:, 0:256].rearrange("p (k t) -> p k t", k=2), func=AF.Copy), r=[tb[6]], w=[t_pT])

                pend.append((None, ple_prep))

                def mk_pg(cc):
                    def load(si):
                        DS(lambda e: e.dma_start(out=slots[si].rearrange("p (c n) -> p c n", c=16), in_=wb["w_ple_gate"][:, 512 * cc:512 * cc + 512].rearrange("(c p) n -> p c n", p=128)), r=[t_wb["w_ple_gate"]], w=[t_slot[si]])

                    def comp(si):
                        W = slots[si].rearrange("p (c n) -> p c n", c=16)
                        WP = WPv
                        for tt in range(4):
                            bg, bp = tt % 2, 2 + tt % 2
                            for kc in range(16):
                                T(lambda e, bg=bg, kc=kc, tt=tt: e.matmul(pb[bg][:, :], lhsT=hT[:, kc, 128 * tt:128 * tt + 128], rhs=W[:, kc, :], start=(kc == 0), stop=(kc == 15)), r=[t_hT, t_slot[si]], w=[tb[bg]])
                            for kc in range(2):
                                T(lambda e, bp=bp, kc=kc, tt=tt: e.matmul(pb[bp][:, :], lhsT=pT[:, kc, 128 * tt:128 * tt + 128], rhs=WP[:, kc, 512 * cc:512 * cc + 512], start=(kc == 0), stop=(kc == 1)), r=[t_pT, t_act], w=[tb[bp]])
                            sgi = tt % 2
                            A(lambda e, bg=bg, sgi=sgi: e.activation(out=sg[sgi], in_=pb[bg][:, :], func=AF.Sigmoid), r=[tb[bg]], w=[t_sg[sgi]])
                            V(lambda e, bp=bp, sgi=sgi: e.tensor_tensor(out=sg[sgi], in0=sg[sgi], in1=pb[bp][:, :], op=ALU.mult), r=[t_sg[sgi], tb[bp]], w=[t_sg[sgi]])
                            xs = X[:, tt, 512 * cc:512 * cc + 512]
                            V(lambda e, sgi=sgi, xs=xs: e.tensor_tensor(out=xs, in0=xs, in1=sg[sgi], op=ALU.add), r=[t_sg[sgi], t_X[tt]], w=[t_X[tt]])
                    return load, comp

                for cc in range(4):
                    pend.append(mk_pg(cc))

                loads = [(i, ld) for i, (ld, _) in enumerate(pend) if ld is not None]
                assigned = {}
                li = [0]

                def issue_loads_upto(n_ahead_idx):
                    while li[0] < len(loads) and loads[li[0]][0] <= n_ahead_idx:
                        idx, ld = loads[li[0]]
                        si = next_slot()
                        assigned[idx] = si
                        ld(si)
                        li[0] += 1

                for i, (ld, cp) in enumerate(pend):
                    cnt = 0
                    j = i
                    tgt = i
                    while j < len(pend) and cnt < 2:
                        if pend[j][0] is not None:
                            cnt += 1
                            tgt = j
                        j += 1
                    issue_loads_upto(tgt)
                    cp(assigned.get(i))
                for tt in range(4):
                    DG(lambda e, tt=tt: e.dma_start(out=out[t0_ + 128 * tt:t0_ + 128 * tt + 128, :], in_=X[:, tt, :]), r=[t_X[tt]], w=[t_out])

            for sI in range(4):
                super_tile(sI)

        if stage in ("all", "pd"):
            _phase5()

        finals = []
        if stage in ('all', 'pd'):
            finals.append(t_out)
        if dbg is not None:
            finals.append(t_dbg)
        P.emit(final_waits=finals)
        print("ops:", P.nops, {e: len(v) for e, v in P.ops.items()}, "sems:", P.nsem)
    return nc


def make_inputs(inputs, b):
    m = {
        "x": np.ascontiguousarray(inputs["x"][b]),
        "p": np.ascontiguousarray(inputs["p"][0, b]),
        "pos": np.ascontiguousarray(inputs["positions"][b].reshape(1, L).astype(np.int32)),
    }
    for n, r, c in WSPECS:
        m[n] = np.ascontiguousarray(inputs[n][0])
    for n, shp in SMALL:
        m[n] = np.ascontiguousarray(inputs[n][0].reshape(shp))
    return m


def kernel(**inputs):
    nc = build("all")
    in_maps = [make_inputs(inputs, b) for b in range(8)]
    res = run_bass_kernel_spmd(nc, in_maps, core_ids=list(range(8)))
    return np.stack([r["out"] for r in res.results], axis=0).astype(np.float32)
```
